# Optimizing a Trainium2 kernel written in Bass

```python
import math
import jax, jax.numpy as jnp
from jax import lax
import numpy as np

D_MODEL = 1024
BATCH = 16
SEQ = 256
DEPTH = 2
DEC_BATCH = 2
DEC_SEQ = 1024
PAST_LEN = 512

GRID_W = 64
HEAD_DIM = 64
MIX_A = D_MODEL // 2
MIX_B = D_MODEL // 4
MIX_C = D_MODEL - MIX_A - MIX_B
N_HEADS_A = MIX_A // (2 * HEAD_DIM)
KEY_DIM_B = HEAD_DIM
VAL_DIM_B = HEAD_DIM
N_HEADS_B = MIX_B // VAL_DIM_B
N_HEADS_C = MIX_C // HEAD_DIM
N_KV_C = N_HEADS_C // 2
GQA_GROUP = N_HEADS_C // N_KV_C
D_FF = 4 * D_MODEL
Q_BLOCK = 128
CHUNK = 64
ROPE_THETA = 10000.0
ROPE_AXIS_PAIRS = HEAD_DIM // 4
ALPHA = (2 * DEPTH) ** 0.25
BETA = (8 * DEPTH) ** -0.25
LN_EPS = 1e-6
RMS_EPS = 1e-6
F_MIN = 1e-6
PROJ_SIZES = (MIX_A, MIX_A, MIX_A,
              N_HEADS_B * KEY_DIM_B, N_HEADS_B * KEY_DIM_B,
              N_HEADS_B * KEY_DIM_B,
              N_HEADS_B * VAL_DIM_B, N_HEADS_B * VAL_DIM_B,
              MIX_C, N_KV_C * HEAD_DIM, N_KV_C * HEAD_DIM)
N_IN = sum(PROJ_SIZES)

kernel_name = 'hybrid_diffusion_diffattn_hgrn2_gqa_step'


def _layernorm(x, g, b):
    xf = x.astype(jnp.float32)
    mu = jnp.mean(xf, axis=-1, keepdims=True)
    var = jnp.mean(jnp.square(xf - mu), axis=-1, keepdims=True)
    return ((xf - mu) * lax.rsqrt(var + LN_EPS)).astype(x.dtype) * g + b


def _rmsnorm(x, g):
    xf = x.astype(jnp.float32)
    y = xf * lax.rsqrt(jnp.mean(jnp.square(xf), axis=-1, keepdims=True) + RMS_EPS)
    return y.astype(x.dtype) * g


def _axial_rope_tables(n_tokens):
    rows = n_tokens // GRID_W
    row = jnp.repeat(jnp.arange(rows, dtype=jnp.float32), GRID_W)
    col = (jnp.arange(rows * GRID_W) % GRID_W).astype(jnp.float32)
    inv = ROPE_THETA ** (-jnp.arange(ROPE_AXIS_PAIRS, dtype=jnp.float32) / ROPE_AXIS_PAIRS)
    ang = jnp.concatenate([row[:, None] * inv, col[:, None] * inv], axis=-1)
    return jnp.cos(ang), jnp.sin(ang)


def _apply_rope(x, cos, sin):
    shp = x.shape
    bshape = (1, cos.shape[0]) + (1,) * (x.ndim - 3) + (cos.shape[1],)
    c, s = cos.reshape(bshape), sin.reshape(bshape)
    xf = x.astype(jnp.float32).reshape(shp[:-1] + (HEAD_DIM // 2, 2))
    x1, x2 = xf[..., 0], xf[..., 1]
    out = jnp.stack([x1 * c - x2 * s, x1 * s + x2 * c], axis=-1)
    return out.reshape(shp).astype(x.dtype)


def _query_block_sweep(fn, q):
    B, T = q.shape[:2]
    nb = T // Q_BLOCK
    qb = jnp.moveaxis(q.reshape((B, nb, Q_BLOCK) + q.shape[2:]), 1, 0)
    out = jnp.moveaxis(lax.map(fn, qb), 0, 1)
    return out.reshape((B, T) + out.shape[3:])


def _diff_attn_block(qb, k, v, lam):
    s = jnp.einsum('bqhmd,bkhmd->bhmqk', qb, k).astype(jnp.float32) * HEAD_DIM ** -0.5
    p = jax.nn.softmax(s, axis=-1)
    w = p[:, :, 0] - lam * p[:, :, 1]
    return jnp.einsum('bhqk,bkhe->bqhe', w.astype(v.dtype), v)


def _gqa_block(qb, k, v):
    s = jnp.einsum('bqngd,bknd->bngqk', qb, k).astype(jnp.float32) * HEAD_DIM ** -0.5
    p = jax.nn.softmax(s, axis=-1).astype(v.dtype)
    return jnp.einsum('bngqk,bknd->bqngd', p, v)


def _log_forget(x, lb):
    lb = lb.astype(jnp.float32)
    f = lb + (1.0 - lb) * jax.nn.sigmoid(x.astype(jnp.float32))
    return jnp.log(jnp.maximum(f, F_MIN))


def _hgrn2_scan(q, k, v, g, s0):
    B, T, H, DK = q.shape
    DV = v.shape[-1]
    nc = T // CHUNK

    def chunks(a):
        a = a.astype(jnp.float32).reshape((B, nc, CHUNK) + a.shape[2:])
        return jnp.moveaxis(a, 1, 0)

    causal = jnp.tril(jnp.ones((CHUNK, CHUNK), dtype=bool))[None, :, :, None, None]

    def step(S, xs):
        qc, kc, vc, gc = xs
        b = jnp.cumsum(gc, axis=1)
        o_inter = jnp.einsum('bthk,bhkv->bthv', qc * jnp.exp(b), S)
        diff = b[:, :, None] - b[:, None, :]
        decay = jnp.where(causal, jnp.exp(jnp.where(causal, diff, 0.0)), 0.0)
        attn = jnp.einsum('bthk,bshk,btshk->bths', qc, kc, decay)
        o_intra = jnp.einsum('bths,bshv->bthv', attn, vc)
        b_end = b[:, -1]
        S = jnp.exp(b_end)[..., None] * S + jnp.einsum(
            'bshk,bshv->bhkv', kc * jnp.exp(b_end[:, None] - b), vc)
        return S, o_inter + o_intra

    s_fin, o = lax.scan(step, s0.astype(jnp.float32), (chunks(q), chunks(k), chunks(v), chunks(g)))
    return jnp.moveaxis(o, 0, 1).reshape(B, T, H, DV), s_fin


def _token_mixer(h, lp, li, rope, ctx):
    B, T, _ = h.shape
    z = h @ lp['w_in']
    qa, ka, va, qb, fb_f, fb_b, ib, gb, qc, kc, vc = jnp.split(
        z, np.cumsum(PROJ_SIZES)[:-1].tolist(), axis=-1)
    qa = qa.reshape(B, T, N_HEADS_A, 2, HEAD_DIM)
    ka = ka.reshape(B, T, N_HEADS_A, 2, HEAD_DIM)
    va = va.reshape(B, T, N_HEADS_A, 2 * HEAD_DIM)
    qc = _rmsnorm(qc.reshape(B, T, N_HEADS_C, HEAD_DIM), lp['qnorm_g'])
    kc = _rmsnorm(kc.reshape(B, T, N_KV_C, HEAD_DIM), lp['knorm_g'])
    vc = vc.reshape(B, T, N_KV_C, HEAD_DIM)
    if ctx is None:
        ka_all, va_all, kc_all, vc_all = ka, va, kc, vc
        s0_f = s0_b = jnp.zeros((B, N_HEADS_B, KEY_DIM_B, VAL_DIM_B), jnp.float32)
    else:
        cos, sin = rope
        qa = _apply_rope(qa, cos, sin)
        qc = _apply_rope(qc, cos, sin)
        ka_all = jnp.concatenate([_apply_rope(ka, cos, sin), ctx[0]], axis=1)
        va_all = jnp.concatenate([va, ctx[1]], axis=1)
        kc_all = jnp.concatenate([_apply_rope(kc, cos, sin), ctx[2]], axis=1)
        vc_all = jnp.concatenate([vc, ctx[3]], axis=1)
        s0_f, s0_b = ctx[4], ctx[5]

    lam_init = 0.8 - 0.6 * math.exp(-0.3 * li)
    lam = (jnp.exp(jnp.sum(lp['lam_q1'].astype(jnp.float32) * lp['lam_k1'].astype(jnp.float32)))
           - jnp.exp(jnp.sum(lp['lam_q2'].astype(jnp.float32) * lp['lam_k2'].astype(jnp.float32)))
           + lam_init)
    oa = _query_block_sweep(lambda q: _diff_attn_block(q, ka_all, va_all, lam), qa)
    oa = _rmsnorm(oa, lp['subln_g']) * (1.0 - lam_init)

    g_f = _log_forget(fb_f, lp['lb_fwd']).reshape(B, T, N_HEADS_B, KEY_DIM_B)
    g_b = _log_forget(fb_b, lp['lb_bwd']).reshape(B, T, N_HEADS_B, KEY_DIM_B)
    qb = jax.nn.silu(qb).reshape(B, T, N_HEADS_B, KEY_DIM_B)
    vb = ib.reshape(B, T, N_HEADS_B, VAL_DIM_B)
    flip = lambda a: jnp.flip(a, axis=1)
    o_f, s_f = _hgrn2_scan(qb, -jnp.expm1(g_f), vb, g_f, s0_f)
    o_b, s_b = _hgrn2_scan(flip(qb), -jnp.expm1(flip(g_b)), flip(vb), flip(g_b), s0_b)
    ob = (o_f + flip(o_b)).astype(h.dtype)
    ob = _rmsnorm(ob, lp['gnorm_g']) * jax.nn.silu(gb.reshape(B, T, N_HEADS_B, VAL_DIM_B))

    oc = _query_block_sweep(lambda q: _gqa_block(q, kc_all, vc_all),
                            qc.reshape(B, T, N_KV_C, GQA_GROUP, HEAD_DIM))

    mixed = jnp.concatenate([oa.reshape(B, T, MIX_A), ob.reshape(B, T, MIX_B),
                             oc.reshape(B, T, MIX_C)], axis=-1)
    own = (ka, va, kc, vc, s_f.astype(h.dtype), s_b.astype(h.dtype))
    return mixed @ lp['w_out'], own


def _layer(x, mod, lp, li, rope, ctx):
    sh1, sc1, g1, sh2, sc2, g2 = jnp.split(mod, 6, axis=-1)
    m, own = _token_mixer(x * (1.0 + sc1) + sh1, lp, li, rope, ctx)
    x = _layernorm(ALPHA * x + g1 * m, lp['ln1_g'], lp['ln1_b'])
    hid = jnp.square(jax.nn.relu((x * (1.0 + sc2) + sh2) @ lp['w_ff1']))
    x = _layernorm(ALPHA * x + g2 * (hid @ lp['w_ff2']), lp['ln2_g'], lp['ln2_b'])
    return x, own


def setup_inputs(seed: int = 0) -> dict:
    key = jax.random.key(seed)
    ks = jax.random.split(key, 32)
    f32 = jnp.float32
    nrm = lambda k, shape, s=1.0: s * jax.random.normal(k, shape, f32)
    return {
        'x_prompt': nrm(ks[0], (BATCH, SEQ, D_MODEL)),
        'x_sample': nrm(ks[1], (DEC_BATCH, DEC_SEQ, D_MODEL)),
        'cache_a_k': nrm(ks[2], (DEC_BATCH, DEPTH, PAST_LEN, N_HEADS_A, 2, HEAD_DIM)),
        'cache_a_v': nrm(ks[3], (DEC_BATCH, DEPTH, PAST_LEN, N_HEADS_A, 2 * HEAD_DIM)),
        'cache_c_k': nrm(ks[4], (DEC_BATCH, DEPTH, PAST_LEN, N_KV_C, HEAD_DIM)),
        'cache_c_v': nrm(ks[5], (DEC_BATCH, DEPTH, PAST_LEN, N_KV_C, HEAD_DIM)),
        'state_b_fwd': nrm(ks[6], (DEC_BATCH, DEPTH, N_HEADS_B, KEY_DIM_B, VAL_DIM_B)),
        'state_b_bwd': nrm(ks[7], (DEC_BATCH, DEPTH, N_HEADS_B, KEY_DIM_B, VAL_DIM_B)),
        'c': nrm(ks[8], (DEC_BATCH, D_MODEL)),
        'c_ctx': nrm(ks[9], (D_MODEL,)),
        'w_ada': nrm(ks[10], (DEPTH, D_MODEL, 6 * D_MODEL), 0.5 * D_MODEL ** -0.5),
        'b_ada': nrm(ks[11], (DEPTH, 6 * D_MODEL), 0.02),
        'w_in': nrm(ks[12], (DEPTH, D_MODEL, N_IN), D_MODEL ** -0.5),
        'w_out': nrm(ks[13], (DEPTH, MIX_A + MIX_B + MIX_C, D_MODEL), BETA * (MIX_A + MIX_B + MIX_C) ** -0.5),
        'lam_q1': nrm(ks[14], (DEPTH, HEAD_DIM), 0.1),
        'lam_k1': nrm(ks[15], (DEPTH, HEAD_DIM), 0.1),
        'lam_q2': nrm(ks[16], (DEPTH, HEAD_DIM), 0.1),
        'lam_k2': nrm(ks[17], (DEPTH, HEAD_DIM), 0.1),
        'subln_g': 1.0 + nrm(ks[18], (DEPTH, 2 * HEAD_DIM), 0.02),
        'lb_logits_fwd': nrm(ks[19], (DEPTH, N_HEADS_B * KEY_DIM_B), 0.5),
        'lb_logits_bwd': nrm(ks[20], (DEPTH, N_HEADS_B * KEY_DIM_B), 0.5),
        'gnorm_g': 1.0 + nrm(ks[21], (DEPTH, VAL_DIM_B), 0.02),
        'qnorm_g': 1.0 + nrm(ks[22], (DEPTH, HEAD_DIM), 0.02),
        'knorm_g': 1.0 + nrm(ks[23], (DEPTH, HEAD_DIM), 0.02),
        'ln1_g': 1.0 + nrm(ks[24], (DEPTH, D_MODEL), 0.02),
        'ln1_b': nrm(ks[25], (DEPTH, D_MODEL), 0.02),
        'ln2_g': 1.0 + nrm(ks[26], (DEPTH, D_MODEL), 0.02),
        'ln2_b': nrm(ks[27], (DEPTH, D_MODEL), 0.02),
        'w_ff1': nrm(ks[28], (DEPTH, D_MODEL, D_FF), D_MODEL ** -0.5),
        'w_ff2': nrm(ks[29], (DEPTH, D_FF, D_MODEL), BETA * D_FF ** -0.5),
    }


def reference(x_prompt, x_sample, cache_a_k, cache_a_v, cache_c_k, cache_c_v, state_b_fwd,
              state_b_bwd, c, c_ctx, w_ada, b_ada, w_in, w_out, lam_q1, lam_k1, lam_q2, lam_k2,
              subln_g, lb_logits_fwd, lb_logits_bwd, gnorm_g, qnorm_g, knorm_g, ln1_g, ln1_b,
              ln2_g, ln2_b, w_ff1, w_ff2):
    sm_f = jax.nn.softmax(lb_logits_fwd.astype(jnp.float32), axis=0)
    sm_b = jax.nn.softmax(lb_logits_bwd.astype(jnp.float32), axis=0)
    lb_f = jnp.cumsum(sm_f, axis=0) - sm_f[0]
    lb_b = jnp.cumsum(sm_b, axis=0) - sm_b[0]
    rope = _axial_rope_tables(x_sample.shape[1])

    y_prompt, y_sample = x_prompt, x_sample
    ctx_layers = []
    for li in range(DEPTH):
        lp = {'w_in': w_in[li], 'w_out': w_out[li], 'lam_q1': lam_q1[li], 'lam_k1': lam_k1[li],
              'lam_q2': lam_q2[li], 'lam_k2': lam_k2[li], 'subln_g': subln_g[li],
              'lb_fwd': lb_f[li], 'lb_bwd': lb_b[li], 'gnorm_g': gnorm_g[li],
              'qnorm_g': qnorm_g[li], 'knorm_g': knorm_g[li], 'ln1_g': ln1_g[li],
              'ln1_b': ln1_b[li], 'ln2_g': ln2_g[li], 'ln2_b': ln2_b[li],
              'w_ff1': w_ff1[li], 'w_ff2': w_ff2[li]}
        mod_ctx = (jax.nn.silu(c_ctx) @ w_ada[li] + b_ada[li])[None, None, :]
        y_prompt, own = _layer(y_prompt, mod_ctx, lp, li, None, None)
        ctx_layers.append(own)
        mod_lat = (jax.nn.silu(c) @ w_ada[li] + b_ada[li])[:, None, :]
        cache = (cache_a_k[:, li], cache_a_v[:, li], cache_c_k[:, li], cache_c_v[:, li],
                 state_b_fwd[:, li], state_b_bwd[:, li])
        y_sample, _ = _layer(y_sample, mod_lat, lp, li, rope, cache)

    new_a_k = jnp.stack([t[0] for t in ctx_layers], axis=1)
    new_a_v = jnp.stack([t[1] for t in ctx_layers], axis=1)
    new_c_k = jnp.stack([t[2] for t in ctx_layers], axis=1)
    new_c_v = jnp.stack([t[3] for t in ctx_layers], axis=1)
    new_state_fwd = jnp.stack([t[4] for t in ctx_layers], axis=1)
    new_state_bwd = jnp.stack([t[5] for t in ctx_layers], axis=1)
    return (y_prompt, y_sample, new_a_k, new_a_v, new_c_k, new_c_v, new_state_fwd, new_state_bwd)
```

```python
import numpy as np
import concourse.bass as bass
import concourse.mybir as mybir
from concourse.bass_utils import run_bass_kernel_spmd
from contextlib import ExitStack

F32 = mybir.dt.float32
BF16 = mybir.dt.bfloat16
AF = mybir.ActivationFunctionType
ALU = mybir.AluOpType
AX = mybir.AxisListType

ENGS = ("pe", "act", "dve", "pool", "sp")
NDMA_SLOTS = {"sp": 12, "pool": 12, "act": 4}
STORES_ON_POOL = True
SAME_ENG_SYNC = {"pe": False, "act": True, "dve": True, "pool": True, "sp": True}


import os
MAXOPS = int(os.environ.get('MAXOPS', '100000000'))


class Res:
    _n = 0

    def __init__(self, name):
        Res._n += 1
        self.id = Res._n
        self.name = name


class T:
    def __init__(self, h, name):
        self.h = h
        self._res = Res(name)

    def __getitem__(self, k):
        return self.h[k]

    def ap(self):
        return self.h.ap()


class Op:
    __slots__ = ("eng", "fn", "reads", "writes", "dma", "idx", "waits", "need_inc",
                 "cnt", "slot", "slot_val", "pre_wait")

    def __init__(self, eng, fn, reads, writes, dma):
        self.eng, self.fn, self.reads, self.writes, self.dma = eng, fn, reads, writes, dma
        self.waits = []
        self.need_inc = False
        self.cnt = None
        self.slot = None
        self.slot_val = None
        self.pre_wait = None


class Prog:
    def __init__(self, nc, stack):
        self.nc = nc
        self.stack = stack
        self.ops = []
        self.state = {}
        self.subs = {}
        self.ndma = {e: 0 for e in ENGS}
        self.psum_banks = []
        self.psum_i = 0
        self.pinned = []

    def sb(self, name, shape, dt=F32):
        t = self.stack.enter_context(self.nc.sbuf_tensor(name, list(shape), dt))
        return T(t, name)

    def dram(self, name, shape, dt=F32, kind="Internal"):
        t = self.nc.dram_tensor(name, list(shape), dt, kind=kind)
        return T(t, name)

    def init_psum(self, n=8):
        for i in range(n):
            t = self.stack.enter_context(self.nc.psum_tensor(f"psb{i}", [128, 512], F32))
            self.psum_banks.append(T(t, f"psb{i}"))

    def psum(self, pin=False):
        assert len(self.pinned) < len(self.psum_banks), "all PSUM banks pinned"
        while True:
            t = self.psum_banks[self.psum_i % len(self.psum_banks)]
            self.psum_i += 1
            if t not in self.pinned:
                break
        if pin:
            self.pinned.append(t)
        return t

    def unpin(self, t):
        self.pinned.remove(t)

    @staticmethod
    def _key(r):
        if isinstance(r, tuple):
            return (r[0]._res.id, r[1])
        return (r._res.id, getattr(r, "_sub", None))

    def _conflicts(self, key):
        rid, sub = key
        subs = self.subs.setdefault(rid, set())
        if sub is None:
            return [(rid, s) for s in subs | {None}]
        return [(rid, sub), (rid, None)]

    def op(self, eng, fn, reads=(), writes=(), dma=False):
        if len(self.ops) >= MAXOPS:
            return None
        o = Op(eng, fn, [self._key(r) for r in reads], [self._key(r) for r in writes], dma)
        o.idx = len(self.ops)
        deps = set()
        for k in o.reads:
            for ck in self._conflicts(k):
                st = self.state.get(ck)
                if st and st[0] is not None:
                    deps.add(st[0])
        for k in o.writes:
            for ck in self._conflicts(k):
                st = self.state.get(ck)
                if st:
                    if st[0] is not None:
                        deps.add(st[0])
                    deps.update(st[1])
        deps.discard(o.idx)
        o.waits = sorted(deps)
        for k in o.reads:
            self.subs.setdefault(k[0], set()).add(k[1])
            st = self.state.setdefault(k, [None, []])
            st[1].append(o.idx)
        for k in o.writes:
            self.subs.setdefault(k[0], set()).add(k[1])
            if k[1] is None:
                for s in list(self.subs[k[0]]):
                    self.state[(k[0], s)] = [o.idx, []]
            else:
                self.state[k] = [o.idx, []]
        if dma:
            j = self.ndma[eng]
            self.ndma[eng] += 1
            K = NDMA_SLOTS[eng]
            o.slot = j % K
            o.slot_val = 16 * (j // K + 1)
            if j >= K:
                o.pre_wait = (o.slot, 16 * (j // K))
        self.ops.append(o)
        return o

    def mm(self, out, lhsT, rhs, start, stop, reads, writes, **kw):
        return self.op("pe", lambda e: e.matmul(out, lhsT, rhs, start=start, stop=stop, **kw),
                       reads, writes)

    def tr(self, out, in_, ident, reads, writes):
        return self.op("pe", lambda e: e.transpose(out, in_, ident), reads, writes)

    def dma(self, eng, out, in_, reads, writes, **kw):
        if eng == "sp" and STORES_ON_POOL and str(getattr(out, "space", "")).endswith("DRAM") and not kw.get("keep_queue"):
            eng = "pool"
        kw.pop("keep_queue", None)
        return self.op(eng, lambda e: e.dma_start(out=out, in_=in_, **kw), reads, writes, dma=True)

    def emit(self):
        nc = self.nc
        ops = self.ops
        for o in ops:
            for d in o.waits:
                D = ops[d]
                if D.dma:
                    continue
                if D.eng == o.eng and not o.dma and not D.dma and not SAME_ENG_SYNC[o.eng]:
                    continue
                D.need_inc = True
        cnt = {e: 0 for e in ENGS}
        for o in ops:
            if not o.dma and o.need_inc:
                cnt[o.eng] += 1
                o.cnt = cnt[o.eng]
        sems = {e: self.stack.enter_context(nc.semaphore(f"s_{e}")) for e in ENGS}
        dsems = {e: [self.stack.enter_context(nc.semaphore(f"d_{e}{i}")) for i in range(n)]
                 for e, n in NDMA_SLOTS.items()}
        block = self.stack.enter_context(nc.Block())
        last_out_waits = []

        def run(engname, e):
            seen_eng = {x: 0 for x in ENGS}
            seen_dma = {}
            for o in ops:
                if o.eng != engname:
                    continue
                if o.dma and o.pre_wait is not None:
                    s, v = o.pre_wait
                    key = (engname, s)
                    if seen_dma.get(key, 0) < v:
                        e.wait_ge(dsems[engname][s], v)
                        seen_dma[key] = v
                for d in o.waits:
                    D = ops[d]
                    if D.dma:
                        key = (D.eng, D.slot)
                        if seen_dma.get(key, 0) < D.slot_val:
                            e.wait_ge(dsems[D.eng][D.slot], D.slot_val)
                            seen_dma[key] = D.slot_val
                    else:
                        if D.eng == engname and not o.dma and not SAME_ENG_SYNC[engname]:
                            continue
                        if seen_eng[D.eng] < D.cnt:
                            e.wait_ge(sems[D.eng], D.cnt)
                            seen_eng[D.eng] = D.cnt
                ins = o.fn(e)
                if o.dma:
                    ins.then_inc(dsems[engname][o.slot], 16)
                elif o.need_inc:
                    ins.then_inc(sems[engname], 1)
            for s in range(NDMA_SLOTS.get(engname, 0)):
                lastv = 0
                for o in ops:
                    if o.dma and o.eng == engname and o.slot == s:
                        lastv = o.slot_val
                if lastv and seen_dma.get((engname, s), 0) < lastv:
                    e.wait_ge(dsems[engname][s], lastv)

        @block.tensor
        def _(e):
            run("pe", e)

        @block.scalar
        def _(e):
            run("act", e)

        @block.vector
        def _(e):
            run("dve", e)

        @block.gpsimd
        def _(e):
            run("pool", e)

        @block.sync
        def _(e):
            run("sp", e)


D = 1024
NIN = 3328
ALPHA = 4 ** 0.25
LN_EPS = 1e-6
RMS_EPS = 1e-6
F_MIN = 1e-6
CH = 32
NCH = 256 // CH
C_ID, C_BONES, C_MF, C_MB, C_BLK2 = 0, 128, 256, 384, 512
C_ROW, C_COL, C_RESET, C_PSW, C_RC, C_RS, C_SEL2, C_SELBC = 640, 644, 1156, 1412, 1540, 2564, 3588, 3590
NCONST = 3590 + 256


def make_consts(tok0):
    c = np.zeros((128, NCONST), np.float32)
    p = np.arange(128)
    c[:, C_ID:C_ID + 128] = np.eye(128)
    c[:, C_BONES:C_BONES + 128] = (p[:, None] // 64 == p[None, :] // 64) / 64.0
    same = (p[:, None] // CH == p[None, :] // CH)
    c[:, C_MF:C_MF + 128] = same & (p[:, None] <= p[None, :])
    c[:, C_MB:C_MB + 128] = same & (p[:, None] >= p[None, :])
    c[:, C_BLK2:C_BLK2 + 128] = (p[:, None] // 64 == p[None, :] // 64)
    c[:, C_ROW:C_ROW + 4] = (p[:, None] // CH == np.arange(4)[None, :])
    c[:, C_COL:C_COL + 512] = np.broadcast_to((np.arange(4)[:, None] == (p[None, :] // CH)).reshape(1, 512), (128, 512))
    t = np.arange(256)
    c[:, C_RESET:C_RESET + 256] = np.broadcast_to((t % CH != 0)[None, :], (128, 256))
    c[:, C_PSW:C_PSW + 128] = (p[:, None] == (p[None, :] ^ 1))
    pos = np.arange(1024)
    row = (pos // 64).astype(np.float32)
    col = (pos % 64).astype(np.float32)
    inv = (10000.0 ** (-np.arange(16, dtype=np.float32) / 16)).astype(np.float32)
    ang = np.concatenate([row[:, None] * inv, col[:, None] * inv], axis=-1).astype(np.float32)
    f = p % 64
    cosT = np.cos(ang)[:, f // 2].T
    sinT = np.sin(ang)[:, f // 2].T
    sgn = np.where(f % 2 == 0, -1.0, 1.0)[:, None]
    c[:, C_RC:C_RC + 1024] = cosT
    c[:, C_RS:C_RS + 1024] = sinT * sgn
    c[0, C_SEL2] = 1.0
    c[1, C_SEL2 + 1] = 1.0
    c[0, C_SELBC:C_SELBC + 128] = 1.0
    c[1, C_SELBC + 128:C_SELBC + 256] = 1.0
    return c


def build_nc(with_sample=True, stop=None):
    nc = bass.Bass("TRN2", target_bir_lowering=False)
    din = lambda n, s: nc.dram_tensor(n, list(s), F32, kind="ExternalInput").ap()
    dout = lambda n, s: nc.dram_tensor(n, list(s), F32, kind="ExternalOutput").ap()
    xp = din("xp", [512, D]); xs = din("xs", [1024, D])
    cmodT = din("cmodT", [128, 16]); consts = din("consts", [128, NCONST])
    w_ada = din("w_ada", [2, D, 6 * D]); b_ada = din("b_ada", [2, 6 * D])
    w_in = din("w_in", [2, D, NIN]); w_out = din("w_out", [2, D, D])
    w_ff1 = din("w_ff1", [2, D, 4 * D]); w_ff2 = din("w_ff2", [2, 4 * D, D])
    lamv = din("lamv", [2, 4, 64]); subln_g = din("subln_g", [2, 128])
    lbl = din("lbl", [2, 2, 256])
    gnorm_g = din("gnorm_g", [2, 64]); qnorm_g = din("qnorm_g", [2, 64]); knorm_g = din("knorm_g", [2, 64])
    lnp = din("lnp", [2, 4, D])
    ck_a = din("ck_a", [2, 512, 512]); cv_a = din("cv_a", [2, 512, 512])
    ck_c = din("ck_c", [2, 512, 128]); cv_c = din("cv_c", [2, 512, 128])
    st_f = din("st_f", [2, 4, 64, 64]); st_b = din("st_b", [2, 4, 64, 64])
    onehot = din("onehot", [128, 8])
    yp = dout("yp", [512, D]); ys = dout("ys", [256, D])
    o_ak = dout("o_ak", [2, 2, 256, 512]); o_av = dout("o_av", [2, 2, 256, 512])
    o_ck = dout("o_ck", [2, 2, 256, 128]); o_cv = dout("o_cv", [2, 2, 256, 128])
    o_sf = dout("o_sf", [2, 2, 4, 64, 64]); o_sb = dout("o_sb", [2, 2, 4, 64, 64])

    with ExitStack() as stk:
        P = Prog(nc, stk)
        P.init_psum(8)
        rr = [0]

        def evac_eng():
            rr[0] += 1
            return "act" if rr[0] % 2 else "dve"

        cst = P.sb("cst", [128, NCONST])
        idf = cst[:, C_ID:C_ID + 128]
        x_t = P.sb("x_t", [128, 2, D])
        x_tm = [x_t, x_t, x_t]
        XP = [P.dram(f"XP{u}", [128, 2, D]) for u in range(2)]
        hT = P.sb("hT", [128, 8, 256], BF16)
        wr = [P.sb(f"wr{i}", [128, 4096], BF16) for i in range(3)]
        wctr = [0]
        x_tB = P.sb("x_tB", [128, 2, D])
        scT = P.sb("scT", [128, 16]); scTb = P.sb("scTb", [128, 16], BF16)
        mc = P.sb("mc", [128, 4, 8, 2])
        gbc = P.sb("gbc", [128, 2, 2, D], BF16)
        lnbc = P.sb("lnbc", [128, 4, D])
        pswb = P.sb("pswb", [128, 128], BF16)
        zqk = P.sb("zqk", [128, 12, 256], BF16)
        kseq = P.sb("kseq", [128, 6, 1536], BF16)
        ropeb = P.sb("ropeb", [128, 256], BF16)
        XS = [P.dram(f"XS{q}", [128, 2, D]) for q in range(4)]
        QS = [P.dram(f"QS{q}", [128, 6, 256], BF16) for q in range(4)]
        HS = [P.dram(f"HS{q}", [128, 4, 2, 256], BF16) for q in range(4)]
        KT = [P.dram(f"KT{q}", [128, 2, 2, 256], BF16) for q in range(4)]
        EE = [P.dram(f"EE{q}", [128, 2, 2, NCH]) for q in range(4)]
        IB = [P.dram(f"IB{q}", [128, 2, 256], BF16) for q in range(4)]
        WG = [P.dram(f"WG{q}", [128, 2, 256]) for q in range(4)]
        SS = [P.dram(f"SS{q}", [128, 2, 2, NCH + 1, 128], BF16) for q in range(4)]
        ftmp = [P.sb(f"ftmp{i}", [128, 256]) for i in range(4)]
        fctr = [0]
        ttmp = [P.sb(f"ttmp{i}", [128, 512]) for i in range(2)]
        tctr = [0]
        vaug = P.sb("vaug", [128, 12, 4, 129], BF16)
        vcaug = P.sb("vcaug", [128, 12, 2, 65], BF16)
        ibb = P.sb("ibb", [128, 2, 256], BF16)
        wg = P.sb("wg", [128, 2, 256])
        big = P.sb("big", [128, 9216], BF16)

        class View:
            def __init__(self, ap_fn):
                self._res = big._res
                self.ap_fn = ap_fn

            def __getitem__(self, k):
                return self.ap_fn()[k]
        Et = [View(lambda i=i: big[:, i * 3072:(i + 1) * 3072].rearrange("p (k q) -> p k q", k=12)) for i in range(3)]
        hidT = View(lambda: big[:, 0:8192].rearrange("p (k q) -> p k q", k=32))
        ectr = [0]
        mixed = P.sb("mixed", [128, 2, D])
        sq = P.sb("sq", [128, 2, 256])
        qt = [P.sb(f"qt{d}", [128, 2, 256], BF16) for d in range(2)]
        kt = [P.sb(f"kt{d}", [128, 2, 256], BF16) for d in range(2)]
        kt32 = P.sb("kt32", [128, 256])
        ktok = [P.sb(f"ktok{d}", [128, 2, 256], BF16) for d in range(2)]
        eend = [P.sb(f"eend{d}", [128, 2, NCH]) for d in range(2)]
        S32 = [P.sb(f"S32{d}", [128, 2, 2, 128]) for d in range(2)]
        Sbf = [P.sb(f"Sbf{d}", [128, 2, NCH + 1, 128], BF16) for d in range(2)]
        vblk = P.sb("vblk", [128, 2, 2, 4, 128], BF16)
        qblk2 = [P.sb(f"qblk{i}", [128, 4, 128], BF16) for i in range(2)]
        Ue4 = [P.sb(f"Ue{i}", [128, 4, 128]) for i in range(4)]
        Am = [P.sb(f"Am{i}", [128, 2, 128], BF16) for i in range(2)]
        ytmp = P.sb("ytmp", [128, D])
        small = P.sb("small", [128, 64])
        smallA = P.sb("smallA", [128, 64])
        lamt = P.sb("lamt", [128, 2, 4, 64]); lamc = P.sb("lamc", [128, 2, 4])
        subg = P.sb("subg", [128, 2, 128]); gng = P.sb("gng", [128, 2, 256])
        qkg = P.sb("qkg", [128, 2, 2])
        lbt = P.sb("lbt", [128, 2, 2, 2, 2])
        lbraw = P.sb("lbraw", [128, 2, 2, 2])
        stats = P.sb("stats", [128, 16])
        epsc = P.sb("epsc", [128, 1])
        stats2 = [stats, P.sb("statsB", [128, 16])]
        ohT = P.sb("ohT", [128, 8])

        def act(out, in_, func, reads, writes, **kw):
            return P.op("act", lambda e: e.activation(out=out, in_=in_, func=func, **kw), reads, writes)

        def dve(fn, reads, writes):
            return P.op("dve", fn, reads, writes)

        def next_ftmp():
            fctr[0] += 1
            return ftmp[fctr[0] % len(ftmp)]

        def next_ttmp():
            tctr[0] += 1
            return ttmp[tctr[0] % len(ttmp)]

        WB = {}

        def load_w(view, key=None, shape3=None):
            s = wr[wctr[0] % len(wr)]
            wctr[0] += 1
            a, b = view.shape[1], view.shape[2]
            n = a * b
            tile3 = s[:, 0:n].rearrange("p (a b) -> p a b", a=a)
            h = a // 2
            sub1 = 1 if n == 4096 else 0
            if key is None or key not in WB:
                P.dma("pool", tile3[:, 0:h, :], view[:, 0:h, :], [], [(s, 0)])
                P.dma("pool", tile3[:, h:a, :], view[:, h:a, :], [], [(s, sub1)])
                if key is not None:
                    WB[key] = P.dram("WB_%s" % "_".join(str(k) for k in key), [128, n], BF16)
                    P.dma("sp", WB[key].ap(), s[:, 0:n], [(s, 0), (s, 1)], [WB[key]], keep_queue=True)
            else:
                wv = WB[key].ap().rearrange("p (a b) -> p a b", a=a)
                P.dma("sp", tile3[:, 0:h, :], wv[:, 0:h, :], [WB[key]], [(s, 0)])
                P.dma("sp", tile3[:, h:a, :], wv[:, h:a, :], [WB[key]], [(s, sub1)])
            if n < 4096:
                h = a
            return s, tile3, h

        def wsub(s, h, k):
            return (s, 0 if k < h else 1)

        P.dma("sp", cst[:], consts, [], [cst])
        P.dma("sp", scT[:], cmodT, [], [scT])
        P.dma("sp", ohT[:], onehot, [], [ohT])
        P.dma("sp", lamt[:].rearrange("p l f d -> p (l f d)"), lamv.rearrange("l f d -> (l f d)").partition_broadcast(128), [], [lamt])
        P.dma("sp", subg[:].rearrange("p l d -> p (l d)"), subln_g.rearrange("l d -> (l d)").partition_broadcast(128), [], [subg])
        for li in range(2):
            for r in range(4):
                P.dma("sp", gng[:, li, r * 64:(r + 1) * 64], gnorm_g[li].partition_broadcast(128), [], [gng])
            for hh in range(2):
                P.dma("sp", qkg[hh * 64:(hh + 1) * 64, li, 0:1], qnorm_g[li].rearrange("(d o) -> d o", o=1), [], [qkg])
                P.dma("sp", qkg[hh * 64:(hh + 1) * 64, li, 1:2], knorm_g[li].rearrange("(d o) -> d o", o=1), [], [qkg])
        for d in range(2):
            for li in range(2):
                for c in range(2):
                    P.dma("sp", lbraw[:, d, li, c:c + 1], lbl[d, li, c * 128:(c + 1) * 128].rearrange("(p o) -> p o", o=1), [], [lbraw])
        dve(lambda e: e.tensor_copy(out=pswb[:], in_=cst[:, C_PSW:C_PSW + 128]), [cst], [pswb])
        dve(lambda e: e.memset(epsc[:], 1e-6), [], [epsc])
        for li in range(2):
            for j in range(2):
                dve(lambda e, li=li, j=j: e.tensor_tensor(out=lamt[:, li, 2 * j, :], in0=lamt[:, li, 2 * j, :],
                                                          in1=lamt[:, li, 2 * j + 1, :], op=ALU.mult), [lamt], [lamt])
                dve(lambda e, li=li, j=j: e.reduce_sum(out=lamc[:, li, j:j + 1], in_=lamt[:, li, 2 * j, :], axis=AX.X),
                    [lamt], [lamc])
            act(lamc[:, li, 0:2], lamc[:, li, 0:2], AF.Exp, [lamc], [lamc])
            lam_init = 0.8 - 0.6 * float(np.exp(-0.3 * li))
            dve(lambda e, li=li, lam_init=lam_init: e.scalar_tensor_tensor(
                out=lamc[:, li, 2:3], in0=lamc[:, li, 1:2], scalar=-lam_init, in1=lamc[:, li, 0:1],
                op0=ALU.add, op1=ALU.subtract), [lamc], [lamc])
            dve(lambda e, li=li, lam_init=lam_init: e.tensor_scalar_mul(out=subg[:, li, :], in0=subg[:, li, :],
                                                                      scalar1=1.0 - lam_init), [subg], [subg])
        for d in range(2):
            dve(lambda e, d=d: e.memset(lbt[:, d, 0, :, 0:1], 0.0), [], [lbt])
            dve(lambda e, d=d: e.memset(lbt[:, d, 0, :, 1:2], 1.0), [], [lbt])
            dve(lambda e, d=d: e.tensor_sub(out=lbraw[:, d, 1, :], in0=lbraw[:, d, 1, :], in1=lbraw[:, d, 0, :]), [lbraw], [lbraw])
            act(lbt[:, d, 1, :, 0], lbraw[:, d, 1, :], AF.Sigmoid, [lbraw], [lbt])
            dve(lambda e, d=d: e.tensor_scalar(out=lbt[:, d, 1, :, 1], in0=lbt[:, d, 1, :, 0], scalar1=-1.0, scalar2=1.0,
                                               op0=ALU.mult, op1=ALU.add), [lbt], [lbt])
        act(scTb[:], scT[:], AF.Silu, [scT], [scTb])
        dve(lambda e: e.memset(vaug[:, :, :, 128:129], 1.0), [], [vaug])
        dve(lambda e: e.memset(vcaug[:, :, :, 64:65], 1.0), [], [vcaug])

        def mod_part(li, part):
            if part == 0:
                P.dma("sp", lnbc[:].rearrange("p a d -> p (a d)"), lnp[li].rearrange("a d -> (a d)").partition_broadcast(128), [], [lnbc])
            wv = w_ada[li].rearrange("(kc p) n -> p kc n", p=128)
            psc = P.psum(pin=True)
            vmap = {1: 0, 0: 1, 4: 2, 3: 3}
            groups = range(0, 4) if part == 0 else range(4, 12)
            for g in groups:
                vec, hf = g // 2, g % 2
                s, t3, h = load_w(wv[:, :, g * 512:(g + 1) * 512])
                bt = bada[g % 2]
                P.dma("sp", bt[0:1, :], b_ada[li:li + 1, g * 512:(g + 1) * 512], [], [bt])
                P.dma("sp", bt[1:2, :], b_ada[li:li + 1, g * 512:(g + 1) * 512], [], [bt])
                ps = P.psum()
                for kc in range(8):
                    P.mm(ps[0:2, :], scTb[:, 2 * kc:2 * kc + 2], t3[:, kc, :], kc == 0, kc == 7,
                         [scTb, wsub(s, h, kc)], [ps])
                mr = modrows[g % 2]
                dve(lambda e, ps=ps, mr=mr, bt=bt: e.tensor_add(out=mr[:], in0=ps[0:2, :], in1=bt[:]), [ps, bt], [mr])
                if vec in vmap:
                    vi = vmap[vec]
                    for jj in range(4):
                        j = hf * 4 + jj
                        P.mm(psc[:, (vi * 8 + j) * 2:(vi * 8 + j) * 2 + 2], mr[0:2, jj * 128:(jj + 1) * 128],
                             cst[0:2, C_SEL2:C_SEL2 + 2], True, True, [mr, cst], [psc])
                else:
                    gi = 0 if vec == 2 else 1
                    for src in range(2):
                        ps2 = P.psum()
                        P.mm(ps2[:, :], cst[0:2, C_SELBC + src * 128:C_SELBC + (src + 1) * 128], mr[0:2, :],
                             True, True, [mr, cst], [ps2])
                        act(gbc[:, gi, src, hf * 512:(hf + 1) * 512], ps2[:, :], AF.Identity, [ps2], [gbc])
            v0 = 0 if part == 0 else 2
            dve(lambda e: e.tensor_copy(out=mc[:, v0:v0 + 2].rearrange("p a b c -> p (a b c)"), in_=psc[:, v0 * 16:(v0 + 2) * 16]), [psc], [mc])
            P.unpin(psc)
            dve(lambda e: e.tensor_scalar_add(out=mc[:, v0], in0=mc[:, v0], scalar1=1.0), [mc], [mc])
            dve(lambda e: e.memset(small[:, 55:56], 0.0), [], [ytmp, ttmp[0], ttmp[1]])

        def mod_T(u, li, which):
            src = 1 if u == 2 else 0
            for lt in range(2):
                for half in range(2):
                    ps = P.psum()
                    for jj in range(4):
                        j = half * 4 + jj
                        P.tr(ps[:, jj * 128:(jj + 1) * 128], x_tm[u][:, lt, j * 128:(j + 1) * 128], idf, [x_tm[u], cst], [ps])
                    eng_bank = evac_eng()
                    for jj in range(4):
                        j = half * 4 + jj
                        a_col = mc[:, 2 * which, j, src:src + 1]
                        b_col = mc[:, 2 * which + 1, j, src:src + 1]
                        o = hT[:, j, lt * 128:(lt + 1) * 128]
                        i_ = ps[:, jj * 128:(jj + 1) * 128]
                        if eng_bank == "act":
                            act(o, i_, AF.Identity, [ps, mc], [(hT, j)], scale=a_col, bias=b_col)
                        else:
                            dve(lambda e, o=o, i_=i_, a_col=a_col, b_col=b_col: e.tensor_scalar(
                                out=o, in0=i_, scalar1=a_col, scalar2=b_col, op0=ALU.mult, op1=ALU.add), [ps, mc], [(hT, j)])

        def layernorm_to_x(u, lt, li, site):
            for hf in range(2):
                dve(lambda e, hf=hf: e.bn_stats(out=stats[:, hf * 6:(hf + 1) * 6], in_=ytmp[:, hf * 512:(hf + 1) * 512]), [ytmp], [stats])
            dve(lambda e: e.bn_aggr(out=stats[:, 12:14], in_=stats[:, 0:12]), [stats], [stats])
            act(stats[:, 14:15], stats[:, 13:14], AF.Ln, [stats], [stats], bias=epsc[:, 0:1], scale=1.0)
            act(stats[:, 15:16], stats[:, 14:15], AF.Exp, [stats], [stats], scale=-0.5)
            dve(lambda e: e.tensor_scalar(out=ytmp[:], in0=ytmp[:], scalar1=stats[:, 12:13], scalar2=stats[:, 15:16],
                                          op0=ALU.subtract, op1=ALU.mult), [ytmp, stats], [ytmp])
            dve(lambda e: e.tensor_mul(out=ytmp[:], in0=ytmp[:], in1=lnbc[:, 2 * site, :]), [ytmp, lnbc], [ytmp])
            xt = x_tm[u]
            dve(lambda e, xt=xt: e.tensor_add(out=xt[:, lt, :], in0=ytmp[:], in1=lnbc[:, 2 * site + 1, :]), [ytmp, lnbc], [xt])

        def residual_ln(u, lt, li, site, banks):
            src = 1 if u == 2 else 0
            for hf in range(2):
                ps = banks[hf]
                t = next_ttmp()
                dve(lambda e, ps=ps, t=t, hf=hf: e.tensor_mul(out=t[:], in0=ps[:, :], in1=gbc[:, site, src, hf * 512:(hf + 1) * 512]),
                    [ps, gbc], [t])
                xt = x_tm[u]
                dve(lambda e, t=t, hf=hf, xt=xt: e.scalar_tensor_tensor(out=ytmp[:, hf * 512:(hf + 1) * 512],
                                                                        in0=xt[:, lt, hf * 512:(hf + 1) * 512], scalar=ALPHA, in1=t[:],
                                                                        op0=ALU.mult, op1=ALU.add), [t, xt], [ytmp])
            layernorm_to_x(u, lt, li, site)

        def g_resln(u, lt, li, site, banks):
            src = 1 if u == 2 else 0
            xt = x_tm[u]
            st = stats2[lt]
            yv = mixed[:, lt, :]
            yres = (mixed, lt)
            for hf in range(2):
                ps = banks[hf]
                t = next_ttmp()
                dve(lambda e, ps=ps, t=t, hf=hf: e.tensor_mul(out=t[:], in0=ps[:, :], in1=gbc[:, site, src, hf * 512:(hf + 1) * 512]),
                    [ps, gbc], [t])
                P.unpin(ps)
                dve(lambda e, t=t, hf=hf: e.scalar_tensor_tensor(out=yv[:, hf * 512:(hf + 1) * 512],
                                                                 in0=xt[:, lt, hf * 512:(hf + 1) * 512], scalar=ALPHA, in1=t[:],
                                                                 op0=ALU.mult, op1=ALU.add), [t, xt], [yres])
            for hf in range(2):
                dve(lambda e, hf=hf: e.bn_stats(out=st[:, hf * 6:(hf + 1) * 6], in_=yv[:, hf * 512:(hf + 1) * 512]), [yres], [st])
            dve(lambda e: e.bn_aggr(out=st[:, 12:14], in_=st[:, 0:12]), [st], [st])
            yield
            act(st[:, 14:15], st[:, 13:14], AF.Ln, [st], [st], bias=epsc[:, 0:1], scale=1.0)
            act(st[:, 15:16], st[:, 14:15], AF.Exp, [st], [st], scale=-0.5)
            yield
            dve(lambda e: e.tensor_scalar(out=yv, in0=yv, scalar1=st[:, 12:13], scalar2=st[:, 15:16],
                                          op0=ALU.subtract, op1=ALU.mult), [yres, st], [yres])
            dve(lambda e: e.tensor_mul(out=yv, in0=yv, in1=lnbc[:, 2 * site, :]), [yres, lnbc], [yres])
            dve(lambda e: e.tensor_add(out=xt[:, lt, :], in0=yv, in1=lnbc[:, 2 * site + 1, :]), [yres, lnbc], [xt])

        def rms_feat(ps, gcol, out_ap, out_res, reads_extra=()):
            t = next_ftmp()
            act(t[:], ps[:, 0:256], AF.Square, [ps], [t])
            ps2 = P.psum()
            P.mm(ps2[:, 0:256], cst[:, C_BONES:C_BONES + 128], t[:], True, True, [cst, t], [ps2])
            t2 = next_ftmp()
            act(t2[:], ps2[:, 0:256], AF.Ln, [ps2], [t2], bias=epsc[:, 0:1], scale=1.0)
            act(t2[:], t2[:], AF.Exp, [t2], [t2], scale=-0.5)
            dve(lambda e, t2=t2: e.scalar_tensor_tensor(out=out_ap, in0=ps[:, 0:256], scalar=gcol, in1=t2[:],
                                                        op0=ALU.mult, op1=ALU.mult), [ps, t2, qkg], [out_res])

        def attention(u, li, nk, keyA, keyC, keyCs, kres):
            dve(lambda e: e.memset(small[:, 58:59], 0.0), [], [big])
            chains = []
            for h in range(4):
                Es = []
                for m in range(2):
                    Es.append(Et[ectr[0] % 3])
                    ectr[0] += 1
                for kb in range(nk // 2):
                    pss = [P.psum(), P.psum()]
                    for k2 in range(2):
                        kc = kb * 2 + k2
                        for m in range(2):
                            P.mm(pss[m][:, k2 * 256:(k2 + 1) * 256], keyA(h, m, kc), zqk[m * 64:(m + 1) * 64, h, :], True, True,
                                 [zqk, kres], [pss[m]])
                    for m in range(2):
                        E = Es[m]
                        act(E[:, kb * 2:kb * 2 + 2, :], pss[m][:, :].rearrange("p (a b) -> p a b", a=2), AF.Exp, [pss[m]], [(E, ('E', id(E)))], scale=0.125)
                for lt in range(2):
                    po = P.psum()
                    for m in range(2):
                        for kc in range(nk):
                            P.mm(po[:, m * 129:(m + 1) * 129], Es[m][:, kc, lt * 128:(lt + 1) * 128], vaug[:, kc, h, :],
                                 kc == 0, kc == nk - 1, [(Es[m], ('E', id(Es[m]))), vaug], [po])
                    k = h * 2 + lt
                    sm = smallA[:, k * 8:(k + 1) * 8]
                    smr = (smallA, k)
                    actr[0] += 1
                    t1 = atmp[actr[0] % len(atmp)]
                    pv = po[:, 0:258].rearrange("p (m c) -> p m c", c=129)
                    dve(lambda e, pv=pv, sm=sm: e.reciprocal(out=sm[:, 0:2].rearrange("p (m o) -> p m o", o=1), in_=pv[:, :, 128:129]), [po], [smr])
                    dve(lambda e, sm=sm: e.tensor_mul(out=sm[:, 2:3], in0=sm[:, 1:2], in1=lamc[:, li, 2:3]), [smr, lamc], [smr])
                    dve(lambda e, po=po, t1=t1, sm=sm: e.tensor_scalar_mul(out=t1[:, 0:128], in0=po[:, 0:128], scalar1=sm[:, 0:1]), [po, smr], [t1])
                    dve(lambda e, po=po, t1=t1, sm=sm: e.scalar_tensor_tensor(out=t1[:, 0:128], in0=po[:, 129:257], scalar=sm[:, 2:3],
                                                                              in1=t1[:, 0:128], op0=ALU.mult, op1=ALU.add), [po, smr, t1], [t1])
                    dve(lambda e, sm=sm: e.memset(sm[:, 3:4], 0.0), [], [smr])

                    def chain(t1=t1, sm=sm, smr=smr, lt=lt, h=h):
                        act(t1[:, 128:256], t1[:, 0:128], AF.Square, [t1], [t1, smr], accum_out=sm[:, 3:4])
                        act(sm[:, 4:5], sm[:, 3:4], AF.Ln, [smr], [smr], bias=epsc[:, 0:1], scale=1.0 / 128)
                        act(sm[:, 5:6], sm[:, 4:5], AF.Exp, [smr], [smr], scale=-0.5)
                        yield
                        dve(lambda e: e.scalar_tensor_tensor(out=mixed[:, lt, h * 128:(h + 1) * 128], in0=t1[:, 0:128],
                                                             scalar=sm[:, 5:6], in1=subg[:, li, :],
                                                             op0=ALU.mult, op1=ALU.mult), [t1, smr, subg], [(mixed, lt)])
                    chains.append(chain())
            pos = [P.psum(pin=True), P.psum(pin=True)]
            for hp in range(2):
                hqs = (2 * hp, 2 * hp + 1)
                Ec = {}
                for hq in hqs:
                    Ec[hq] = Et[ectr[0] % 3]
                    ectr[0] += 1
                for kb in range(nk // 2):
                    pss = {hq: P.psum() for hq in hqs}
                    for k2 in range(2):
                        kc = kb * 2 + k2
                        for hq in hqs:
                            pb = (hq % 2) * 64
                            kfn = keyC if hq in (0, 3) else keyCs
                            P.mm(pss[hq][:, k2 * 256:(k2 + 1) * 256], kfn(pb, kc), zqk[pb:pb + 64, 8 + hq // 2, :], True, True, [zqk, kres], [pss[hq]])
                    for hq in hqs:
                        E = Ec[hq]
                        act(E[:, kb * 2:kb * 2 + 2, :], pss[hq][:, :].rearrange("p (a b) -> p a b", a=2), AF.Exp, [pss[hq]], [(E, ('E', id(E)))], scale=0.125)
                for hq in hqs:
                    E = Ec[hq]
                    for lt in range(2):
                        for kc in range(nk):
                            P.mm(pos[lt][:, hq * 65:(hq + 1) * 65], E[:, kc, lt * 128:(lt + 1) * 128], vcaug[:, kc, hq // 2, :],
                                 kc == 0, kc == nk - 1, [(E, ('E', id(E))), vcaug], [pos[lt]])
            run_rr(chains)
            for lt in range(2):
                pv = pos[lt][:, 0:260].rearrange("p (h c) -> p h c", c=65)
                dve(lambda e, pv=pv: e.reciprocal(out=small[:, 8:12].rearrange("p (h o) -> p h o", o=1), in_=pv[:, :, 64:65]), [pos[lt]], [small])
                dve(lambda e, pv=pv, lt=lt: e.tensor_tensor(out=mixed[:, lt, 768:1024].rearrange("p (h c) -> p h c", c=64), in0=pv[:, :, 0:64],
                                                            in1=small[:, 8:12].rearrange("p (h o) -> p h o", o=1).to_broadcast([128, 4, 64]),
                                                            op=ALU.mult), [pos[lt], small], [(mixed, lt)])
                P.unpin(pos[lt])

        def hgrn_elementwise(li, d, c, sig):
            lbc = lbt[:, d, li, c, 0:1]; omc = lbt[:, d, li, c, 1:2]
            dve(lambda e: e.tensor_scalar(out=sig[:], in0=sig[:], scalar1=omc, scalar2=lbc, op0=ALU.mult, op1=ALU.add), [sig, lbt], [sig])
            dve(lambda e: e.tensor_scalar_max(out=sig[:], in0=sig[:], scalar1=F_MIN), [sig], [sig])
            g = next_ftmp()
            act(g[:], sig[:], AF.Ln, [sig], [g])
            b = next_ftmp()
            dve(lambda e: e.tensor_tensor_scan(out=b[:], data0=cst[:, C_RESET:C_RESET + 256], data1=g[:], initial=0.0,
                                               op0=ALU.mult, op1=ALU.add), [cst, g], [b])
            if d == 1:
                dve(lambda e: e.tensor_sub(out=kt32[:], in0=g[:], in1=b[:]), [g, b], [kt32])
                dve(lambda e: e.tensor_tensor(out=g[:].rearrange("p (j t) -> p j t", t=CH), in0=kt32[:].rearrange("p (j t) -> p j t", t=CH),
                                              in1=b[:].rearrange("p (j t) -> p j t", t=CH)[:, :, CH - 1:CH].to_broadcast([128, NCH, CH]),
                                              op=ALU.add), [kt32, b], [g])
                bb, eb = g, b
            else:
                bb, eb = b, g
            dve(lambda e: e.tensor_scalar(out=sig[:], in0=sig[:], scalar1=-1.0, scalar2=1.0, op0=ALU.mult, op1=ALU.add), [sig], [sig])
            act(eb[:], bb[:], AF.Exp, [bb], [eb])
            act(bb[:], bb[:], AF.Exp, [bb], [bb], scale=-1.0)
            pos_e = CH - 1 if d == 0 else 0
            dve(lambda e: e.tensor_copy(out=eend[d][:, c, :], in_=eb[:].rearrange("p (j t) -> p j t", t=CH)[:, :, pos_e]), [eb], [eend[d]])
            dve(lambda e: e.tensor_mul(out=qt[d][:, c, :], in0=sq[:, c, :], in1=eb[:]), [sq, eb], [(qt[d], c)])
            dve(lambda e: e.tensor_mul(out=kt32[:], in0=sig[:], in1=bb[:]), [sig, bb], [kt32])
            dve(lambda e: e.tensor_copy(out=kt[d][:, c, :], in_=kt32[:]), [kt32], [(kt[d], c)])
            ps = P.psum()
            for lt in range(2):
                P.tr(ps[:, lt * 128:(lt + 1) * 128], kt32[:, lt * 128:(lt + 1) * 128], idf, [kt32, cst], [ps])
            act(ktok[d][:, :, c * 128:(c + 1) * 128], ps[:, 0:256].rearrange("p (l k) -> p l k", l=2), AF.Identity, [ps], [(ktok[d], c)])

        def hgrn_scan(u, li, dirs=(0, 1), ibs=None, vbs=None):
            if ibs is None:
                pairs = [(ibb, vblk)]
                vbs = {d: vblk for d in dirs}
            else:
                pairs = [(ibs[d], vbs[d]) for d in dirs]
            for ibx, vbx in pairs:
                for c in range(2):
                    for lt in range(2):
                        dve(lambda e, c=c, lt=lt, ibx=ibx, vbx=vbx: e.tensor_tensor(
                            out=vbx[:, lt, c, :, :], in0=ibx[:, lt, c * 128:(c + 1) * 128].unsqueeze(1).to_broadcast([128, 4, 128]),
                            in1=cst[:, C_ROW:C_ROW + 4].unsqueeze(2).to_broadcast([128, 4, 128]), op=ALU.mult), [ibx, cst], [vbx])

            def chain(d, c, UeT):
                vblk = vbs[d]
                for step in range(2):
                    lt = step if d == 0 else 1 - step
                    ps = P.psum()
                    P.mm(ps[:, :], ktok[d][:, lt, c * 128:(c + 1) * 128], vblk[:, lt, c, :, :].rearrange("p j v -> p (j v)"), True, True,
                         [(ktok[d], c), vblk], [ps])
                    dve(lambda e, ps=ps: e.tensor_tensor(out=UeT[:], in0=ps[:, :].rearrange("p (j v) -> p j v", j=4),
                                                         in1=cst[:, C_BLK2:C_BLK2 + 128].unsqueeze(1).to_broadcast([128, 4, 128]),
                                                         op=ALU.mult), [ps, cst], [UeT])
                    yield
                    dve(lambda e, lt=lt: e.tensor_tensor(out=UeT[:], in0=UeT[:],
                                                         in1=eend[d][:, c, lt * 4:(lt + 1) * 4].unsqueeze(2).to_broadcast([128, 4, 128]),
                                                         op=ALU.mult), [UeT, eend[d]], [UeT])
                    yield
                    for s4 in range(4):
                        jl = s4 if d == 0 else 3 - s4
                        j = lt * 4 + jl
                        slot = step * 4 + s4
                        pp = slot % 2
                        if slot == 0:
                            act(Sbf[d][:, c, 0, :], S32[d][:, c, 0, :], AF.Identity, [(S32[d], (c, 0))], [(Sbf[d], c)])
                        dve(lambda e, j=j, jl=jl, pp=pp: e.scalar_tensor_tensor(
                            out=S32[d][:, c, 1 - pp, :], in0=S32[d][:, c, pp, :], scalar=eend[d][:, c, j:j + 1], in1=UeT[:, jl, :],
                            op0=ALU.mult, op1=ALU.add), [UeT, eend[d], (S32[d], (c, pp))], [(S32[d], (c, 1 - pp))])
                        act(Sbf[d][:, c, slot + 1, :], S32[d][:, c, 1 - pp, :], AF.Identity, [(S32[d], (c, 1 - pp))], [(Sbf[d], c)])
                        yield

            gens = [chain(d, c, Ue4[(d * 2 + c) % 4]) for d in dirs for c in range(2)]
            while gens:
                for g in list(gens):
                    try:
                        next(g)
                    except StopIteration:
                        gens.remove(g)

        def hgrn_output(u, li, copy_state=True):
            combos = [(lt, c) for lt in range(2) for c in range(2)]

            def qb(i):
                return SubView(big, ("qb", i), lambda i=i: big[:, i * 512:(i + 1) * 512].rearrange("p (j t) -> p j t", j=4))

            def am(i):
                return SubView(big, ("am", i), lambda i=i: big[:, 4096 + i * 128:4096 + (i + 1) * 128])

            for ci, (lt, c) in enumerate(combos):
                for d in range(2):
                    Q = qb(ci * 2 + d)
                    dve(lambda e, d=d, c=c, lt=lt, Q=Q: e.tensor_tensor(
                        out=Q[:], in0=qt[d][:, c, lt * 128:(lt + 1) * 128].unsqueeze(1).to_broadcast([128, 4, 128]),
                        in1=cst[:, C_COL:C_COL + 512].rearrange("p (j t) -> p j t", j=4), op=ALU.mult), [(qt[d], c), cst], [Q])
            for ci, (lt, c) in enumerate(combos):
                for d in range(2):
                    moff = C_MF if d == 0 else C_MB
                    for hh in range(2):
                        pa = P.psum()
                        P.mm(pa[:, 0:128], kt[d][hh * 64:(hh + 1) * 64, c, lt * 128:(lt + 1) * 128],
                             qt[d][hh * 64:(hh + 1) * 64, c, lt * 128:(lt + 1) * 128], True, True, [(kt[d], c), (qt[d], c)], [pa])
                        A = am((ci * 2 + d) * 2 + hh)
                        dve(lambda e, pa=pa, A=A, moff=moff: e.tensor_tensor(
                            out=A[:], in0=pa[:, 0:128], in1=cst[:, moff:moff + 128], op=ALU.mult), [pa, cst], [A])
            chains = []
            for ci, (lt, c) in enumerate(combos):
                po = P.psum(pin=True)
                first = True
                for d in range(2):
                    Q = qb(ci * 2 + d)
                    for jl in range(4):
                        j = lt * 4 + jl
                        slot = j if d == 0 else (NCH - 1 - j)
                        P.mm(po[:, 0:128], Q[:, jl, :], Sbf[d][:, c, slot, :], first, False, [Q, (Sbf[d], c)], [po])
                        first = False
                for d in range(2):
                    for hh in range(2):
                        A = am((ci * 2 + d) * 2 + hh)
                        last = (d == 1 and hh == 1)
                        P.mm(po[:, hh * 64:(hh + 1) * 64], A[:], ibb[:, lt, (c * 2 + hh) * 64:(c * 2 + hh + 1) * 64], False, last,
                             [A, ibb], [po])

                def norm_chain(po=po, lt=lt, c=c, ci=ci):
                    actr[0] += 1
                    t = atmp[actr[0] % len(atmp)]
                    sm = smallA[:, ci * 8:(ci + 1) * 8]
                    smr = (smallA, ci)
                    act(t[:, 0:128], po[:, 0:128], AF.Square, [po], [t])
                    yield
                    dve(lambda e: e.reduce_sum(out=sm[:, 0:2], in_=t[:, 0:128].rearrange("p (h v) -> p h v", h=2), axis=AX.X), [t], [smr])
                    yield
                    act(sm[:, 2:4], sm[:, 0:2], AF.Ln, [smr], [smr], bias=epsc[:, 0:1], scale=1.0 / 64)
                    act(sm[:, 4:6], sm[:, 2:4], AF.Exp, [smr], [smr], scale=-0.5)
                    yield
                    dve(lambda e: e.tensor_tensor(out=t[:, 128:256].rearrange("p (h v) -> p h v", h=2),
                                                  in0=po[:, 0:128].rearrange("p (h v) -> p h v", h=2),
                                                  in1=sm[:, 4:6].unsqueeze(2).to_broadcast([128, 2, 64]), op=ALU.mult),
                        [po, smr], [t])
                    P.unpin(po)
                    dve(lambda e: e.tensor_mul(out=mixed[:, lt, 512 + c * 128:512 + (c + 1) * 128], in0=t[:, 128:256],
                                               in1=wg[:, lt, c * 128:(c + 1) * 128]), [t, wg], [(mixed, lt)])
                chains.append(norm_chain())
            run_rr(chains)

        class SubView:
            def __init__(self, parent, sub, fn):
                self._res = parent._res
                self._sub = sub
                self.fn = fn

            def __getitem__(self, k):
                return self.fn()[k]

        ptmp = list(ftmp) + [SubView(mixed, ("f", i), lambda i=i: mixed[:].rearrange("p t d -> p (t d)")[:, i * 256:(i + 1) * 256])
                             for i in range(8)]
        pctr = [0]

        def next_ptmp():
            pctr[0] += 1
            return ptmp[pctr[0] % len(ptmp)]

        ropebs = [SubView(big, ("rp", i), lambda i=i: big[:, i * 256:(i + 1) * 256]) for i in range(4)]
        atmp = list(ftmp) + [SubView(ytmp, ("y", i), lambda i=i: ytmp[:, i * 256:(i + 1) * 256]) for i in range(4)]
        bada = [SubView(ttmp[i], ("bada",), lambda i=i: ttmp[i][0:2, :]) for i in range(2)]
        modrows = [SubView(ytmp, ("mr", i), lambda i=i: ytmp[0:2, i * 512:(i + 1) * 512]) for i in range(2)]
        actr = [0]
        rctr = [0]

        def run_rr(gens):
            gens = list(gens)
            while gens:
                for g in list(gens):
                    try:
                        next(g)
                    except StopIteration:
                        gens.remove(g)

        def g_rope(ps, tin, out_ap, out_res, q):
            rb = ropebs[rctr[0] % 4]
            rctr[0] += 1
            if ps is not None:
                t = next_ptmp()
                act(t[:], ps[:, 0:256], AF.Identity, [ps], [t])
                act(rb[:], ps[:, 0:256], AF.Identity, [ps], [rb])
                P.unpin(ps)
            else:
                t = tin
                act(rb[:], t[:], AF.Identity, [t], [rb])
            yield
            pr = P.psum(pin=True)
            P.mm(pr[:, 0:256], pswb[:], rb[:], True, True, [pswb, rb], [pr])
            yield
            dve(lambda e, t=t: e.tensor_mul(out=t[:], in0=t[:], in1=cst[:, C_RC + q * 256:C_RC + (q + 1) * 256]), [t, cst], [t])
            t2 = next_ptmp()
            dve(lambda e, t2=t2, pr=pr: e.tensor_mul(out=t2[:], in0=pr[:, 0:256], in1=cst[:, C_RS + q * 256:C_RS + (q + 1) * 256]), [pr, cst], [t2])
            P.unpin(pr)
            dve(lambda e, t=t, t2=t2: e.tensor_add(out=out_ap, in0=t[:], in1=t2[:]), [t, t2], [out_res])

        def g_rms(ps, gcol, out_ap, out_res):
            t = next_ptmp()
            act(t[:], ps[:, 0:256], AF.Square, [ps], [t])
            yield
            ps2 = P.psum(pin=True)
            P.mm(ps2[:, 0:256], cst[:, C_BONES:C_BONES + 128], t[:], True, True, [cst, t], [ps2])
            yield
            t2 = next_ptmp()
            act(t2[:], ps2[:, 0:256], AF.Ln, [ps2], [t2], bias=epsc[:, 0:1], scale=1.0)
            P.unpin(ps2)
            act(t2[:], t2[:], AF.Exp, [t2], [t2], scale=-0.5)
            yield
            dve(lambda e, t2=t2: e.scalar_tensor_tensor(out=out_ap, in0=ps[:, 0:256], scalar=gcol, in1=t2[:],
                                                        op0=ALU.mult, op1=ALU.mult), [ps, t2, qkg], [out_res])
            P.unpin(ps)

        def g_hgrn(li, d, c, ps):
            sig = next_ptmp()
            act(sig[:], ps[:, 0:256], AF.Sigmoid, [ps], [sig])
            P.unpin(ps)
            yield
            lbc = lbt[:, d, li, c, 0:1]
            omc = lbt[:, d, li, c, 1:2]
            dve(lambda e: e.tensor_scalar(out=sig[:], in0=sig[:], scalar1=omc, scalar2=lbc, op0=ALU.mult, op1=ALU.add), [sig, lbt], [sig])
            dve(lambda e: e.tensor_scalar_max(out=sig[:], in0=sig[:], scalar1=F_MIN), [sig], [sig])
            yield
            g = next_ptmp()
            act(g[:], sig[:], AF.Ln, [sig], [g])
            yield
            b = next_ptmp()
            k32 = next_ptmp()
            dve(lambda e: e.tensor_tensor_scan(out=b[:], data0=cst[:, C_RESET:C_RESET + 256], data1=g[:], initial=0.0,
                                               op0=ALU.mult, op1=ALU.add), [cst, g], [b])
            if d == 1:
                dve(lambda e: e.tensor_sub(out=k32[:], in0=g[:], in1=b[:]), [g, b], [k32])
                dve(lambda e: e.tensor_tensor(out=g[:].rearrange("p (j t) -> p j t", t=CH), in0=k32[:].rearrange("p (j t) -> p j t", t=CH),
                                              in1=b[:].rearrange("p (j t) -> p j t", t=CH)[:, :, CH - 1:CH].to_broadcast([128, NCH, CH]),
                                              op=ALU.add), [k32, b], [g])
                bb, eb = g, b
            else:
                bb, eb = b, g
            dve(lambda e: e.tensor_scalar(out=sig[:], in0=sig[:], scalar1=-1.0, scalar2=1.0, op0=ALU.mult, op1=ALU.add), [sig], [sig])
            yield
            act(eb[:], bb[:], AF.Exp, [bb], [eb])
            act(bb[:], bb[:], AF.Exp, [bb], [bb], scale=-1.0)
            yield
            pos_e = CH - 1 if d == 0 else 0
            dve(lambda e: e.tensor_copy(out=eend[d][:, c, :], in_=eb[:].rearrange("p (j t) -> p j t", t=CH)[:, :, pos_e]), [eb], [eend[d]])
            dve(lambda e: e.tensor_mul(out=qt[d][:, c, :], in0=sq[:, c, :], in1=eb[:]), [sq, eb], [(qt[d], c)])
            dve(lambda e: e.tensor_mul(out=k32[:], in0=sig[:], in1=bb[:]), [sig, bb], [k32])
            dve(lambda e: e.tensor_copy(out=kt[d][:, c, :], in_=k32[:]), [k32], [(kt[d], c)])
            yield
            pt = P.psum(pin=True)
            for lt in range(2):
                P.tr(pt[:, lt * 128:(lt + 1) * 128], k32[:, lt * 128:(lt + 1) * 128], idf, [k32, cst], [pt])
            yield
            act(ktok[d][:, :, c * 128:(c + 1) * 128], pt[:, 0:256].rearrange("p (l k) -> p l k", l=2), AF.Identity, [pt], [(ktok[d], c)])
            P.unpin(pt)

        def g_out_T(t, dst_ap):
            pt = P.psum(pin=True)
            for lt in range(2):
                P.tr(pt[:, lt * 128:(lt + 1) * 128], t[:, lt * 128:(lt + 1) * 128], idf, [t, cst], [pt])
            yield
            t2 = next_ptmp()
            act(t2[:], pt[:, 0:256], AF.Identity, [pt], [t2])
            P.unpin(pt)
            P.dma("sp", dst_ap, t2[:].rearrange("p (l f) -> p l f", l=2), [t2], [])

        def proj_phase(u, li, q=0):
            is_s = (u == 2)
            mark = lambda nm: (print("MARK", li, u, nm, len(P.ops)) if os.environ.get("DEBUGP") else None)
            mark("start")
            dve(lambda e: e.memset(small[:, 59:60], 0.0), [], [mixed, big])
            mod_T(u, li, 0)
            mark("modT")
            win = w_in[li].rearrange("(kc p) n -> p kc n", p=128)
            pieces = [
                (0, [("qa", 0), ("qa", 1), ("qa", 2), ("qa", 3)], []),
                (512, [("ka", 0), ("ka", 1), ("ka", 2), ("ka", 3)], []),
                (1024, [], [("va", 0, 512)]),
                (1536, [("qb", 0), ("qb", 1), ("ff", 0), ("ff", 1)], []),
                (2048, [("fb", 0), ("fb", 1)], [("ib", 256, 256)]),
                (2560, [None, None, ("qc", 0), ("qc", 1)], [("gb", 0, 256)]),
                (3072, [("kc", 0)], [("vc", 128, 128)]),
            ]

            def f_handler(kind, idx, ps):
                if kind == "qa" or kind == "ka":
                    zc = idx if kind == "qa" else 4 + idx
                    if not is_s:
                        act(zqk[:, zc, :], ps[:, 0:256], AF.Identity, [ps], [(zqk, zc)])
                        if kind == "ka":
                            t = next_ptmp()
                            act(t[:], ps[:, 0:256], AF.Identity, [ps], [t])
                            P.unpin(ps)
                            yield
                            yield from g_out_T(t, o_ak[u, li, :, idx * 128:(idx + 1) * 128].rearrange("(l p) f -> p l f", p=128))
                        else:
                            P.unpin(ps)
                    elif kind == "qa":
                        yield from g_rope(ps, None, zqk[:, zc, :], (zqk, zc), q)
                    else:
                        yield from g_rope(ps, None, kseq[:, idx, q * 256:(q + 1) * 256], (kseq, (idx, q)), q)
                elif kind == "qc":
                    gcol = qkg[:, li, 0:1]
                    if not is_s:
                        yield from g_rms(ps, gcol, zqk[:, 8 + idx, :], (zqk, 8 + idx))
                    else:
                        t = next_ptmp()
                        yield from g_rms(ps, gcol, t[:], t)
                        yield
                        yield from g_rope(None, t, zqk[:, 8 + idx, :], (zqk, 8 + idx), q)
                elif kind == "kc":
                    gcol = qkg[:, li, 1:2]
                    t = next_ptmp()
                    yield from g_rms(ps, gcol, t[:], t)
                    yield
                    if not is_s:
                        dve(lambda e, t=t: e.tensor_copy(out=zqk[:, 10, :], in_=t[:]), [t], [(zqk, 10)])
                        P.dma("sp", zqk[0:64, 11, :], zqk[64:128, 10, :], [(zqk, 10)], [(zqk, 11)])
                        P.dma("sp", zqk[64:128, 11, :], zqk[0:64, 10, :], [(zqk, 10)], [(zqk, 11)])
                        yield from g_out_T(t, o_ck[u, li, :, :].rearrange("(l p) f -> p l f", p=128))
                    else:
                        yield from g_rope(None, t, kseq[:, 4, q * 256:(q + 1) * 256], (kseq, (4, q)), q)
                        P.dma("sp", kseq[0:64, 5, q * 256:(q + 1) * 256], kseq[64:128, 4, q * 256:(q + 1) * 256], [(kseq, (4, q))], [(kseq, (5, q))])
                        P.dma("sp", kseq[64:128, 5, q * 256:(q + 1) * 256], kseq[0:64, 4, q * 256:(q + 1) * 256], [(kseq, (4, q))], [(kseq, (5, q))])
                elif kind == "qb":
                    act(sq[:, idx, :], ps[:, 0:256], AF.Silu, [ps], [sq])
                    P.unpin(ps)
                elif kind in ("ff", "fb"):
                    yield from g_hgrn(li, 0 if kind == "ff" else 1, idx, ps)

            def t_handler(kind, lt, ps):
                vk = (2 * q + lt) if is_s else lt
                if kind == "va":
                    act(vaug[:, vk, :, 0:128], ps[:, :].rearrange("p (h v) -> p h v", h=4), AF.Identity, [ps], [vaug])
                    if not is_s:
                        t = next_ttmp()
                        act(t[:], ps[:, :], AF.Identity, [ps], [t])
                        P.dma("sp", o_av[u, li, lt * 128:(lt + 1) * 128, :], t[:], [t], [])
                elif kind == "vc":
                    act(vcaug[:, vk, :, 0:64], ps[:, 0:128].rearrange("p (h v) -> p h v", h=2), AF.Identity, [ps], [vcaug])
                    if not is_s:
                        t = next_ttmp()
                        act(t[:, 0:128], ps[:, 0:128], AF.Identity, [ps], [t])
                        P.dma("sp", o_cv[u, li, lt * 128:(lt + 1) * 128, :], t[:, 0:128], [t], [])
                elif kind == "ib":
                    act(ibb[:, lt, :], ps[:, 0:256], AF.Identity, [ps], [ibb])
                elif kind == "gb":
                    t = next_ptmp()
                    act(t[:], ps[:, 0:256], AF.Silu, [ps], [t])
                    yield
                    dve(lambda e, t=t, lt=lt: e.tensor_mul(out=wg[:, lt, :], in0=t[:], in1=gng[:, li, :]), [t, gng], [wg])
                P.unpin(ps)
                return
                yield

            for col0, fch, tgr in pieces:
                ncol = min(512, NIN - col0)
                s, t3, h = load_w(win[:, :, col0:col0 + ncol], key=("in", li, col0))
                gens = []
                for ci, fc in enumerate(fch):
                    if fc is None:
                        continue
                    kind, idx = fc
                    mark("F " + kind + str(idx))
                    ps = P.psum(pin=True)
                    for kc in range(8):
                        P.mm(ps[:, 0:256], t3[:, kc, ci * 128:(ci + 1) * 128], hT[:, kc, :], kc == 0, kc == 7, [hT, wsub(s, h, kc)], [ps])
                    gens.append(f_handler(kind, idx, ps))
                for (kind, lc0, n) in tgr:
                    mark("T " + kind)
                    for lt in range(2):
                        ps = P.psum(pin=True)
                        for kc in range(8):
                            P.mm(ps[:, 0:n], hT[:, kc, lt * 128:(lt + 1) * 128], t3[:, kc, lc0:lc0 + n], kc == 0, kc == 7, [hT, wsub(s, h, kc)], [ps])
                        gens.append(t_handler(kind, lt, ps))
                run_rr(gens)

        def prompt_mixers(u, li):
            mark = lambda nm: (print("MARK", li, u, nm, len(P.ops)) if os.environ.get("DEBUGP") else None)
            dve(lambda e: e.memset(small[:, 61:62], 0.0), [], [big, mixed])
            for d in range(2):
                dve(lambda e, d=d: e.memset(S32[d][:, :, 0, :], 0.0), [], [S32[d]])
            mark("scan")
            hgrn_scan(u, li)
            mark("hout")
            for d in range(2):
                dst = o_sf if d == 0 else o_sb
                for hd in range(4):
                    c, hh = hd // 2, hd % 2
                    P.dma("sp", dst[u, li, hd, :, :], S32[d][hh * 64:(hh + 1) * 64, c, 0, hh * 64:(hh + 1) * 64], [S32[d]], [])
            hgrn_output(u, li)
            mark("attn")
            attention(u, li, 2,
                      lambda h, m, kc: zqk[m * 64:(m + 1) * 64, 4 + h, kc * 128:(kc + 1) * 128],
                      lambda pb, kc: zqk[pb:pb + 64, 10, kc * 128:(kc + 1) * 128],
                      lambda pb, kc: zqk[pb:pb + 64, 11, kc * 128:(kc + 1) * 128], zqk)

        def tail_phase(u, li):
            mark = lambda nm: (print("MARK", li, u, nm, len(P.ops)) if os.environ.get("DEBUGP") else None)
            mark("outproj")
            for lt in range(2):
                for half in range(2):
                    ps = P.psum()
                    for jj in range(4):
                        j = half * 4 + jj
                        P.tr(ps[:, jj * 128:(jj + 1) * 128], mixed[:, lt, j * 128:(j + 1) * 128], idf, [(mixed, lt), cst], [ps])
                    if evac_eng() == "act":
                        act(hT[:, half * 4:half * 4 + 4, lt * 128:(lt + 1) * 128], ps[:, :].rearrange("p (j t) -> p j t", j=4), AF.Identity, [ps], [hT])
                    else:
                        dve(lambda e, ps=ps, half=half, lt=lt: e.tensor_copy(out=hT[:, half * 4:half * 4 + 4, lt * 128:(lt + 1) * 128],
                                                                            in_=ps[:, :].rearrange("p (j t) -> p j t", j=4)), [ps], [hT])
            wo = w_out[li].rearrange("(kc p) n -> p kc n", p=128)
            slots = [load_w(wo[:, :, hf * 512:(hf + 1) * 512], key=("out", li, hf)) for hf in range(2)]
            chains = []
            for lt in range(2):
                banks = []
                for hf in range(2):
                    s, t3, h = slots[hf]
                    ps = P.psum(pin=True)
                    for kc in range(8):
                        P.mm(ps[:, :], hT[:, kc, lt * 128:(lt + 1) * 128], t3[:, kc, :], kc == 0, kc == 7, [hT, wsub(s, h, kc)], [ps])
                    banks.append(ps)
                chains.append(g_resln(u, lt, li, 0, banks))
            run_rr(chains)
            mark("ln1 done")
            mark("mlp")
            mod_T(u, li, 1)
            dve(lambda e: e.memset(small[:, 60:61], 0.0), [], [big])
            w1 = w_ff1[li].rearrange("(kc p) n -> p kc n", p=128)
            for pc in range(8):
                s, t3, h = load_w(w1[:, :, pc * 512:(pc + 1) * 512], key=("f1", li, pc))
                for ci in range(4):
                    ps = P.psum()
                    for kc in range(8):
                        P.mm(ps[:, 0:256], t3[:, kc, ci * 128:(ci + 1) * 128], hT[:, kc, :], kc == 0, kc == 7, [hT, wsub(s, h, kc)], [ps])
                    t = next_ftmp()
                    act(t[:], ps[:, 0:256], AF.Relu, [ps], [t])
                    dve(lambda e, t=t, pc=pc, ci=ci: e.tensor_mul(out=hidT[:, pc * 4 + ci, :], in0=t[:], in1=t[:]), [t], [(hidT, ('h', pc * 4 + ci))])
            mark("ff2")
            w2 = w_ff2[li].rearrange("(kc p) n -> p kc n", p=128)
            banks = [[P.psum(pin=True), P.psum(pin=True)] for lt in range(2)]
            for pc in range(8):
                s, t3, h = load_w(w2[:, pc * 4:(pc + 1) * 4, :], key=("f2", li, pc))
                for lt in range(2):
                    for hf in range(2):
                        for kl in range(4):
                            kc = pc * 4 + kl
                            P.mm(banks[lt][hf][:, :], hidT[:, kc, lt * 128:(lt + 1) * 128], t3[:, kl, hf * 512:(hf + 1) * 512],
                                 kc == 0, kc == 31, [(hidT, ('h', kc)), wsub(s, h, kl)], [banks[lt][hf]])
            run_rr([g_resln(u, lt, li, 1, banks[lt]) for lt in range(2)])

        def rope(ps, tin, out_ap, out_res, q):
            if ps is not None:
                t = next_ftmp()
                act(t[:], ps[:, 0:256], AF.Identity, [ps], [t])
                act(ropeb[:], ps[:, 0:256], AF.Identity, [ps], [ropeb])
            else:
                t = tin
                act(ropeb[:], t[:], AF.Identity, [t], [ropeb])
            pr = P.psum()
            P.mm(pr[:, 0:256], pswb[:], ropeb[:], True, True, [pswb, ropeb], [pr])
            dve(lambda e, t=t: e.tensor_mul(out=t[:], in0=t[:], in1=cst[:, C_RC + q * 256:C_RC + (q + 1) * 256]), [t, cst], [t])
            t2 = next_ftmp()
            dve(lambda e, t2=t2, pr=pr: e.tensor_mul(out=t2[:], in0=pr[:, 0:256], in1=cst[:, C_RS + q * 256:C_RS + (q + 1) * 256]), [pr, cst], [t2])
            dve(lambda e, t=t, t2=t2: e.tensor_add(out=out_ap, in0=t[:], in1=t2[:]), [t, t2], [out_res])

        def xsrc(li, q):
            if li == 0:
                return xs[q * 256:(q + 1) * 256, :].rearrange("(t p) d -> p t d", p=128), []
            return XS[q].ap(), [XS[q]]

        def sample_layer(li, x0_loaded=False):
            for kc in range(4):
                t = next_ttmp()
                P.dma("sp", t[:], ck_a[li, kc * 128:(kc + 1) * 128, :], [], [t])
                ps = P.psum()
                for h in range(4):
                    P.tr(ps[:, h * 128:(h + 1) * 128], t[:, h * 128:(h + 1) * 128], idf, [t, cst], [ps])
                act(kseq[:, 0:4, 1024 + kc * 128:1024 + (kc + 1) * 128], ps[:, :].rearrange("p (h k) -> p h k", h=4), AF.Identity,
                    [ps], [(kseq, ("c", kc))])
                t2 = next_ttmp()
                P.dma("sp", t2[:, 0:128], ck_c[li, kc * 128:(kc + 1) * 128, :], [], [t2])
                ps2 = P.psum()
                P.tr(ps2[:, 0:128], t2[:, 0:128], idf, [t2, cst], [ps2])
                act(kseq[:, 4, 1024 + kc * 128:1024 + (kc + 1) * 128], ps2[:, 0:128], AF.Identity, [ps2], [(kseq, ("cc", kc))])
                P.dma("sp", kseq[0:64, 5, 1024 + kc * 128:1024 + (kc + 1) * 128], kseq[64:128, 4, 1024 + kc * 128:1024 + (kc + 1) * 128],
                      [(kseq, ("cc", kc))], [(kseq, ("cs", kc))])
                P.dma("sp", kseq[64:128, 5, 1024 + kc * 128:1024 + (kc + 1) * 128], kseq[0:64, 4, 1024 + kc * 128:1024 + (kc + 1) * 128],
                      [(kseq, ("cc", kc))], [(kseq, ("cs", kc))])
                P.dma("pool", vaug[:, 8 + kc, :, 0:128], cv_a[li, kc * 128:(kc + 1) * 128, :].rearrange("p (h v) -> p h v", h=4), [], [vaug])
                P.dma("pool", vcaug[:, 8 + kc, :, 0:64], cv_c[li, kc * 128:(kc + 1) * 128, :].rearrange("p (h v) -> p h v", h=2), [], [vcaug])
            for q in range(4):
                if not (q == 0 and x0_loaded):
                    src, rd = xsrc(li, q)
                    P.dma("sp", x_tm[2][:], src, rd, [x_tm[2]])
                proj_phase(2, li, q)
                P.dma("sp", QS[q].ap()[:, 0:4, :], zqk[:, 0:4, :], [zqk], [QS[q]])
                P.dma("sp", QS[q].ap()[:, 4:6, :], zqk[:, 8:10, :], [zqk], [QS[q]])
                for d in range(2):
                    P.dma("sp", HS[q].ap()[:, 2 * d], qt[d][:], [qt[d]], [HS[q]])
                    P.dma("sp", HS[q].ap()[:, 2 * d + 1], kt[d][:], [kt[d]], [HS[q]])
                    P.dma("sp", KT[q].ap()[:, d], ktok[d][:], [ktok[d]], [KT[q]])
                    P.dma("sp", EE[q].ap()[:, d], eend[d][:], [eend[d]], [EE[q]])
                P.dma("sp", IB[q].ap(), ibb[:], [ibb], [IB[q]])
                P.dma("sp", WG[q].ap(), wg[:], [wg], [WG[q]])
            dve(lambda e: e.memset(small[:, 56:57], 0.0), [], [big])
            ibS = {d: SubView(big, ("ib", d), lambda d=d: big[:, d * 2560:d * 2560 + 512].rearrange("p (l f) -> p l f", l=2)) for d in range(2)}
            vbS = {d: SubView(big, ("vb", d), lambda d=d: big[:, d * 2560 + 512:(d + 1) * 2560].rearrange("p (l c j v) -> p l c j v", l=2, c=2, j=4))
                   for d in range(2)}
            for d in range(2):
                stin = st_f if d == 0 else st_b
                dve(lambda e, d=d: e.memset(S32[d][:, :, 0, :], 0.0), [], [S32[d]])
                for hd in range(4):
                    c, hh = hd // 2, hd % 2
                    P.dma("sp", S32[d][hh * 64:(hh + 1) * 64, c, 0, hh * 64:(hh + 1) * 64], stin[li, hd, :, :], [], [S32[d]])
            for qi in range(4):
                for d in range(2):
                    q = qi if d == 0 else 3 - qi
                    P.dma("sp", ktok[d][:], KT[q].ap()[:, d], [KT[q]], [ktok[d]])
                    P.dma("sp", eend[d][:], EE[q].ap()[:, d], [EE[q]], [eend[d]])
                    P.dma("sp", ibS[d][:], IB[q].ap(), [IB[q]], [ibS[d]])
                hgrn_scan(2, li, dirs=(0, 1), ibs=ibS, vbs=vbS)
                for d in range(2):
                    q = qi if d == 0 else 3 - qi
                    P.dma("sp", SS[q].ap()[:, d], Sbf[d][:], [Sbf[d]], [SS[q]])
            def load_x(q):
                src, rd = xsrc(li, q)
                P.dma("sp", x_tm[2][:], src, rd, [x_tm[2]])

            def load_ops(q, eng):
                P.dma(eng, zqk[:, 0:4, :], QS[q].ap()[:, 0:4, :], [QS[q]], [zqk])
                P.dma(eng, zqk[:, 8:10, :], QS[q].ap()[:, 4:6, :], [QS[q]], [zqk])
                for d in range(2):
                    P.dma(eng, qt[d][:], HS[q].ap()[:, 2 * d], [HS[q]], [qt[d]])
                    P.dma(eng, kt[d][:], HS[q].ap()[:, 2 * d + 1], [HS[q]], [kt[d]])
                    P.dma(eng, Sbf[d][:], SS[q].ap()[:, d], [SS[q]], [Sbf[d]])
                P.dma(eng, ibb[:], IB[q].ap(), [IB[q]], [ibb])
                P.dma(eng, wg[:], WG[q].ap(), [WG[q]], [wg])

            def load_own():
                dve(lambda e: e.memset(small[:, 62:63], 0.0), [], [big])
                tmp32 = mixed[:].rearrange("p t d -> p (t d)")
                off = {True: 0, False: 0}

                def sel(dst, dst_res, n, srcs, f32):
                    cap = 2048 if f32 else 9216
                    base = tmp32 if f32 else big
                    pres = mixed if f32 else big
                    tms = []
                    for q in range(1, 4):
                        if off[f32] + n > cap:
                            off[f32] = 0
                            dve(lambda e: e.memset(small[:, 57:58], 0.0), [], [pres])
                        o = off[f32]
                        off[f32] += n
                        tms.append((base[:, o:o + n], (pres, ("sel", o, n))))
                    sap, sres = srcs(0)
                    P.dma("sp", dst, sap, [sres], [dst_res])
                    for q in range(1, 4):
                        sap, sres = srcs(q)
                        P.dma("sp", tms[q - 1][0], sap, [sres], [tms[q - 1][1]])
                    dve(lambda e, dst=dst: e.tensor_scalar_mul(out=dst, in0=dst, scalar1=ohT[:, 0:1]), [dst_res, ohT], [dst_res])
                    for q in range(1, 4):
                        tm = tms[q - 1][0]
                        dve(lambda e, dst=dst, tm=tm, q=q: e.scalar_tensor_tensor(out=dst, in0=tm, scalar=ohT[:, q:q + 1], in1=dst,
                                                                                  op0=ALU.mult, op1=ALU.add), [tms[q - 1][1], dst_res, ohT], [dst_res])
                xv = x_tm[2][:].rearrange("p t d -> p (t d)")
                for blk in range(4):
                    sel(xv[:, blk * 512:(blk + 1) * 512], x_tm[2], 512,
                        lambda q, blk=blk: (XS[q].ap().rearrange("p t d -> p (t d)")[:, blk * 512:(blk + 1) * 512], XS[q]), True)
                sel(zqk[:, 0:4, :].rearrange("p a b -> p (a b)"), zqk, 1024, lambda q: (QS[q].ap()[:, 0:4, :].rearrange("p a b -> p (a b)"), QS[q]), False)
                sel(zqk[:, 8:10, :].rearrange("p a b -> p (a b)"), zqk, 512, lambda q: (QS[q].ap()[:, 4:6, :].rearrange("p a b -> p (a b)"), QS[q]), False)
                for d in range(2):
                    sel(qt[d][:].rearrange("p a b -> p (a b)"), qt[d], 512, lambda q, d=d: (HS[q].ap()[:, 2 * d].rearrange("p a b -> p (a b)"), HS[q]), False)
                    sel(kt[d][:].rearrange("p a b -> p (a b)"), kt[d], 512, lambda q, d=d: (HS[q].ap()[:, 2 * d + 1].rearrange("p a b -> p (a b)"), HS[q]), False)
                    sel(Sbf[d][:].rearrange("p a b c -> p (a b c)"), Sbf[d], 2304, lambda q, d=d: (SS[q].ap()[:, d].rearrange("p a b c -> p (a b c)"), SS[q]), False)
                sel(ibb[:].rearrange("p a b -> p (a b)"), ibb, 512, lambda q: (IB[q].ap().rearrange("p a b -> p (a b)"), IB[q]), False)
                sel(wg[:].rearrange("p a b -> p (a b)"), wg, 512, lambda q: (WG[q].ap().rearrange("p a b -> p (a b)"), WG[q]), True)

            def mix_tail(prefetch=None):
                dve(lambda e: e.memset(small[:, 61:62], 0.0), [], [big, mixed])
                hgrn_output(2, li, copy_state=False)
                attention(2, li, 12,
                          lambda h, m, kc: kseq[m * 64:(m + 1) * 64, h, kc * 128:(kc + 1) * 128],
                          lambda pb, kc: kseq[pb:pb + 64, 4, kc * 128:(kc + 1) * 128],
                          lambda pb, kc: kseq[pb:pb + 64, 5, kc * 128:(kc + 1) * 128], kseq)
                if prefetch is not None:
                    load_ops(prefetch, "pool")
                tail_phase(2, li)

            if li == 0:
                load_ops(0, "sp")
                for q in range(4):
                    load_x(q)
                    mix_tail(prefetch=q + 1 if q < 3 else None)
                    P.dma("sp", XS[q].ap(), x_tm[2][:], [x_tm[2]], [XS[q]])
            else:
                load_own()
                mix_tail()
                P.dma("sp", ys.rearrange("(t p) d -> p t d", p=128), x_tm[2][:], [x_tm[2]], [])

        units = [0, 1]
        nlayers = 2
        if stop is not None:
            units = units[:stop.get("units", len(units))]
            nlayers = stop.get("layers", 2)
        xbuf = [x_t, x_tB]
        xsel = [0]

        def use_x(i):
            xsel[0] = i
            x_tm[0] = x_tm[1] = x_tm[2] = xbuf[i]
            return xbuf[i]

        def load_prompt_x(li, u, xt):
            if li == 0:
                P.dma("sp", xt[:], xp[u * 256:(u + 1) * 256, :].rearrange("(t p) d -> p t d", p=128), [], [xt])
            else:
                P.dma("sp", xt[:], XP[u].ap(), [XP[u]], [xt])

        for li in range(nlayers):
            mod_part(li, 0)
            cur = use_x(xsel[0])
            load_prompt_x(li, units[0], cur) if units else None
            for ui, u in enumerate(units):
                cur = xbuf[xsel[0]]
                proj_phase(u, li)
                if u == 0:
                    mod_part(li, 1)
                prompt_mixers(u, li)
                nxt = xbuf[1 - xsel[0]]
                if ui + 1 < len(units):
                    load_prompt_x(li, units[ui + 1], nxt)
                elif with_sample:
                    src, rd = xsrc(li, 0)
                    P.dma("sp", nxt[:], src, rd, [nxt])
                tail_phase(u, li)
                if li == 0:
                    P.dma("sp", XP[u].ap(), cur[:], [cur], [XP[u]])
                else:
                    P.dma("sp", yp[u * 256:(u + 1) * 256, :].rearrange("(t p) d -> p t d", p=128), cur[:], [cur], [])
                use_x(1 - xsel[0])
            if with_sample and (stop is None or stop.get("sample", True)):
                sample_layer(li, x0_loaded=bool(units))
        P.emit()
    return nc


def kernel(**inp):
    f32 = lambda a: np.ascontiguousarray(np.asarray(a, dtype=np.float32))
    x_prompt = f32(inp["x_prompt"]); x_sample = f32(inp["x_sample"])
    nc = build_nc(with_sample=WITH_SAMPLE)
    shared = {
        "w_ada": f32(inp["w_ada"]), "b_ada": f32(inp["b_ada"]), "w_in": f32(inp["w_in"]), "w_out": f32(inp["w_out"]),
        "w_ff1": f32(inp["w_ff1"]), "w_ff2": f32(inp["w_ff2"]),
        "lamv": f32(np.stack([inp["lam_q1"], inp["lam_k1"], inp["lam_q2"], inp["lam_k2"]], axis=1)),
        "subln_g": f32(inp["subln_g"]),
        "lbl": f32(np.stack([inp["lb_logits_fwd"], inp["lb_logits_bwd"]], axis=0)),
        "gnorm_g": f32(inp["gnorm_g"]), "qnorm_g": f32(inp["qnorm_g"]), "knorm_g": f32(inp["knorm_g"]),
        "lnp": f32(np.stack([inp["ln1_g"], inp["ln1_b"], inp["ln2_g"], inp["ln2_b"]], axis=1)),
    }
    in_maps = []
    for i in range(8):
        sq_, qd = i // 4, i % 4
        cm = np.stack([np.asarray(inp["c_ctx"], np.float32), np.asarray(inp["c"], np.float32)[sq_]], axis=0)
        cmodT = f32(cm.reshape(2, 8, 128).transpose(2, 1, 0).reshape(128, 16))
        oh = np.zeros((128, 8), np.float32)
        oh[:, qd] = 1.0
        m = dict(shared)
        m.update({
            "xp": f32(x_prompt[2 * i:2 * i + 2].reshape(512, D)),
            "xs": f32(x_sample[sq_]),
            "cmodT": cmodT, "consts": make_consts(0),
            "ck_a": f32(np.asarray(inp["cache_a_k"])[sq_].reshape(2, 512, 512)),
            "cv_a": f32(np.asarray(inp["cache_a_v"])[sq_].reshape(2, 512, 512)),
            "ck_c": f32(np.asarray(inp["cache_c_k"])[sq_].reshape(2, 512, 128)),
            "cv_c": f32(np.asarray(inp["cache_c_v"])[sq_].reshape(2, 512, 128)),
            "st_f": f32(np.asarray(inp["state_b_fwd"])[sq_]), "st_b": f32(np.asarray(inp["state_b_bwd"])[sq_]),
            "onehot": oh,
        })
        in_maps.append(m)
    if DEBUG_HOOK is not None:
        return DEBUG_HOOK(in_maps)
    res = run_bass_kernel_spmd(nc, in_maps, core_ids=list(range(8)))
    R = res.results
    y_prompt = np.concatenate([r["yp"].reshape(2, 256, D) for r in R], axis=0)
    y_sample = np.stack([np.concatenate([R[s * 4 + q]["ys"] for q in range(4)], axis=0) for s in range(2)], axis=0)
    cat = lambda k, shp: np.concatenate([r[k] for r in R], axis=0).reshape(shp)
    return (y_prompt.astype(np.float32), y_sample.astype(np.float32),
            cat("o_ak", (16, 2, 256, 4, 2, 64)), cat("o_av", (16, 2, 256, 4, 128)),
            cat("o_ck", (16, 2, 256, 2, 64)), cat("o_cv", (16, 2, 256, 2, 64)),
            cat("o_sf", (16, 2, 4, 64, 64)), cat("o_sb", (16, 2, 4, 64, 64)))


WITH_SAMPLE = True
DEBUG_HOOK = None
```

```python
import numpy as np
import concourse.bass as bass
import concourse.mybir as mybir
from concourse.bass_utils import run_bass_kernel_spmd
from contextlib import ExitStack

F32 = mybir.dt.float32
BF16 = mybir.dt.bfloat16
AF = mybir.ActivationFunctionType
ALU = mybir.AluOpType
AX = mybir.AxisListType

ENGS = ("pe", "act", "dve", "pool", "sp")
NDMA_SLOTS = {"sp": 12, "pool": 12, "act": 4}
STORES_ON_POOL = True
SAME_ENG_SYNC = {"pe": False, "act": True, "dve": True, "pool": True, "sp": True}


import os
MAXOPS = int(os.environ.get('MAXOPS', '100000000'))


class Res:
    _n = 0

    def __init__(self, name):
        Res._n += 1
        self.id = Res._n
        self.name = name


class T:
    def __init__(self, h, name):
        self.h = h
        self._res = Res(name)

    def __getitem__(self, k):
        return self.h[k]

    def ap(self):
        return self.h.ap()


class Op:
    __slots__ = ("eng", "fn", "reads", "writes", "dma", "idx", "waits", "need_inc",
                 "cnt", "slot", "slot_val", "pre_wait")

    def __init__(self, eng, fn, reads, writes, dma):
        self.eng, self.fn, self.reads, self.writes, self.dma = eng, fn, reads, writes, dma
        self.waits = []
        self.need_inc = False
        self.cnt = None
        self.slot = None
        self.slot_val = None
        self.pre_wait = None


class Prog:
    def __init__(self, nc, stack):
        self.nc = nc
        self.stack = stack
        self.ops = []
        self.state = {}
        self.subs = {}
        self.ndma = {e: 0 for e in ENGS}
        self.psum_banks = []
        self.psum_i = 0
        self.pinned = []

    def sb(self, name, shape, dt=F32):
        t = self.stack.enter_context(self.nc.sbuf_tensor(name, list(shape), dt))
        return T(t, name)

    def dram(self, name, shape, dt=F32, kind="Internal"):
        t = self.nc.dram_tensor(name, list(shape), dt, kind=kind)
        return T(t, name)

    def init_psum(self, n=8):
        for i in range(n):
            t = self.stack.enter_context(self.nc.psum_tensor(f"psb{i}", [128, 512], F32))
            self.psum_banks.append(T(t, f"psb{i}"))

    def psum(self, pin=False):
        assert len(self.pinned) < len(self.psum_banks), "all PSUM banks pinned"
        while True:
            t = self.psum_banks[self.psum_i % len(self.psum_banks)]
            self.psum_i += 1
            if t not in self.pinned:
                break
        if pin:
            self.pinned.append(t)
        return t

    def unpin(self, t):
        self.pinned.remove(t)

    @staticmethod
    def _key(r):
        if isinstance(r, tuple):
            return (r[0]._res.id, r[1])
        return (r._res.id, getattr(r, "_sub", None))

    def _conflicts(self, key):
        rid, sub = key
        subs = self.subs.setdefault(rid, set())
        if sub is None:
            return [(rid, s) for s in subs | {None}]
        return [(rid, sub), (rid, None)]

    def op(self, eng, fn, reads=(), writes=(), dma=False):
        if len(self.ops) >= MAXOPS:
            return None
        o = Op(eng, fn, [self._key(r) for r in reads], [self._key(r) for r in writes], dma)
        o.idx = len(self.ops)
        deps = set()
        for k in o.reads:
            for ck in self._conflicts(k):
                st = self.state.get(ck)
                if st and st[0] is not None:
                    deps.add(st[0])
        for k in o.writes:
            for ck in self._conflicts(k):
                st = self.state.get(ck)
                if st:
                    if st[0] is not None:
                        deps.add(st[0])
                    deps.update(st[1])
        deps.discard(o.idx)
        o.waits = sorted(deps)
        for k in o.reads:
            self.subs.setdefault(k[0], set()).add(k[1])
            st = self.state.setdefault(k, [None, []])
            st[1].append(o.idx)
        for k in o.writes:
            self.subs.setdefault(k[0], set()).add(k[1])
            if k[1] is None:
                for s in list(self.subs[k[0]]):
                    self.state[(k[0], s)] = [o.idx, []]
            else:
                self.state[k] = [o.idx, []]
        if dma:
            j = self.ndma[eng]
            self.ndma[eng] += 1
            K = NDMA_SLOTS[eng]
            o.slot = j % K
            o.slot_val = 16 * (j // K + 1)
            if j >= K:
                o.pre_wait = (o.slot, 16 * (j // K))
        self.ops.append(o)
        return o

    def mm(self, out, lhsT, rhs, start, stop, reads, writes, **kw):
        return self.op("pe", lambda e: e.matmul(out, lhsT, rhs, start=start, stop=stop, **kw),
                       reads, writes)

    def tr(self, out, in_, ident, reads, writes):
        return self.op("pe", lambda e: e.transpose(out, in_, ident), reads, writes)

    def dma(self, eng, out, in_, reads, writes, **kw):
        if eng == "sp" and STORES_ON_POOL and str(getattr(out, "space", "")).endswith("DRAM") and not kw.get("keep_queue"):
            eng = "pool"
        kw.pop("keep_queue", None)
        return self.op(eng, lambda e: e.dma_start(out=out, in_=in_, **kw), reads, writes, dma=True)

    def emit(self):
        nc = self.nc
        ops = self.ops
        for o in ops:
            for d in o.waits:
                D = ops[d]
                if D.dma:
                    continue
                if D.eng == o.eng and not o.dma and not D.dma and not SAME_ENG_SYNC[o.eng]:
                    continue
                D.need_inc = True
        cnt = {e: 0 for e in ENGS}
        for o in ops:
            if not o.dma and o.need_inc:
                cnt[o.eng] += 1
                o.cnt = cnt[o.eng]
        sems = {e: self.stack.enter_context(nc.semaphore(f"s_{e}")) for e in ENGS}
        dsems = {e: [self.stack.enter_context(nc.semaphore(f"d_{e}{i}")) for i in range(n)]
                 for e, n in NDMA_SLOTS.items()}
        block = self.stack.enter_context(nc.Block())
        last_out_waits = []

        def run(engname, e):
            seen_eng = {x: 0 for x in ENGS}
            seen_dma = {}
            for o in ops:
                if o.eng != engname:
                    continue
                if o.dma and o.pre_wait is not None:
                    s, v = o.pre_wait
                    key = (engname, s)
                    if seen_dma.get(key, 0) < v:
                        e.wait_ge(dsems[engname][s], v)
                        seen_dma[key] = v
                for d in o.waits:
                    D = ops[d]
                    if D.dma:
                        key = (D.eng, D.slot)
                        if seen_dma.get(key, 0) < D.slot_val:
                            e.wait_ge(dsems[D.eng][D.slot], D.slot_val)
                            seen_dma[key] = D.slot_val
                    else:
                        if D.eng == engname and not o.dma and not SAME_ENG_SYNC[engname]:
                            continue
                        if seen_eng[D.eng] < D.cnt:
                            e.wait_ge(sems[D.eng], D.cnt)
                            seen_eng[D.eng] = D.cnt
                ins = o.fn(e)
                if o.dma:
                    ins.then_inc(dsems[engname][o.slot], 16)
                elif o.need_inc:
                    ins.then_inc(sems[engname], 1)
            for s in range(NDMA_SLOTS.get(engname, 0)):
                lastv = 0
                for o in ops:
                    if o.dma and o.eng == engname and o.slot == s:
                        lastv = o.slot_val
                if lastv and seen_dma.get((engname, s), 0) < lastv:
                    e.wait_ge(dsems[engname][s], lastv)

        @block.tensor
        def _(e):
            run("pe", e)

        @block.scalar
        def _(e):
            run("act", e)

        @block.vector
        def _(e):
            run("dve", e)

        @block.gpsimd
        def _(e):
            run("pool", e)

        @block.sync
        def _(e):
            run("sp", e)


D = 1024
NIN = 3328
ALPHA = 4 ** 0.25
LN_EPS = 1e-6
RMS_EPS = 1e-6
F_MIN = 1e-6
CH = 32
NCH = 256 // CH
C_ID, C_BONES, C_MF, C_MB, C_BLK2 = 0, 128, 256, 384, 512
C_ROW, C_COL, C_RESET, C_PSW, C_RC, C_RS, C_SEL2, C_SELBC = 640, 644, 1156, 1412, 1540, 2564, 3588, 3590
NCONST = 3590 + 256


def make_consts(tok0):
    c = np.zeros((128, NCONST), np.float32)
    p = np.arange(128)
    c[:, C_ID:C_ID + 128] = np.eye(128)
    c[:, C_BONES:C_BONES + 128] = (p[:, None] // 64 == p[None, :] // 64) / 64.0
    same = (p[:, None] // CH == p[None, :] // CH)
    c[:, C_MF:C_MF + 128] = same & (p[:, None] <= p[None, :])
    c[:, C_MB:C_MB + 128] = same & (p[:, None] >= p[None, :])
    c[:, C_BLK2:C_BLK2 + 128] = (p[:, None] // 64 == p[None, :] // 64)
    c[:, C_ROW:C_ROW + 4] = (p[:, None] // CH == np.arange(4)[None, :])
    c[:, C_COL:C_COL + 512] = np.broadcast_to((np.arange(4)[:, None] == (p[None, :] // CH)).reshape(1, 512), (128, 512))
    t = np.arange(256)
    c[:, C_RESET:C_RESET + 256] = np.broadcast_to((t % CH != 0)[None, :], (128, 256))
    c[:, C_PSW:C_PSW + 128] = (p[:, None] == (p[None, :] ^ 1))
    pos = np.arange(1024)
    row = (pos // 64).astype(np.float32)
    col = (pos % 64).astype(np.float32)
    inv = (10000.0 ** (-np.arange(16, dtype=np.float32) / 16)).astype(np.float32)
    ang = np.concatenate([row[:, None] * inv, col[:, None] * inv], axis=-1).astype(np.float32)
    f = p % 64
    cosT = np.cos(ang)[:, f // 2].T
    sinT = np.sin(ang)[:, f // 2].T
    sgn = np.where(f % 2 == 0, -1.0, 1.0)[:, None]
    c[:, C_RC:C_RC + 1024] = cosT
    c[:, C_RS:C_RS + 1024] = sinT * sgn
    c[0, C_SEL2] = 1.0
    c[1, C_SEL2 + 1] = 1.0
    c[0, C_SELBC:C_SELBC + 128] = 1.0
    c[1, C_SELBC + 128:C_SELBC + 256] = 1.0
    return c


def build_nc(with_sample=True, stop=None):
    nc = bass.Bass("TRN2", target_bir_lowering=False)
    din = lambda n, s: nc.dram_tensor(n, list(s), F32, kind="ExternalInput").ap()
    dout = lambda n, s: nc.dram_tensor(n, list(s), F32, kind="ExternalOutput").ap()
    xp = din("xp", [512, D]); xs = din("xs", [1024, D])
    cmodT = din("cmodT", [128, 16]); consts = din("consts", [128, NCONST])
    w_ada = din("w_ada", [2, D, 6 * D]); b_ada = din("b_ada", [2, 6 * D])
    w_in = din("w_in", [2, D, NIN]); w_out = din("w_out", [2, D, D])
    w_ff1 = din("w_ff1", [2, D, 4 * D]); w_ff2 = din("w_ff2", [2, 4 * D, D])
    lamv = din("lamv", [2, 4, 64]); subln_g = din("subln_g", [2, 128])
    lbl = din("lbl", [2, 2, 256])
    gnorm_g = din("gnorm_g", [2, 64]); qnorm_g = din("qnorm_g", [2, 64]); knorm_g = din("knorm_g", [2, 64])
    lnp = din("lnp", [2, 4, D])
    ck_a = din("ck_a", [2, 512, 512]); cv_a = din("cv_a", [2, 512, 512])
    ck_c = din("ck_c", [2, 512, 128]); cv_c = din("cv_c", [2, 512, 128])
    st_f = din("st_f", [2, 4, 64, 64]); st_b = din("st_b", [2, 4, 64, 64])
    onehot = din("onehot", [128, 8])
    yp = dout("yp", [512, D]); ys = dout("ys", [256, D])
    o_ak = dout("o_ak", [2, 2, 256, 512]); o_av = dout("o_av", [2, 2, 256, 512])
    o_ck = dout("o_ck", [2, 2, 256, 128]); o_cv = dout("o_cv", [2, 2, 256, 128])
    o_sf = dout("o_sf", [2, 2, 4, 64, 64]); o_sb = dout("o_sb", [2, 2, 4, 64, 64])

    with ExitStack() as stk:
        P = Prog(nc, stk)
        P.init_psum(8)
        rr = [0]

        def evac_eng():
            rr[0] += 1
            return "act" if rr[0] % 2 else "dve"

        cst = P.sb("cst", [128, NCONST])
        idf = cst[:, C_ID:C_ID + 128]
        x_t = P.sb("x_t", [128, 2, D])
        x_tm = [x_t, x_t, x_t]
        XP = [P.dram(f"XP{u}", [128, 2, D]) for u in range(2)]
        hT = P.sb("hT", [128, 8, 256], BF16)
        wr = [P.sb(f"wr{i}", [128, 4096], BF16) for i in range(3)]
        wctr = [0]
        x_tB = P.sb("x_tB", [128, 2, D])
        scT = P.sb("scT", [128, 16]); scTb = P.sb("scTb", [128, 16], BF16)
        mc = P.sb("mc", [128, 4, 8, 2])
        gbc = P.sb("gbc", [128, 2, 2, D], BF16)
        lnbc = P.sb("lnbc", [128, 4, D])
        pswb = P.sb("pswb", [128, 128], BF16)
        zqk = P.sb("zqk", [128, 12, 256], BF16)
        kseq = P.sb("kseq", [128, 6, 1536], BF16)
        ropeb = P.sb("ropeb", [128, 256], BF16)
        XS = [P.dram(f"XS{q}", [128, 2, D]) for q in range(4)]
        QS = [P.dram(f"QS{q}", [128, 6, 256], BF16) for q in range(4)]
        HS = [P.dram(f"HS{q}", [128, 4, 2, 256], BF16) for q in range(4)]
        KT = [P.dram(f"KT{q}", [128, 2, 2, 256], BF16) for q in range(4)]
        EE = [P.dram(f"EE{q}", [128, 2, 2, NCH]) for q in range(4)]
        IB = [P.dram(f"IB{q}", [128, 2, 256], BF16) for q in range(4)]
        WG = [P.dram(f"WG{q}", [128, 2, 256]) for q in range(4)]
        SS = [P.dram(f"SS{q}", [128, 2, 2, NCH + 1, 128], BF16) for q in range(4)]
        ftmp = [P.sb(f"ftmp{i}", [128, 256]) for i in range(4)]
        fctr = [0]
        ttmp = [P.sb(f"ttmp{i}", [128, 512]) for i in range(2)]
        tctr = [0]
        vaug = P.sb("vaug", [128, 12, 4, 129], BF16)
        vcaug = P.sb("vcaug", [128, 12, 2, 65], BF16)
        ibb = P.sb("ibb", [128, 2, 256], BF16)
        wg = P.sb("wg", [128, 2, 256])
        big = P.sb("big", [128, 9216], BF16)

        class View:
            def __init__(self, ap_fn):
                self._res = big._res
                self.ap_fn = ap_fn

            def __getitem__(self, k):
                return self.ap_fn()[k]
        Et = [View(lambda i=i: big[:, i * 3072:(i + 1) * 3072].rearrange("p (k q) -> p k q", k=12)) for i in range(3)]
        hidT = View(lambda: big[:, 0:8192].rearrange("p (k q) -> p k q", k=32))
        ectr = [0]
        mixed = P.sb("mixed", [128, 2, D])
        sq = P.sb("sq", [128, 2, 256])
        qt = [P.sb(f"qt{d}", [128, 2, 256], BF16) for d in range(2)]
        kt = [P.sb(f"kt{d}", [128, 2, 256], BF16) for d in range(2)]
        kt32 = P.sb("kt32", [128, 256])
        ktok = [P.sb(f"ktok{d}", [128, 2, 256], BF16) for d in range(2)]
        eend = [P.sb(f"eend{d}", [128, 2, NCH]) for d in range(2)]
        S32 = [P.sb(f"S32{d}", [128, 2, 2, 128]) for d in range(2)]
        Sbf = [P.sb(f"Sbf{d}", [128, 2, NCH + 1, 128], BF16) for d in range(2)]
        vblk = P.sb("vblk", [128, 2, 2, 4, 128], BF16)
        qblk2 = [P.sb(f"qblk{i}", [128, 4, 128], BF16) for i in range(2)]
        Ue4 = [P.sb(f"Ue{i}", [128, 4, 128]) for i in range(4)]
        Am = [P.sb(f"Am{i}", [128, 2, 128], BF16) for i in range(2)]
        ytmp = P.sb("ytmp", [128, D])
        small = P.sb("small", [128, 64])
        smallA = P.sb("smallA", [128, 64])
        lamt = P.sb("lamt", [128, 2, 4, 64]); lamc = P.sb("lamc", [128, 2, 4])
        subg = P.sb("subg", [128, 2, 128]); gng = P.sb("gng", [128, 2, 256])
        qkg = P.sb("qkg", [128, 2, 2])
        lbt = P.sb("lbt", [128, 2, 2, 2, 2])
        lbraw = P.sb("lbraw", [128, 2, 2, 2])
        stats = P.sb("stats", [128, 16])
        epsc = P.sb("epsc", [128, 1])
        stats2 = [stats, P.sb("statsB", [128, 16])]
        ohT = P.sb("ohT", [128, 8])

        def act(out, in_, func, reads, writes, **kw):
            return P.op("act", lambda e: e.activation(out=out, in_=in_, func=func, **kw), reads, writes)

        def dve(fn, reads, writes):
            return P.op("dve", fn, reads, writes)

        def next_ftmp():
            fctr[0] += 1
            return ftmp[fctr[0] % len(ftmp)]

        def next_ttmp():
            tctr[0] += 1
            return ttmp[tctr[0] % len(ttmp)]

        WB = {}

        def load_w(view, key=None, shape3=None):
            s = wr[wctr[0] % len(wr)]
            wctr[0] += 1
            a, b = view.shape[1], view.shape[2]
            n = a * b
            tile3 = s[:, 0:n].rearrange("p (a b) -> p a b", a=a)
            h = a // 2
            sub1 = 1 if n == 4096 else 0
            if key is None or key not in WB:
                P.dma("pool", tile3[:, 0:h, :], view[:, 0:h, :], [], [(s, 0)])
                P.dma("pool", tile3[:, h:a, :], view[:, h:a, :], [], [(s, sub1)])
                if key is not None:
                    WB[key] = P.dram("WB_%s" % "_".join(str(k) for k in key), [128, n], BF16)
                    P.dma("sp", WB[key].ap(), s[:, 0:n], [(s, 0), (s, 1)], [WB[key]], keep_queue=True)
            else:
                wv = WB[key].ap().rearrange("p (a b) -> p a b", a=a)
                P.dma("sp", tile3[:, 0:h, :], wv[:, 0:h, :], [WB[key]], [(s, 0)])
                P.dma("sp", tile3[:, h:a, :], wv[:, h:a, :], [WB[key]], [(s, sub1)])
            if n < 4096:
                h = a
            return s, tile3, h

        def wsub(s, h, k):
            return (s, 0 if k < h else 1)

        P.dma("sp", cst[:], consts, [], [cst])
        P.dma("sp", scT[:], cmodT, [], [scT])
        P.dma("sp", ohT[:], onehot, [], [ohT])
        P.dma("sp", lamt[:].rearrange("p l f d -> p (l f d)"), lamv.rearrange("l f d -> (l f d)").partition_broadcast(128), [], [lamt])
        P.dma("sp", subg[:].rearrange("p l d -> p (l d)"), subln_g.rearrange("l d -> (l d)").partition_broadcast(128), [], [subg])
        for li in range(2):
            for r in range(4):
                P.dma("sp", gng[:, li, r * 64:(r + 1) * 64], gnorm_g[li].partition_broadcast(128), [], [gng])
            for hh in range(2):
                P.dma("sp", qkg[hh * 64:(hh + 1) * 64, li, 0:1], qnorm_g[li].rearrange("(d o) -> d o", o=1), [], [qkg])
                P.dma("sp", qkg[hh * 64:(hh + 1) * 64, li, 1:2], knorm_g[li].rearrange("(d o) -> d o", o=1), [], [qkg])
        for d in range(2):
            for li in range(2):
                for c in range(2):
                    P.dma("sp", lbraw[:, d, li, c:c + 1], lbl[d, li, c * 128:(c + 1) * 128].rearrange("(p o) -> p o", o=1), [], [lbraw])
        dve(lambda e: e.tensor_copy(out=pswb[:], in_=cst[:, C_PSW:C_PSW + 128]), [cst], [pswb])
        dve(lambda e: e.memset(epsc[:], 1e-6), [], [epsc])
        for li in range(2):
            for j in range(2):
                dve(lambda e, li=li, j=j: e.tensor_tensor(out=lamt[:, li, 2 * j, :], in0=lamt[:, li, 2 * j, :],
                                                          in1=lamt[:, li, 2 * j + 1, :], op=ALU.mult), [lamt], [lamt])
                dve(lambda e, li=li, j=j: e.reduce_sum(out=lamc[:, li, j:j + 1], in_=lamt[:, li, 2 * j, :], axis=AX.X),
                    [lamt], [lamc])
            act(lamc[:, li, 0:2], lamc[:, li, 0:2], AF.Exp, [lamc], [lamc])
            lam_init = 0.8 - 0.6 * float(np.exp(-0.3 * li))
            dve(lambda e, li=li, lam_init=lam_init: e.scalar_tensor_tensor(
                out=lamc[:, li, 2:3], in0=lamc[:, li, 1:2], scalar=-lam_init, in1=lamc[:, li, 0:1],
                op0=ALU.add, op1=ALU.subtract), [lamc], [lamc])
            dve(lambda e, li=li, lam_init=lam_init: e.tensor_scalar_mul(out=subg[:, li, :], in0=subg[:, li, :],
                                                                      scalar1=1.0 - lam_init), [subg], [subg])
        for d in range(2):
            dve(lambda e, d=d: e.memset(lbt[:, d, 0, :, 0:1], 0.0), [], [lbt])
            dve(lambda e, d=d: e.memset(lbt[:, d, 0, :, 1:2], 1.0), [], [lbt])
            dve(lambda e, d=d: e.tensor_sub(out=lbraw[:, d, 1, :], in0=lbraw[:, d, 1, :], in1=lbraw[:, d, 0, :]), [lbraw], [lbraw])
            act(lbt[:, d, 1, :, 0], lbraw[:, d, 1, :], AF.Sigmoid, [lbraw], [lbt])
            dve(lambda e, d=d: e.tensor_scalar(out=lbt[:, d, 1, :, 1], in0=lbt[:, d, 1, :, 0], scalar1=-1.0, scalar2=1.0,
                                               op0=ALU.mult, op1=ALU.add), [lbt], [lbt])
        act(scTb[:], scT[:], AF.Silu, [scT], [scTb])
        dve(lambda e: e.memset(vaug[:, :, :, 128:129], 1.0), [], [vaug])
        dve(lambda e: e.memset(vcaug[:, :, :, 64:65], 1.0), [], [vcaug])

        def mod_part(li, part):
            dve(lambda e: e.memset(small[:, 54:55], 0.0), [], [ytmp, ttmp[0], ttmp[1]])
            if part == 0:
                P.dma("sp", lnbc[:].rearrange("p a d -> p (a d)"), lnp[li].rearrange("a d -> (a d)").partition_broadcast(128), [], [lnbc])
            wv = w_ada[li].rearrange("(kc p) n -> p kc n", p=128)
            psc = P.psum(pin=True)
            vmap = {1: 0, 0: 1, 4: 2, 3: 3}
            groups = range(0, 4) if part == 0 else range(4, 12)
            for g in groups:
                vec, hf = g // 2, g % 2
                s, t3, h = load_w(wv[:, :, g * 512:(g + 1) * 512])
                bt = bada[g % 2]
                P.dma("sp", bt[0:1, :], b_ada[li:li + 1, g * 512:(g + 1) * 512], [], [bt])
                P.dma("sp", bt[1:2, :], b_ada[li:li + 1, g * 512:(g + 1) * 512], [], [bt])
                ps = P.psum()
                for kc in range(8):
                    P.mm(ps[0:2, :], scTb[:, 2 * kc:2 * kc + 2], t3[:, kc, :], kc == 0, kc == 7,
                         [scTb, wsub(s, h, kc)], [ps])
                mr = modrows[g % 2]
                dve(lambda e, ps=ps, mr=mr, bt=bt: e.tensor_add(out=mr[:], in0=ps[0:2, :], in1=bt[:]), [ps, bt], [mr])
                if vec in vmap:
                    vi = vmap[vec]
                    for jj in range(4):
                        j = hf * 4 + jj
                        P.mm(psc[:, (vi * 8 + j) * 2:(vi * 8 + j) * 2 + 2], mr[0:2, jj * 128:(jj + 1) * 128],
                             cst[0:2, C_SEL2:C_SEL2 + 2], True, True, [mr, cst], [psc])
                else:
                    gi = 0 if vec == 2 else 1
                    for src in range(2):
                        ps2 = P.psum()
                        P.mm(ps2[:, :], cst[0:2, C_SELBC + src * 128:C_SELBC + (src + 1) * 128], mr[0:2, :],
                             True, True, [mr, cst], [ps2])
                        act(gbc[:, gi, src, hf * 512:(hf + 1) * 512], ps2[:, :], AF.Identity, [ps2], [gbc])
            v0 = 0 if part == 0 else 2
            dve(lambda e: e.tensor_copy(out=mc[:, v0:v0 + 2].rearrange("p a b c -> p (a b c)"), in_=psc[:, v0 * 16:(v0 + 2) * 16]), [psc], [mc])
            P.unpin(psc)
            dve(lambda e: e.tensor_scalar_add(out=mc[:, v0], in0=mc[:, v0], scalar1=1.0), [mc], [mc])
            dve(lambda e: e.memset(small[:, 55:56], 0.0), [], [ytmp, ttmp[0], ttmp[1]])

        def mod_T(u, li, which):
            src = 1 if u == 2 else 0
            for lt in range(2):
                for half in range(2):
                    ps = P.psum()
                    for jj in range(4):
                        j = half * 4 + jj
                        P.tr(ps[:, jj * 128:(jj + 1) * 128], x_tm[u][:, lt, j * 128:(j + 1) * 128], idf, [x_tm[u], cst], [ps])
                    eng_bank = evac_eng()
                    for jj in range(4):
                        j = half * 4 + jj
                        a_col = mc[:, 2 * which, j, src:src + 1]
                        b_col = mc[:, 2 * which + 1, j, src:src + 1]
                        o = hT[:, j, lt * 128:(lt + 1) * 128]
                        i_ = ps[:, jj * 128:(jj + 1) * 128]
                        if eng_bank == "act":
                            act(o, i_, AF.Identity, [ps, mc], [(hT, j)], scale=a_col, bias=b_col)
                        else:
                            dve(lambda e, o=o, i_=i_, a_col=a_col, b_col=b_col: e.tensor_scalar(
                                out=o, in0=i_, scalar1=a_col, scalar2=b_col, op0=ALU.mult, op1=ALU.add), [ps, mc], [(hT, j)])

        def layernorm_to_x(u, lt, li, site):
            for hf in range(2):
                dve(lambda e, hf=hf: e.bn_stats(out=stats[:, hf * 6:(hf + 1) * 6], in_=ytmp[:, hf * 512:(hf + 1) * 512]), [ytmp], [stats])
            dve(lambda e: e.bn_aggr(out=stats[:, 12:14], in_=stats[:, 0:12]), [stats], [stats])
            act(stats[:, 14:15], stats[:, 13:14], AF.Ln, [stats], [stats], bias=epsc[:, 0:1], scale=1.0)
            act(stats[:, 15:16], stats[:, 14:15], AF.Exp, [stats], [stats], scale=-0.5)
            dve(lambda e: e.tensor_scalar(out=ytmp[:], in0=ytmp[:], scalar1=stats[:, 12:13], scalar2=stats[:, 15:16],
                                          op0=ALU.subtract, op1=ALU.mult), [ytmp, stats], [ytmp])
            dve(lambda e: e.tensor_mul(out=ytmp[:], in0=ytmp[:], in1=lnbc[:, 2 * site, :]), [ytmp, lnbc], [ytmp])
            xt = x_tm[u]
            dve(lambda e, xt=xt: e.tensor_add(out=xt[:, lt, :], in0=ytmp[:], in1=lnbc[:, 2 * site + 1, :]), [ytmp, lnbc], [xt])

        def residual_ln(u, lt, li, site, banks):
            src = 1 if u == 2 else 0
            for hf in range(2):
                ps = banks[hf]
                t = next_ttmp()
                dve(lambda e, ps=ps, t=t, hf=hf: e.tensor_mul(out=t[:], in0=ps[:, :], in1=gbc[:, site, src, hf * 512:(hf + 1) * 512]),
                    [ps, gbc], [t])
                xt = x_tm[u]
                dve(lambda e, t=t, hf=hf, xt=xt: e.scalar_tensor_tensor(out=ytmp[:, hf * 512:(hf + 1) * 512],
                                                                        in0=xt[:, lt, hf * 512:(hf + 1) * 512], scalar=ALPHA, in1=t[:],
                                                                        op0=ALU.mult, op1=ALU.add), [t, xt], [ytmp])
            layernorm_to_x(u, lt, li, site)

        def g_resln(u, lt, li, site, banks):
            src = 1 if u == 2 else 0
            xt = x_tm[u]
            st = stats2[lt]
            yv = mixed[:, lt, :]
            yres = (mixed, lt)
            for hf in range(2):
                ps = banks[hf]
                t = next_ttmp()
                dve(lambda e, ps=ps, t=t, hf=hf: e.tensor_mul(out=t[:], in0=ps[:, :], in1=gbc[:, site, src, hf * 512:(hf + 1) * 512]),
                    [ps, gbc], [t])
                P.unpin(ps)
                dve(lambda e, t=t, hf=hf: e.scalar_tensor_tensor(out=yv[:, hf * 512:(hf + 1) * 512],
                                                                 in0=xt[:, lt, hf * 512:(hf + 1) * 512], scalar=ALPHA, in1=t[:],
                                                                 op0=ALU.mult, op1=ALU.add), [t, xt], [yres])
            for hf in range(2):
                dve(lambda e, hf=hf: e.bn_stats(out=st[:, hf * 6:(hf + 1) * 6], in_=yv[:, hf * 512:(hf + 1) * 512]), [yres], [st])
            dve(lambda e: e.bn_aggr(out=st[:, 12:14], in_=st[:, 0:12]), [st], [st])
            yield
            act(st[:, 14:15], st[:, 13:14], AF.Ln, [st], [st], bias=epsc[:, 0:1], scale=1.0)
            act(st[:, 15:16], st[:, 14:15], AF.Exp, [st], [st], scale=-0.5)
            yield
            dve(lambda e: e.tensor_scalar(out=yv, in0=yv, scalar1=st[:, 12:13], scalar2=st[:, 15:16],
                                          op0=ALU.subtract, op1=ALU.mult), [yres, st], [yres])
            dve(lambda e: e.tensor_mul(out=yv, in0=yv, in1=lnbc[:, 2 * site, :]), [yres, lnbc], [yres])
            dve(lambda e: e.tensor_add(out=xt[:, lt, :], in0=yv, in1=lnbc[:, 2 * site + 1, :]), [yres, lnbc], [xt])

        def rms_feat(ps, gcol, out_ap, out_res, reads_extra=()):
            t = next_ftmp()
            act(t[:], ps[:, 0:256], AF.Square, [ps], [t])
            ps2 = P.psum()
            P.mm(ps2[:, 0:256], cst[:, C_BONES:C_BONES + 128], t[:], True, True, [cst, t], [ps2])
            t2 = next_ftmp()
            act(t2[:], ps2[:, 0:256], AF.Ln, [ps2], [t2], bias=epsc[:, 0:1], scale=1.0)
            act(t2[:], t2[:], AF.Exp, [t2], [t2], scale=-0.5)
            dve(lambda e, t2=t2: e.scalar_tensor_tensor(out=out_ap, in0=ps[:, 0:256], scalar=gcol, in1=t2[:],
                                                        op0=ALU.mult, op1=ALU.mult), [ps, t2, qkg], [out_res])

        def attention(u, li, nk, keyA, keyC, keyCs, kres):
            dve(lambda e: e.memset(small[:, 58:59], 0.0), [], [big])
            chains = []
            for h in range(4):
                Es = []
                for m in range(2):
                    Es.append(Et[ectr[0] % 3])
                    ectr[0] += 1
                for kb in range(nk // 2):
                    pss = [P.psum(), P.psum()]
                    for k2 in range(2):
                        kc = kb * 2 + k2
                        for m in range(2):
                            P.mm(pss[m][:, k2 * 256:(k2 + 1) * 256], keyA(h, m, kc), zqk[m * 64:(m + 1) * 64, h, :], True, True,
                                 [zqk, kres], [pss[m]])
                    for m in range(2):
                        E = Es[m]
                        act(E[:, kb * 2:kb * 2 + 2, :], pss[m][:, :].rearrange("p (a b) -> p a b", a=2), AF.Exp, [pss[m]], [(E, ('E', id(E)))], scale=0.125)
                for lt in range(2):
                    po = P.psum()
                    for m in range(2):
                        for kc in range(nk):
                            P.mm(po[:, m * 129:(m + 1) * 129], Es[m][:, kc, lt * 128:(lt + 1) * 128], vaug[:, kc, h, :],
                                 kc == 0, kc == nk - 1, [(Es[m], ('E', id(Es[m]))), vaug], [po])
                    k = h * 2 + lt
                    sm = smallA[:, k * 8:(k + 1) * 8]
                    smr = (smallA, k)
                    actr[0] += 1
                    t1 = atmp[actr[0] % len(atmp)]
                    pv = po[:, 0:258].rearrange("p (m c) -> p m c", c=129)
                    dve(lambda e, pv=pv, sm=sm: e.reciprocal(out=sm[:, 0:2].rearrange("p (m o) -> p m o", o=1), in_=pv[:, :, 128:129]), [po], [smr])
                    dve(lambda e, sm=sm: e.tensor_mul(out=sm[:, 2:3], in0=sm[:, 1:2], in1=lamc[:, li, 2:3]), [smr, lamc], [smr])
                    dve(lambda e, po=po, t1=t1, sm=sm: e.tensor_scalar_mul(out=t1[:, 0:128], in0=po[:, 0:128], scalar1=sm[:, 0:1]), [po, smr], [t1])
                    dve(lambda e, po=po, t1=t1, sm=sm: e.scalar_tensor_tensor(out=t1[:, 0:128], in0=po[:, 129:257], scalar=sm[:, 2:3],
                                                                              in1=t1[:, 0:128], op0=ALU.mult, op1=ALU.add), [po, smr, t1], [t1])
                    dve(lambda e, sm=sm: e.memset(sm[:, 3:4], 0.0), [], [smr])

                    def chain(t1=t1, sm=sm, smr=smr, lt=lt, h=h):
                        act(t1[:, 128:256], t1[:, 0:128], AF.Square, [t1], [t1, smr], accum_out=sm[:, 3:4])
                        act(sm[:, 4:5], sm[:, 3:4], AF.Ln, [smr], [smr], bias=epsc[:, 0:1], scale=1.0 / 128)
                        act(sm[:, 5:6], sm[:, 4:5], AF.Exp, [smr], [smr], scale=-0.5)
                        yield
                        dve(lambda e: e.scalar_tensor_tensor(out=mixed[:, lt, h * 128:(h + 1) * 128], in0=t1[:, 0:128],
                                                             scalar=sm[:, 5:6], in1=subg[:, li, :],
                                                             op0=ALU.mult, op1=ALU.mult), [t1, smr, subg], [(mixed, lt)])
                    chains.append(chain())
            pos = [P.psum(pin=True), P.psum(pin=True)]
            for hp in range(2):
                hqs = (2 * hp, 2 * hp + 1)
                Ec = {}
                for hq in hqs:
                    Ec[hq] = Et[ectr[0] % 3]
                    ectr[0] += 1
                for kb in range(nk // 2):
                    pss = {hq: P.psum() for hq in hqs}
                    for k2 in range(2):
                        kc = kb * 2 + k2
                        for hq in hqs:
                            pb = (hq % 2) * 64
                            kfn = keyC if hq in (0, 3) else keyCs
                            P.mm(pss[hq][:, k2 * 256:(k2 + 1) * 256], kfn(pb, kc), zqk[pb:pb + 64, 8 + hq // 2, :], True, True, [zqk, kres], [pss[hq]])
                    for hq in hqs:
                        E = Ec[hq]
                        act(E[:, kb * 2:kb * 2 + 2, :], pss[hq][:, :].rearrange("p (a b) -> p a b", a=2), AF.Exp, [pss[hq]], [(E, ('E', id(E)))], scale=0.125)
                for hq in hqs:
                    E = Ec[hq]
                    for lt in range(2):
                        for kc in range(nk):
                            P.mm(pos[lt][:, hq * 65:(hq + 1) * 65], E[:, kc, lt * 128:(lt + 1) * 128], vcaug[:, kc, hq // 2, :],
                                 kc == 0, kc == nk - 1, [(E, ('E', id(E))), vcaug], [pos[lt]])
            run_rr(chains)
            for lt in range(2):
                pv = pos[lt][:, 0:260].rearrange("p (h c) -> p h c", c=65)
                dve(lambda e, pv=pv: e.reciprocal(out=small[:, 8:12].rearrange("p (h o) -> p h o", o=1), in_=pv[:, :, 64:65]), [pos[lt]], [small])
                dve(lambda e, pv=pv, lt=lt: e.tensor_tensor(out=mixed[:, lt, 768:1024].rearrange("p (h c) -> p h c", c=64), in0=pv[:, :, 0:64],
                                                            in1=small[:, 8:12].rearrange("p (h o) -> p h o", o=1).to_broadcast([128, 4, 64]),
                                                            op=ALU.mult), [pos[lt], small], [(mixed, lt)])
                P.unpin(pos[lt])

        def hgrn_elementwise(li, d, c, sig):
            lbc = lbt[:, d, li, c, 0:1]; omc = lbt[:, d, li, c, 1:2]
            dve(lambda e: e.tensor_scalar(out=sig[:], in0=sig[:], scalar1=omc, scalar2=lbc, op0=ALU.mult, op1=ALU.add), [sig, lbt], [sig])
            dve(lambda e: e.tensor_scalar_max(out=sig[:], in0=sig[:], scalar1=F_MIN), [sig], [sig])
            g = next_ftmp()
            act(g[:], sig[:], AF.Ln, [sig], [g])
            b = next_ftmp()
            dve(lambda e: e.tensor_tensor_scan(out=b[:], data0=cst[:, C_RESET:C_RESET + 256], data1=g[:], initial=0.0,
                                               op0=ALU.mult, op1=ALU.add), [cst, g], [b])
            if d == 1:
                dve(lambda e: e.tensor_sub(out=kt32[:], in0=g[:], in1=b[:]), [g, b], [kt32])
                dve(lambda e: e.tensor_tensor(out=g[:].rearrange("p (j t) -> p j t", t=CH), in0=kt32[:].rearrange("p (j t) -> p j t", t=CH),
                                              in1=b[:].rearrange("p (j t) -> p j t", t=CH)[:, :, CH - 1:CH].to_broadcast([128, NCH, CH]),
                                              op=ALU.add), [kt32, b], [g])
                bb, eb = g, b
            else:
                bb, eb = b, g
            dve(lambda e: e.tensor_scalar(out=sig[:], in0=sig[:], scalar1=-1.0, scalar2=1.0, op0=ALU.mult, op1=ALU.add), [sig], [sig])
            act(eb[:], bb[:], AF.Exp, [bb], [eb])
            act(bb[:], bb[:], AF.Exp, [bb], [bb], scale=-1.0)
            pos_e = CH - 1 if d == 0 else 0
            dve(lambda e: e.tensor_copy(out=eend[d][:, c, :], in_=eb[:].rearrange("p (j t) -> p j t", t=CH)[:, :, pos_e]), [eb], [eend[d]])
            dve(lambda e: e.tensor_mul(out=qt[d][:, c, :], in0=sq[:, c, :], in1=eb[:]), [sq, eb], [(qt[d], c)])
            dve(lambda e: e.tensor_mul(out=kt32[:], in0=sig[:], in1=bb[:]), [sig, bb], [kt32])
            dve(lambda e: e.tensor_copy(out=kt[d][:, c, :], in_=kt32[:]), [kt32], [(kt[d], c)])
            ps = P.psum()
            for lt in range(2):
                P.tr(ps[:, lt * 128:(lt + 1) * 128], kt32[:, lt * 128:(lt + 1) * 128], idf, [kt32, cst], [ps])
            act(ktok[d][:, :, c * 128:(c + 1) * 128], ps[:, 0:256].rearrange("p (l k) -> p l k", l=2), AF.Identity, [ps], [(ktok[d], c)])

        def hgrn_scan(u, li, dirs=(0, 1), ibs=None, vbs=None):
            if ibs is None:
                pairs = [(ibb, vblk)]
                vbs = {d: vblk for d in dirs}
            else:
                pairs = [(ibs[d], vbs[d]) for d in dirs]
            for ibx, vbx in pairs:
                for c in range(2):
                    for lt in range(2):
                        dve(lambda e, c=c, lt=lt, ibx=ibx, vbx=vbx: e.tensor_tensor(
                            out=vbx[:, lt, c, :, :], in0=ibx[:, lt, c * 128:(c + 1) * 128].unsqueeze(1).to_broadcast([128, 4, 128]),
                            in1=cst[:, C_ROW:C_ROW + 4].unsqueeze(2).to_broadcast([128, 4, 128]), op=ALU.mult), [ibx, cst], [vbx])

            def chain(d, c, UeT):
                vblk = vbs[d]
                for step in range(2):
                    lt = step if d == 0 else 1 - step
                    ps = P.psum()
                    P.mm(ps[:, :], ktok[d][:, lt, c * 128:(c + 1) * 128], vblk[:, lt, c, :, :].rearrange("p j v -> p (j v)"), True, True,
                         [(ktok[d], c), vblk], [ps])
                    dve(lambda e, ps=ps: e.tensor_tensor(out=UeT[:], in0=ps[:, :].rearrange("p (j v) -> p j v", j=4),
                                                         in1=cst[:, C_BLK2:C_BLK2 + 128].unsqueeze(1).to_broadcast([128, 4, 128]),
                                                         op=ALU.mult), [ps, cst], [UeT])
                    yield
                    dve(lambda e, lt=lt: e.tensor_tensor(out=UeT[:], in0=UeT[:],
                                                         in1=eend[d][:, c, lt * 4:(lt + 1) * 4].unsqueeze(2).to_broadcast([128, 4, 128]),
                                                         op=ALU.mult), [UeT, eend[d]], [UeT])
                    yield
                    for s4 in range(4):
                        jl = s4 if d == 0 else 3 - s4
                        j = lt * 4 + jl
                        slot = step * 4 + s4
                        pp = slot % 2
                        if slot == 0:
                            act(Sbf[d][:, c, 0, :], S32[d][:, c, 0, :], AF.Identity, [(S32[d], (c, 0))], [(Sbf[d], c)])
                        dve(lambda e, j=j, jl=jl, pp=pp: e.scalar_tensor_tensor(
                            out=S32[d][:, c, 1 - pp, :], in0=S32[d][:, c, pp, :], scalar=eend[d][:, c, j:j + 1], in1=UeT[:, jl, :],
                            op0=ALU.mult, op1=ALU.add), [UeT, eend[d], (S32[d], (c, pp))], [(S32[d], (c, 1 - pp))])
                        act(Sbf[d][:, c, slot + 1, :], S32[d][:, c, 1 - pp, :], AF.Identity, [(S32[d], (c, 1 - pp))], [(Sbf[d], c)])
                        yield

            gens = [chain(d, c, Ue4[(d * 2 + c) % 4]) for d in dirs for c in range(2)]
            while gens:
                for g in list(gens):
                    try:
                        next(g)
                    except StopIteration:
                        gens.remove(g)

        def hgrn_output(u, li, copy_state=True):
            combos = [(lt, c) for lt in range(2) for c in range(2)]

            def qb(i):
                return SubView(big, ("qb", i), lambda i=i: big[:, i * 512:(i + 1) * 512].rearrange("p (j t) -> p j t", j=4))

            def am(i):
                return SubView(big, ("am", i), lambda i=i: big[:, 4096 + i * 128:4096 + (i + 1) * 128])

            for ci, (lt, c) in enumerate(combos):
                for d in range(2):
                    Q = qb(ci * 2 + d)
                    dve(lambda e, d=d, c=c, lt=lt, Q=Q: e.tensor_tensor(
                        out=Q[:], in0=qt[d][:, c, lt * 128:(lt + 1) * 128].unsqueeze(1).to_broadcast([128, 4, 128]),
                        in1=cst[:, C_COL:C_COL + 512].rearrange("p (j t) -> p j t", j=4), op=ALU.mult), [(qt[d], c), cst], [Q])
            for ci, (lt, c) in enumerate(combos):
                for d in range(2):
                    moff = C_MF if d == 0 else C_MB
                    for hh in range(2):
                        pa = P.psum()
                        P.mm(pa[:, 0:128], kt[d][hh * 64:(hh + 1) * 64, c, lt * 128:(lt + 1) * 128],
                             qt[d][hh * 64:(hh + 1) * 64, c, lt * 128:(lt + 1) * 128], True, True, [(kt[d], c), (qt[d], c)], [pa])
                        A = am((ci * 2 + d) * 2 + hh)
                        dve(lambda e, pa=pa, A=A, moff=moff: e.tensor_tensor(
                            out=A[:], in0=pa[:, 0:128], in1=cst[:, moff:moff + 128], op=ALU.mult), [pa, cst], [A])
            chains = []
            for ci, (lt, c) in enumerate(combos):
                po = P.psum(pin=True)
                first = True
                for d in range(2):
                    Q = qb(ci * 2 + d)
                    for jl in range(4):
                        j = lt * 4 + jl
                        slot = j if d == 0 else (NCH - 1 - j)
                        P.mm(po[:, 0:128], Q[:, jl, :], Sbf[d][:, c, slot, :], first, False, [Q, (Sbf[d], c)], [po])
                        first = False
                for d in range(2):
                    for hh in range(2):
                        A = am((ci * 2 + d) * 2 + hh)
                        last = (d == 1 and hh == 1)
                        P.mm(po[:, hh * 64:(hh + 1) * 64], A[:], ibb[:, lt, (c * 2 + hh) * 64:(c * 2 + hh + 1) * 64], False, last,
                             [A, ibb], [po])

                def norm_chain(po=po, lt=lt, c=c, ci=ci):
                    actr[0] += 1
                    t = atmp[actr[0] % len(atmp)]
                    sm = smallA[:, ci * 8:(ci + 1) * 8]
                    smr = (smallA, ci)
                    act(t[:, 0:128], po[:, 0:128], AF.Square, [po], [t])
                    yield
                    dve(lambda e: e.reduce_sum(out=sm[:, 0:2], in_=t[:, 0:128].rearrange("p (h v) -> p h v", h=2), axis=AX.X), [t], [smr])
                    yield
                    act(sm[:, 2:4], sm[:, 0:2], AF.Ln, [smr], [smr], bias=epsc[:, 0:1], scale=1.0 / 64)
                    act(sm[:, 4:6], sm[:, 2:4], AF.Exp, [smr], [smr], scale=-0.5)
                    yield
                    dve(lambda e: e.tensor_tensor(out=t[:, 128:256].rearrange("p (h v) -> p h v", h=2),
                                                  in0=po[:, 0:128].rearrange("p (h v) -> p h v", h=2),
                                                  in1=sm[:, 4:6].unsqueeze(2).to_broadcast([128, 2, 64]), op=ALU.mult),
                        [po, smr], [t])
                    P.unpin(po)
                    dve(lambda e: e.tensor_mul(out=mixed[:, lt, 512 + c * 128:512 + (c + 1) * 128], in0=t[:, 128:256],
                                               in1=wg[:, lt, c * 128:(c + 1) * 128]), [t, wg], [(mixed, lt)])
                chains.append(norm_chain())
            run_rr(chains)

        class SubView:
            def __init__(self, parent, sub, fn):
                self._res = parent._res
                self._sub = sub
                self.fn = fn

            def __getitem__(self, k):
                return self.fn()[k]

        ptmp = list(ftmp) + [SubView(mixed, ("f", i), lambda i=i: mixed[:].rearrange("p t d -> p (t d)")[:, i * 256:(i + 1) * 256])
                             for i in range(8)] + [SubView(ytmp, ("y", i), lambda i=i: ytmp[:, i * 256:(i + 1) * 256]) for i in range(4)]
        pfree = list(ptmp)

        def next_ptmp():
            assert pfree, "projection temp pool exhausted"
            return pfree.pop(0)

        def free_ptmp(*ts):
            for t in ts:
                pfree.append(t)

        ropebs = [SubView(big, ("rp", i), lambda i=i: big[:, i * 256:(i + 1) * 256]) for i in range(8)]
        atmp = list(ftmp) + [SubView(ytmp, ("y", i), lambda i=i: ytmp[:, i * 256:(i + 1) * 256]) for i in range(4)]
        bada = [SubView(ttmp[i], ("bada",), lambda i=i: ttmp[i][0:2, :]) for i in range(2)]
        modrows = [SubView(ytmp, ("mr", i), lambda i=i: ytmp[0:2, i * 512:(i + 1) * 512]) for i in range(2)]
        actr = [0]
        rctr = [0]

        def run_rr(gens):
            gens = list(gens)
            while gens:
                for g in list(gens):
                    try:
                        next(g)
                    except StopIteration:
                        gens.remove(g)

        def g_rope(ps, tin, out_ap, out_res, q):
            rb = ropebs[rctr[0] % 8]
            rctr[0] += 1
            if ps is not None:
                t = next_ptmp()
                act(t[:], ps[:, 0:256], AF.Identity, [ps], [t])
                act(rb[:], ps[:, 0:256], AF.Identity, [ps], [rb])
                P.unpin(ps)
            else:
                t = tin
                act(rb[:], t[:], AF.Identity, [t], [rb])
            yield
            pr = P.psum(pin=True)
            P.mm(pr[:, 0:256], pswb[:], rb[:], True, True, [pswb, rb], [pr])
            yield
            dve(lambda e, t=t: e.tensor_mul(out=t[:], in0=t[:], in1=cst[:, C_RC + q * 256:C_RC + (q + 1) * 256]), [t, cst], [t])
            t2 = next_ptmp()
            dve(lambda e, t2=t2, pr=pr: e.tensor_mul(out=t2[:], in0=pr[:, 0:256], in1=cst[:, C_RS + q * 256:C_RS + (q + 1) * 256]), [pr, cst], [t2])
            P.unpin(pr)
            dve(lambda e, t=t, t2=t2: e.tensor_add(out=out_ap, in0=t[:], in1=t2[:]), [t, t2], [out_res])
            free_ptmp(t2)
            if ps is not None:
                free_ptmp(t)

        def g_rms(ps, gcol, out_ap, out_res):
            t = next_ptmp()
            act(t[:], ps[:, 0:256], AF.Square, [ps], [t])
            yield
            ps2 = P.psum(pin=True)
            P.mm(ps2[:, 0:256], cst[:, C_BONES:C_BONES + 128], t[:], True, True, [cst, t], [ps2])
            yield
            t2 = next_ptmp()
            act(t2[:], ps2[:, 0:256], AF.Ln, [ps2], [t2], bias=epsc[:, 0:1], scale=1.0)
            P.unpin(ps2)
            act(t2[:], t2[:], AF.Exp, [t2], [t2], scale=-0.5)
            yield
            dve(lambda e, t2=t2: e.scalar_tensor_tensor(out=out_ap, in0=ps[:, 0:256], scalar=gcol, in1=t2[:],
                                                        op0=ALU.mult, op1=ALU.mult), [ps, t2, qkg], [out_res])
            P.unpin(ps)
            free_ptmp(t, t2)

        def g_hgrn(li, d, c, ps):
            sig = next_ptmp()
            act(sig[:], ps[:, 0:256], AF.Sigmoid, [ps], [sig])
            P.unpin(ps)
            yield
            lbc = lbt[:, d, li, c, 0:1]
            omc = lbt[:, d, li, c, 1:2]
            dve(lambda e: e.tensor_scalar(out=sig[:], in0=sig[:], scalar1=omc, scalar2=lbc, op0=ALU.mult, op1=ALU.add), [sig, lbt], [sig])
            dve(lambda e: e.tensor_scalar_max(out=sig[:], in0=sig[:], scalar1=F_MIN), [sig], [sig])
            yield
            g = next_ptmp()
            act(g[:], sig[:], AF.Ln, [sig], [g])
            yield
            b = next_ptmp()
            k32 = next_ptmp()
            dve(lambda e: e.tensor_tensor_scan(out=b[:], data0=cst[:, C_RESET:C_RESET + 256], data1=g[:], initial=0.0,
                                               op0=ALU.mult, op1=ALU.add), [cst, g], [b])
            if d == 1:
                dve(lambda e: e.tensor_sub(out=k32[:], in0=g[:], in1=b[:]), [g, b], [k32])
                dve(lambda e: e.tensor_tensor(out=g[:].rearrange("p (j t) -> p j t", t=CH), in0=k32[:].rearrange("p (j t) -> p j t", t=CH),
                                              in1=b[:].rearrange("p (j t) -> p j t", t=CH)[:, :, CH - 1:CH].to_broadcast([128, NCH, CH]),
                                              op=ALU.add), [k32, b], [g])
                bb, eb = g, b
            else:
                bb, eb = b, g
            dve(lambda e: e.tensor_scalar(out=sig[:], in0=sig[:], scalar1=-1.0, scalar2=1.0, op0=ALU.mult, op1=ALU.add), [sig], [sig])
            yield
            act(eb[:], bb[:], AF.Exp, [bb], [eb])
            act(bb[:], bb[:], AF.Exp, [bb], [bb], scale=-1.0)
            yield
            pos_e = CH - 1 if d == 0 else 0
            dve(lambda e: e.tensor_copy(out=eend[d][:, c, :], in_=eb[:].rearrange("p (j t) -> p j t", t=CH)[:, :, pos_e]), [eb], [eend[d]])
            dve(lambda e: e.tensor_mul(out=qt[d][:, c, :], in0=sq[:, c, :], in1=eb[:]), [sq, eb], [(qt[d], c)])
            dve(lambda e: e.tensor_mul(out=k32[:], in0=sig[:], in1=bb[:]), [sig, bb], [k32])
            dve(lambda e: e.tensor_copy(out=kt[d][:, c, :], in_=k32[:]), [k32], [(kt[d], c)])
            yield
            pt = P.psum(pin=True)
            for lt in range(2):
                P.tr(pt[:, lt * 128:(lt + 1) * 128], k32[:, lt * 128:(lt + 1) * 128], idf, [k32, cst], [pt])
            yield
            act(ktok[d][:, :, c * 128:(c + 1) * 128], pt[:, 0:256].rearrange("p (l k) -> p l k", l=2), AF.Identity, [pt], [(ktok[d], c)])
            P.unpin(pt)
            free_ptmp(sig, g, b, k32)

        def g_out_T(t, dst_ap):
            pt = P.psum(pin=True)
            for lt in range(2):
                P.tr(pt[:, lt * 128:(lt + 1) * 128], t[:, lt * 128:(lt + 1) * 128], idf, [t, cst], [pt])
            yield
            t2 = next_ptmp()
            act(t2[:], pt[:, 0:256], AF.Identity, [pt], [t2])
            P.unpin(pt)
            P.dma("sp", dst_ap, t2[:].rearrange("p (l f) -> p l f", l=2), [t2], [])
            free_ptmp(t2)

        def proj_phase(u, li, q=0):
            is_s = (u == 2)
            mark = lambda nm: (print("MARK", li, u, nm, len(P.ops)) if os.environ.get("DEBUGP") else None)
            mark("start")
            dve(lambda e: e.memset(small[:, 59:60], 0.0), [], [mixed, big])
            mod_T(u, li, 0)
            mark("modT")
            win = w_in[li].rearrange("(kc p) n -> p kc n", p=128)
            pieces = [
                (0, [("qa", 0), ("qa", 1), ("qa", 2), ("qa", 3)], []),
                (512, [("ka", 0), ("ka", 1), ("ka", 2), ("ka", 3)], []),
                (1024, [], [("va", 0, 512)]),
                (1536, [("qb", 0), ("qb", 1), ("ff", 0), ("ff", 1)], []),
                (2048, [("fb", 0), ("fb", 1)], [("ib", 256, 256)]),
                (2560, [None, None, ("qc", 0), ("qc", 1)], [("gb", 0, 256)]),
                (3072, [("kc", 0)], [("vc", 128, 128)]),
            ]

            def f_handler(kind, idx, ps):
                if kind == "qa" or kind == "ka":
                    zc = idx if kind == "qa" else 4 + idx
                    if not is_s:
                        act(zqk[:, zc, :], ps[:, 0:256], AF.Identity, [ps], [(zqk, zc)])
                        if kind == "ka":
                            t = next_ptmp()
                            act(t[:], ps[:, 0:256], AF.Identity, [ps], [t])
                            P.unpin(ps)
                            yield
                            yield from g_out_T(t, o_ak[u, li, :, idx * 128:(idx + 1) * 128].rearrange("(l p) f -> p l f", p=128))
                            free_ptmp(t)
                        else:
                            P.unpin(ps)
                    elif kind == "qa":
                        yield from g_rope(ps, None, zqk[:, zc, :], (zqk, zc), q)
                    else:
                        yield from g_rope(ps, None, kseq[:, idx, q * 256:(q + 1) * 256], (kseq, (idx, q)), q)
                elif kind == "qc":
                    gcol = qkg[:, li, 0:1]
                    if not is_s:
                        yield from g_rms(ps, gcol, zqk[:, 8 + idx, :], (zqk, 8 + idx))
                    else:
                        t = next_ptmp()
                        yield from g_rms(ps, gcol, t[:], t)
                        yield
                        yield from g_rope(None, t, zqk[:, 8 + idx, :], (zqk, 8 + idx), q)
                        free_ptmp(t)
                elif kind == "kc":
                    gcol = qkg[:, li, 1:2]
                    t = next_ptmp()
                    yield from g_rms(ps, gcol, t[:], t)
                    yield
                    if not is_s:
                        dve(lambda e, t=t: e.tensor_copy(out=zqk[:, 10, :], in_=t[:]), [t], [(zqk, 10)])
                        P.dma("sp", zqk[0:64, 11, :], zqk[64:128, 10, :], [(zqk, 10)], [(zqk, 11)])
                        P.dma("sp", zqk[64:128, 11, :], zqk[0:64, 10, :], [(zqk, 10)], [(zqk, 11)])
                        yield from g_out_T(t, o_ck[u, li, :, :].rearrange("(l p) f -> p l f", p=128))
                    else:
                        yield from g_rope(None, t, kseq[:, 4, q * 256:(q + 1) * 256], (kseq, (4, q)), q)
                        P.dma("sp", kseq[0:64, 5, q * 256:(q + 1) * 256], kseq[64:128, 4, q * 256:(q + 1) * 256], [(kseq, (4, q))], [(kseq, (5, q))])
                        P.dma("sp", kseq[64:128, 5, q * 256:(q + 1) * 256], kseq[0:64, 4, q * 256:(q + 1) * 256], [(kseq, (4, q))], [(kseq, (5, q))])
                    free_ptmp(t)
                elif kind == "qb":
                    act(sq[:, idx, :], ps[:, 0:256], AF.Silu, [ps], [sq])
                    P.unpin(ps)
                elif kind in ("ff", "fb"):
                    yield from g_hgrn(li, 0 if kind == "ff" else 1, idx, ps)

            def t_handler(kind, lt, ps):
                vk = (2 * q + lt) if is_s else lt
                if kind == "va":
                    act(vaug[:, vk, :, 0:128], ps[:, :].rearrange("p (h v) -> p h v", h=4), AF.Identity, [ps], [vaug])
                    if not is_s:
                        t = next_ttmp()
                        act(t[:], ps[:, :], AF.Identity, [ps], [t])
                        P.dma("sp", o_av[u, li, lt * 128:(lt + 1) * 128, :], t[:], [t], [])
                    P.unpin(ps)
                elif kind == "vc":
                    act(vcaug[:, vk, :, 0:64], ps[:, 0:128].rearrange("p (h v) -> p h v", h=2), AF.Identity, [ps], [vcaug])
                    if not is_s:
                        t = next_ttmp()
                        act(t[:, 0:128], ps[:, 0:128], AF.Identity, [ps], [t])
                        P.dma("sp", o_cv[u, li, lt * 128:(lt + 1) * 128, :], t[:, 0:128], [t], [])
                    P.unpin(ps)
                elif kind == "ib":
                    act(ibb[:, lt, :], ps[:, 0:256], AF.Identity, [ps], [ibb])
                    P.unpin(ps)
                elif kind == "gb":
                    t = next_ptmp()
                    act(t[:], ps[:, 0:256], AF.Silu, [ps], [t])
                    P.unpin(ps)
                    yield
                    dve(lambda e, t=t, lt=lt: e.tensor_mul(out=wg[:, lt, :], in0=t[:], in1=gng[:, li, :]), [t, gng], [wg])
                    free_ptmp(t)
                return
                yield

            prev = []
            for col0, fch, tgr in pieces:
                ncol = min(512, NIN - col0)
                s, t3, h = load_w(win[:, :, col0:col0 + ncol], key=("in", li, col0))
                gens = []
                for ci, fc in enumerate(fch):
                    if fc is None:
                        continue
                    kind, idx = fc
                    mark("F " + kind + str(idx))
                    ps = P.psum(pin=True)
                    for kc in range(8):
                        P.mm(ps[:, 0:256], t3[:, kc, ci * 128:(ci + 1) * 128], hT[:, kc, :], kc == 0, kc == 7, [hT, wsub(s, h, kc)], [ps])
                    gens.append(f_handler(kind, idx, ps))
                for (kind, lc0, n) in tgr:
                    mark("T " + kind)
                    for lt in range(2):
                        ps = P.psum(pin=True)
                        for kc in range(8):
                            P.mm(ps[:, 0:n], hT[:, kc, lt * 128:(lt + 1) * 128], t3[:, kc, lc0:lc0 + n], kc == 0, kc == 7, [hT, wsub(s, h, kc)], [ps])
                        gens.append(t_handler(kind, lt, ps))
                alive = []
                for g in gens:
                    try:
                        next(g)
                        alive.append(g)
                    except StopIteration:
                        pass
                run_rr(prev)
                prev = alive
            run_rr(prev)

        def prompt_mixers(u, li):
            mark = lambda nm: (print("MARK", li, u, nm, len(P.ops)) if os.environ.get("DEBUGP") else None)
            dve(lambda e: e.memset(small[:, 61:62], 0.0), [], [big, mixed])
            for d in range(2):
                dve(lambda e, d=d: e.memset(S32[d][:, :, 0, :], 0.0), [], [S32[d]])
            mark("scan")
            hgrn_scan(u, li)
            mark("hout")
            for d in range(2):
                dst = o_sf if d == 0 else o_sb
                for hd in range(4):
                    c, hh = hd // 2, hd % 2
                    P.dma("sp", dst[u, li, hd, :, :], S32[d][hh * 64:(hh + 1) * 64, c, 0, hh * 64:(hh + 1) * 64], [S32[d]], [])
            hgrn_output(u, li)
            mark("attn")
            attention(u, li, 2,
                      lambda h, m, kc: zqk[m * 64:(m + 1) * 64, 4 + h, kc * 128:(kc + 1) * 128],
                      lambda pb, kc: zqk[pb:pb + 64, 10, kc * 128:(kc + 1) * 128],
                      lambda pb, kc: zqk[pb:pb + 64, 11, kc * 128:(kc + 1) * 128], zqk)

        def tail_phase(u, li):
            mark = lambda nm: (print("MARK", li, u, nm, len(P.ops)) if os.environ.get("DEBUGP") else None)
            mark("outproj")
            for lt in range(2):
                for half in range(2):
                    ps = P.psum()
                    for jj in range(4):
                        j = half * 4 + jj
                        P.tr(ps[:, jj * 128:(jj + 1) * 128], mixed[:, lt, j * 128:(j + 1) * 128], idf, [(mixed, lt), cst], [ps])
                    if evac_eng() == "act":
                        act(hT[:, half * 4:half * 4 + 4, lt * 128:(lt + 1) * 128], ps[:, :].rearrange("p (j t) -> p j t", j=4), AF.Identity, [ps], [hT])
                    else:
                        dve(lambda e, ps=ps, half=half, lt=lt: e.tensor_copy(out=hT[:, half * 4:half * 4 + 4, lt * 128:(lt + 1) * 128],
                                                                            in_=ps[:, :].rearrange("p (j t) -> p j t", j=4)), [ps], [hT])
            wo = w_out[li].rearrange("(kc p) n -> p kc n", p=128)
            slots = [load_w(wo[:, :, hf * 512:(hf + 1) * 512], key=("out", li, hf)) for hf in range(2)]
            chains = []
            for lt in range(2):
                banks = []
                for hf in range(2):
                    s, t3, h = slots[hf]
                    ps = P.psum(pin=True)
                    for kc in range(8):
                        P.mm(ps[:, :], hT[:, kc, lt * 128:(lt + 1) * 128], t3[:, kc, :], kc == 0, kc == 7, [hT, wsub(s, h, kc)], [ps])
                    banks.append(ps)
                chains.append(g_resln(u, lt, li, 0, banks))
            run_rr(chains)
            mark("ln1 done")
            mark("mlp")
            mod_T(u, li, 1)
            dve(lambda e: e.memset(small[:, 60:61], 0.0), [], [big])
            w1 = w_ff1[li].rearrange("(kc p) n -> p kc n", p=128)
            for pc in range(8):
                s, t3, h = load_w(w1[:, :, pc * 512:(pc + 1) * 512], key=("f1", li, pc))
                for ci in range(4):
                    ps = P.psum()
                    for kc in range(8):
                        P.mm(ps[:, 0:256], t3[:, kc, ci * 128:(ci + 1) * 128], hT[:, kc, :], kc == 0, kc == 7, [hT, wsub(s, h, kc)], [ps])
                    t = next_ftmp()
                    act(t[:], ps[:, 0:256], AF.Relu, [ps], [t])
                    dve(lambda e, t=t, pc=pc, ci=ci: e.tensor_mul(out=hidT[:, pc * 4 + ci, :], in0=t[:], in1=t[:]), [t], [(hidT, ('h', pc * 4 + ci))])
            mark("ff2")
            w2 = w_ff2[li].rearrange("(kc p) n -> p kc n", p=128)
            banks = [[P.psum(pin=True), P.psum(pin=True)] for lt in range(2)]
            for pc in range(8):
                s, t3, h = load_w(w2[:, pc * 4:(pc + 1) * 4, :], key=("f2", li, pc))
                for lt in range(2):
                    for hf in range(2):
                        for kl in range(4):
                            kc = pc * 4 + kl
                            P.mm(banks[lt][hf][:, :], hidT[:, kc, lt * 128:(lt + 1) * 128], t3[:, kl, hf * 512:(hf + 1) * 512],
                                 kc == 0, kc == 31, [(hidT, ('h', kc)), wsub(s, h, kl)], [banks[lt][hf]])
            run_rr([g_resln(u, lt, li, 1, banks[lt]) for lt in range(2)])

        def rope(ps, tin, out_ap, out_res, q):
            if ps is not None:
                t = next_ftmp()
                act(t[:], ps[:, 0:256], AF.Identity, [ps], [t])
                act(ropeb[:], ps[:, 0:256], AF.Identity, [ps], [ropeb])
            else:
                t = tin
                act(ropeb[:], t[:], AF.Identity, [t], [ropeb])
            pr = P.psum()
            P.mm(pr[:, 0:256], pswb[:], ropeb[:], True, True, [pswb, ropeb], [pr])
            dve(lambda e, t=t: e.tensor_mul(out=t[:], in0=t[:], in1=cst[:, C_RC + q * 256:C_RC + (q + 1) * 256]), [t, cst], [t])
            t2 = next_ftmp()
            dve(lambda e, t2=t2, pr=pr: e.tensor_mul(out=t2[:], in0=pr[:, 0:256], in1=cst[:, C_RS + q * 256:C_RS + (q + 1) * 256]), [pr, cst], [t2])
            dve(lambda e, t=t, t2=t2: e.tensor_add(out=out_ap, in0=t[:], in1=t2[:]), [t, t2], [out_res])

        def xsrc(li, q):
            if li == 0:
                return xs[q * 256:(q + 1) * 256, :].rearrange("(t p) d -> p t d", p=128), []
            return XS[q].ap(), [XS[q]]

        def sample_layer(li, x0_loaded=False):
            for kc in range(4):
                t = next_ttmp()
                P.dma("sp", t[:], ck_a[li, kc * 128:(kc + 1) * 128, :], [], [t])
                ps = P.psum()
                for h in range(4):
                    P.tr(ps[:, h * 128:(h + 1) * 128], t[:, h * 128:(h + 1) * 128], idf, [t, cst], [ps])
                act(kseq[:, 0:4, 1024 + kc * 128:1024 + (kc + 1) * 128], ps[:, :].rearrange("p (h k) -> p h k", h=4), AF.Identity,
                    [ps], [(kseq, ("c", kc))])
                t2 = next_ttmp()
                P.dma("sp", t2[:, 0:128], ck_c[li, kc * 128:(kc + 1) * 128, :], [], [t2])
                ps2 = P.psum()
                P.tr(ps2[:, 0:128], t2[:, 0:128], idf, [t2, cst], [ps2])
                act(kseq[:, 4, 1024 + kc * 128:1024 + (kc + 1) * 128], ps2[:, 0:128], AF.Identity, [ps2], [(kseq, ("cc", kc))])
                P.dma("sp", kseq[0:64, 5, 1024 + kc * 128:1024 + (kc + 1) * 128], kseq[64:128, 4, 1024 + kc * 128:1024 + (kc + 1) * 128],
                      [(kseq, ("cc", kc))], [(kseq, ("cs", kc))])
                P.dma("sp", kseq[64:128, 5, 1024 + kc * 128:1024 + (kc + 1) * 128], kseq[0:64, 4, 1024 + kc * 128:1024 + (kc + 1) * 128],
                      [(kseq, ("cc", kc))], [(kseq, ("cs", kc))])
                P.dma("pool", vaug[:, 8 + kc, :, 0:128], cv_a[li, kc * 128:(kc + 1) * 128, :].rearrange("p (h v) -> p h v", h=4), [], [vaug])
                P.dma("pool", vcaug[:, 8 + kc, :, 0:64], cv_c[li, kc * 128:(kc + 1) * 128, :].rearrange("p (h v) -> p h v", h=2), [], [vcaug])
            for q in range(4):
                if not (q == 0 and x0_loaded):
                    src, rd = xsrc(li, q)
                    P.dma("sp", x_tm[2][:], src, rd, [x_tm[2]])
                proj_phase(2, li, q)
                P.dma("sp", QS[q].ap()[:, 0:4, :], zqk[:, 0:4, :], [zqk], [QS[q]])
                P.dma("sp", QS[q].ap()[:, 4:6, :], zqk[:, 8:10, :], [zqk], [QS[q]])
                for d in range(2):
                    P.dma("sp", HS[q].ap()[:, 2 * d], qt[d][:], [qt[d]], [HS[q]])
                    P.dma("sp", HS[q].ap()[:, 2 * d + 1], kt[d][:], [kt[d]], [HS[q]])
                    P.dma("sp", KT[q].ap()[:, d], ktok[d][:], [ktok[d]], [KT[q]])
                    P.dma("sp", EE[q].ap()[:, d], eend[d][:], [eend[d]], [EE[q]])
                P.dma("sp", IB[q].ap(), ibb[:], [ibb], [IB[q]])
                P.dma("sp", WG[q].ap(), wg[:], [wg], [WG[q]])
            dve(lambda e: e.memset(small[:, 56:57], 0.0), [], [big])
            ibS = {d: SubView(big, ("ib", d), lambda d=d: big[:, d * 2560:d * 2560 + 512].rearrange("p (l f) -> p l f", l=2)) for d in range(2)}
            vbS = {d: SubView(big, ("vb", d), lambda d=d: big[:, d * 2560 + 512:(d + 1) * 2560].rearrange("p (l c j v) -> p l c j v", l=2, c=2, j=4))
                   for d in range(2)}
            for d in range(2):
                stin = st_f if d == 0 else st_b
                dve(lambda e, d=d: e.memset(S32[d][:, :, 0, :], 0.0), [], [S32[d]])
                for hd in range(4):
                    c, hh = hd // 2, hd % 2
                    P.dma("sp", S32[d][hh * 64:(hh + 1) * 64, c, 0, hh * 64:(hh + 1) * 64], stin[li, hd, :, :], [], [S32[d]])
            for qi in range(4):
                for d in range(2):
                    q = qi if d == 0 else 3 - qi
                    P.dma("sp", ktok[d][:], KT[q].ap()[:, d], [KT[q]], [ktok[d]])
                    P.dma("sp", eend[d][:], EE[q].ap()[:, d], [EE[q]], [eend[d]])
                    P.dma("sp", ibS[d][:], IB[q].ap(), [IB[q]], [ibS[d]])
                hgrn_scan(2, li, dirs=(0, 1), ibs=ibS, vbs=vbS)
                for d in range(2):
                    q = qi if d == 0 else 3 - qi
                    P.dma("sp", SS[q].ap()[:, d], Sbf[d][:], [Sbf[d]], [SS[q]])
            def load_x(q):
                src, rd = xsrc(li, q)
                P.dma("sp", x_tm[2][:], src, rd, [x_tm[2]])

            def load_ops(q, eng):
                P.dma(eng, zqk[:, 0:4, :], QS[q].ap()[:, 0:4, :], [QS[q]], [zqk])
                P.dma(eng, zqk[:, 8:10, :], QS[q].ap()[:, 4:6, :], [QS[q]], [zqk])
                for d in range(2):
                    P.dma(eng, qt[d][:], HS[q].ap()[:, 2 * d], [HS[q]], [qt[d]])
                    P.dma(eng, kt[d][:], HS[q].ap()[:, 2 * d + 1], [HS[q]], [kt[d]])
                    P.dma(eng, Sbf[d][:], SS[q].ap()[:, d], [SS[q]], [Sbf[d]])
                P.dma(eng, ibb[:], IB[q].ap(), [IB[q]], [ibb])
                P.dma(eng, wg[:], WG[q].ap(), [WG[q]], [wg])

            def load_own():
                dve(lambda e: e.memset(small[:, 62:63], 0.0), [], [big])
                tmp32 = mixed[:].rearrange("p t d -> p (t d)")
                off = {True: 0, False: 0}

                def sel(dst, dst_res, n, srcs, f32):
                    cap = 2048 if f32 else 9216
                    base = tmp32 if f32 else big
                    pres = mixed if f32 else big
                    tms = []
                    for q in range(1, 4):
                        if off[f32] + n > cap:
                            off[f32] = 0
                            dve(lambda e: e.memset(small[:, 57:58], 0.0), [], [pres])
                        o = off[f32]
                        off[f32] += n
                        tms.append((base[:, o:o + n], (pres, ("sel", o, n))))
                    sap, sres = srcs(0)
                    P.dma("sp", dst, sap, [sres], [dst_res])
                    for q in range(1, 4):
                        sap, sres = srcs(q)
                        P.dma("sp", tms[q - 1][0], sap, [sres], [tms[q - 1][1]])
                    dve(lambda e, dst=dst: e.tensor_scalar_mul(out=dst, in0=dst, scalar1=ohT[:, 0:1]), [dst_res, ohT], [dst_res])
                    for q in range(1, 4):
                        tm = tms[q - 1][0]
                        dve(lambda e, dst=dst, tm=tm, q=q: e.scalar_tensor_tensor(out=dst, in0=tm, scalar=ohT[:, q:q + 1], in1=dst,
                                                                                  op0=ALU.mult, op1=ALU.add), [tms[q - 1][1], dst_res, ohT], [dst_res])
                xv = x_tm[2][:].rearrange("p t d -> p (t d)")
                for blk in range(4):
                    sel(xv[:, blk * 512:(blk + 1) * 512], x_tm[2], 512,
                        lambda q, blk=blk: (XS[q].ap().rearrange("p t d -> p (t d)")[:, blk * 512:(blk + 1) * 512], XS[q]), True)
                sel(zqk[:, 0:4, :].rearrange("p a b -> p (a b)"), zqk, 1024, lambda q: (QS[q].ap()[:, 0:4, :].rearrange("p a b -> p (a b)"), QS[q]), False)
                sel(zqk[:, 8:10, :].rearrange("p a b -> p (a b)"), zqk, 512, lambda q: (QS[q].ap()[:, 4:6, :].rearrange("p a b -> p (a b)"), QS[q]), False)
                for d in range(2):
                    sel(qt[d][:].rearrange("p a b -> p (a b)"), qt[d], 512, lambda q, d=d: (HS[q].ap()[:, 2 * d].rearrange("p a b -> p (a b)"), HS[q]), False)
                    sel(kt[d][:].rearrange("p a b -> p (a b)"), kt[d], 512, lambda q, d=d: (HS[q].ap()[:, 2 * d + 1].rearrange("p a b -> p (a b)"), HS[q]), False)
                    sel(Sbf[d][:].rearrange("p a b c -> p (a b c)"), Sbf[d], 2304, lambda q, d=d: (SS[q].ap()[:, d].rearrange("p a b c -> p (a b c)"), SS[q]), False)
                sel(ibb[:].rearrange("p a b -> p (a b)"), ibb, 512, lambda q: (IB[q].ap().rearrange("p a b -> p (a b)"), IB[q]), False)
                sel(wg[:].rearrange("p a b -> p (a b)"), wg, 512, lambda q: (WG[q].ap().rearrange("p a b -> p (a b)"), WG[q]), True)

            def mix_tail(prefetch=None):
                dve(lambda e: e.memset(small[:, 61:62], 0.0), [], [big, mixed])
                hgrn_output(2, li, copy_state=False)
                attention(2, li, 12,
                          lambda h, m, kc: kseq[m * 64:(m + 1) * 64, h, kc * 128:(kc + 1) * 128],
                          lambda pb, kc: kseq[pb:pb + 64, 4, kc * 128:(kc + 1) * 128],
                          lambda pb, kc: kseq[pb:pb + 64, 5, kc * 128:(kc + 1) * 128], kseq)
                if prefetch is not None:
                    load_ops(prefetch, "pool")
                tail_phase(2, li)

            if li == 0:
                load_ops(0, "sp")
                for q in range(4):
                    load_x(q)
                    mix_tail(prefetch=q + 1 if q < 3 else None)
                    P.dma("sp", XS[q].ap(), x_tm[2][:], [x_tm[2]], [XS[q]])
            else:
                load_own()
                mix_tail()
                P.dma("sp", ys.rearrange("(t p) d -> p t d", p=128), x_tm[2][:], [x_tm[2]], [])

        units = [0, 1]
        nlayers = 2
        if stop is not None:
            units = units[:stop.get("units", len(units))]
            nlayers = stop.get("layers", 2)
        xbuf = [x_t, x_tB]
        xsel = [0]

        def use_x(i):
            xsel[0] = i
            x_tm[0] = x_tm[1] = x_tm[2] = xbuf[i]
            return xbuf[i]

        def load_prompt_x(li, u, xt):
            if li == 0:
                P.dma("sp", xt[:], xp[u * 256:(u + 1) * 256, :].rearrange("(t p) d -> p t d", p=128), [], [xt])
            else:
                P.dma("sp", xt[:], XP[u].ap(), [XP[u]], [xt])

        for li in range(nlayers):
            mod_part(li, 0)
            cur = use_x(xsel[0])
            load_prompt_x(li, units[0], cur) if units else None
            for ui, u in enumerate(units):
                cur = xbuf[xsel[0]]
                proj_phase(u, li)
                if u == 0:
                    mod_part(li, 1)
                prompt_mixers(u, li)
                nxt = xbuf[1 - xsel[0]]
                if ui + 1 < len(units):
                    load_prompt_x(li, units[ui + 1], nxt)
                elif with_sample:
                    src, rd = xsrc(li, 0)
                    P.dma("sp", nxt[:], src, rd, [nxt])
                tail_phase(u, li)
                if li == 0:
                    P.dma("sp", XP[u].ap(), cur[:], [cur], [XP[u]])
                else:
                    P.dma("sp", yp[u * 256:(u + 1) * 256, :].rearrange("(t p) d -> p t d", p=128), cur[:], [cur], [])
                use_x(1 - xsel[0])
            if with_sample and (stop is None or stop.get("sample", True)):
                sample_layer(li, x0_loaded=bool(units))
        P.emit()
    return nc


def kernel(**inp):
    f32 = lambda a: np.ascontiguousarray(np.asarray(a, dtype=np.float32))
    x_prompt = f32(inp["x_prompt"]); x_sample = f32(inp["x_sample"])
    nc = build_nc(with_sample=WITH_SAMPLE)
    shared = {
        "w_ada": f32(inp["w_ada"]), "b_ada": f32(inp["b_ada"]), "w_in": f32(inp["w_in"]), "w_out": f32(inp["w_out"]),
        "w_ff1": f32(inp["w_ff1"]), "w_ff2": f32(inp["w_ff2"]),
        "lamv": f32(np.stack([inp["lam_q1"], inp["lam_k1"], inp["lam_q2"], inp["lam_k2"]], axis=1)),
        "subln_g": f32(inp["subln_g"]),
        "lbl": f32(np.stack([inp["lb_logits_fwd"], inp["lb_logits_bwd"]], axis=0)),
        "gnorm_g": f32(inp["gnorm_g"]), "qnorm_g": f32(inp["qnorm_g"]), "knorm_g": f32(inp["knorm_g"]),
        "lnp": f32(np.stack([inp["ln1_g"], inp["ln1_b"], inp["ln2_g"], inp["ln2_b"]], axis=1)),
    }
    in_maps = []
    for i in range(8):
        sq_, qd = i // 4, i % 4
        cm = np.stack([np.asarray(inp["c_ctx"], np.float32), np.asarray(inp["c"], np.float32)[sq_]], axis=0)
        cmodT = f32(cm.reshape(2, 8, 128).transpose(2, 1, 0).reshape(128, 16))
        oh = np.zeros((128, 8), np.float32)
        oh[:, qd] = 1.0
        m = dict(shared)
        m.update({
            "xp": f32(x_prompt[2 * i:2 * i + 2].reshape(512, D)),
            "xs": f32(x_sample[sq_]),
            "cmodT": cmodT, "consts": make_consts(0),
            "ck_a": f32(np.asarray(inp["cache_a_k"])[sq_].reshape(2, 512, 512)),
            "cv_a": f32(np.asarray(inp["cache_a_v"])[sq_].reshape(2, 512, 512)),
            "ck_c": f32(np.asarray(inp["cache_c_k"])[sq_].reshape(2, 512, 128)),
            "cv_c": f32(np.asarray(inp["cache_c_v"])[sq_].reshape(2, 512, 128)),
            "st_f": f32(np.asarray(inp["state_b_fwd"])[sq_]), "st_b": f32(np.asarray(inp["state_b_bwd"])[sq_]),
            "onehot": oh,
        })
        in_maps.append(m)
    if DEBUG_HOOK is not None:
        return DEBUG_HOOK(in_maps)
    res = run_bass_kernel_spmd(nc, in_maps, core_ids=list(range(8)))
    R = res.results
    y_prompt = np.concatenate([r["yp"].reshape(2, 256, D) for r in R], axis=0)
    y_sample = np.stack([np.concatenate([R[s * 4 + q]["ys"] for q in range(4)], axis=0) for s in range(2)], axis=0)
    cat = lambda k, shp: np.concatenate([r[k] for r in R], axis=0).reshape(shp)
    return (y_prompt.astype(np.float32), y_sample.astype(np.float32),
            cat("o_ak", (16, 2, 256, 4, 2, 64)), cat("o_av", (16, 2, 256, 4, 128)),
            cat("o_ck", (16, 2, 256, 2, 64)), cat("o_cv", (16, 2, 256, 2, 64)),
            cat("o_sf", (16, 2, 4, 64, 64)), cat("o_sb", (16, 2, 4, 64, 64)))


WITH_SAMPLE = True
DEBUG_HOOK = None
```

```python
import numpy as np
import concourse.bass as bass
import concourse.mybir as mybir
from concourse.bass_utils import run_bass_kernel_spmd
from contextlib import ExitStack

F32 = mybir.dt.float32
BF16 = mybir.dt.bfloat16
AF = mybir.ActivationFunctionType
ALU = mybir.AluOpType
AX = mybir.AxisListType

ENGS = ("pe", "act", "dve", "pool", "sp")
NDMA_SLOTS = {"sp": 24, "pool": 24, "act": 4}
STORES_ON_POOL = True
SAME_ENG_SYNC = {"pe": False, "act": True, "dve": True, "pool": True, "sp": True}


import os
MAXOPS = int(os.environ.get('MAXOPS', '100000000'))


class Res:
    _n = 0

    def __init__(self, name):
        Res._n += 1
        self.id = Res._n
        self.name = name


class T:
    def __init__(self, h, name):
        self.h = h
        self._res = Res(name)

    def __getitem__(self, k):
        return self.h[k]

    def ap(self):
        return self.h.ap()


class Op:
    __slots__ = ("eng", "fn", "reads", "writes", "dma", "idx", "waits", "need_inc",
                 "cnt", "slot", "slot_val", "pre_wait")

    def __init__(self, eng, fn, reads, writes, dma):
        self.eng, self.fn, self.reads, self.writes, self.dma = eng, fn, reads, writes, dma
        self.waits = []
        self.need_inc = False
        self.cnt = None
        self.slot = None
        self.slot_val = None
        self.pre_wait = None


class Prog:
    def __init__(self, nc, stack):
        self.nc = nc
        self.stack = stack
        self.ops = []
        self.state = {}
        self.subs = {}
        self.ndma = {e: 0 for e in ENGS}
        self.psum_banks = []
        self.psum_i = 0
        self.pinned = []

    def sb(self, name, shape, dt=F32):
        t = self.stack.enter_context(self.nc.sbuf_tensor(name, list(shape), dt))
        return T(t, name)

    def dram(self, name, shape, dt=F32, kind="Internal"):
        t = self.nc.dram_tensor(name, list(shape), dt, kind=kind)
        return T(t, name)

    def init_psum(self, n=8):
        for i in range(n):
            t = self.stack.enter_context(self.nc.psum_tensor(f"psb{i}", [128, 512], F32))
            self.psum_banks.append(T(t, f"psb{i}"))

    def psum(self, pin=False):
        assert len(self.pinned) < len(self.psum_banks), "all PSUM banks pinned"
        while True:
            t = self.psum_banks[self.psum_i % len(self.psum_banks)]
            self.psum_i += 1
            if t not in self.pinned:
                break
        if pin:
            self.pinned.append(t)
        return t

    def unpin(self, t):
        self.pinned.remove(t)

    @staticmethod
    def _key(r):
        if isinstance(r, tuple):
            return (r[0]._res.id, r[1])
        return (r._res.id, getattr(r, "_sub", None))

    def _conflicts(self, key):
        rid, sub = key
        subs = self.subs.setdefault(rid, set())
        if sub is None:
            return [(rid, s) for s in subs | {None}]
        return [(rid, sub), (rid, None)]

    def op(self, eng, fn, reads=(), writes=(), dma=False):
        if len(self.ops) >= MAXOPS:
            return None
        o = Op(eng, fn, [self._key(r) for r in reads], [self._key(r) for r in writes], dma)
        o.idx = len(self.ops)
        deps = set()
        for k in o.reads:
            for ck in self._conflicts(k):
                st = self.state.get(ck)
                if st and st[0] is not None:
                    deps.add(st[0])
        for k in o.writes:
            for ck in self._conflicts(k):
                st = self.state.get(ck)
                if st:
                    if st[0] is not None:
                        deps.add(st[0])
                    deps.update(st[1])
        deps.discard(o.idx)
        o.waits = sorted(deps)
        for k in o.reads:
            self.subs.setdefault(k[0], set()).add(k[1])
            st = self.state.setdefault(k, [None, []])
            st[1].append(o.idx)
        for k in o.writes:
            self.subs.setdefault(k[0], set()).add(k[1])
            if k[1] is None:
                for s in list(self.subs[k[0]]):
                    self.state[(k[0], s)] = [o.idx, []]
            else:
                self.state[k] = [o.idx, []]
        if dma:
            j = self.ndma[eng]
            self.ndma[eng] += 1
            K = NDMA_SLOTS[eng]
            o.slot = j % K
            o.slot_val = 16 * (j // K + 1)
            if j >= K:
                o.pre_wait = (o.slot, 16 * (j // K))
        self.ops.append(o)
        return o

    def mm(self, out, lhsT, rhs, start, stop, reads, writes, **kw):
        return self.op("pe", lambda e: e.matmul(out, lhsT, rhs, start=start, stop=stop, **kw),
                       reads, writes)

    def tr(self, out, in_, ident, reads, writes):
        return self.op("pe", lambda e: e.transpose(out, in_, ident), reads, writes)

    def dma(self, eng, out, in_, reads, writes, **kw):
        if eng == "sp" and STORES_ON_POOL and str(getattr(out, "space", "")).endswith("DRAM") and not kw.get("keep_queue"):
            eng = "pool"
        kw.pop("keep_queue", None)
        return self.op(eng, lambda e: e.dma_start(out=out, in_=in_, **kw), reads, writes, dma=True)

    def emit(self):
        nc = self.nc
        ops = self.ops
        for o in ops:
            for d in o.waits:
                D = ops[d]
                if D.dma:
                    continue
                if D.eng == o.eng and not o.dma and not D.dma and not SAME_ENG_SYNC[o.eng]:
                    continue
                D.need_inc = True
        cnt = {e: 0 for e in ENGS}
        for o in ops:
            if not o.dma and o.need_inc:
                cnt[o.eng] += 1
                o.cnt = cnt[o.eng]
        sems = {e: self.stack.enter_context(nc.semaphore(f"s_{e}")) for e in ENGS}
        dsems = {e: [self.stack.enter_context(nc.semaphore(f"d_{e}{i}")) for i in range(n)]
                 for e, n in NDMA_SLOTS.items()}
        block = self.stack.enter_context(nc.Block())
        last_out_waits = []

        def run(engname, e):
            seen_eng = {x: 0 for x in ENGS}
            seen_dma = {}
            for o in ops:
                if o.eng != engname:
                    continue
                if o.dma and o.pre_wait is not None:
                    s, v = o.pre_wait
                    key = (engname, s)
                    if seen_dma.get(key, 0) < v:
                        e.wait_ge(dsems[engname][s], v)
                        seen_dma[key] = v
                for d in o.waits:
                    D = ops[d]
                    if D.dma:
                        key = (D.eng, D.slot)
                        if seen_dma.get(key, 0) < D.slot_val:
                            e.wait_ge(dsems[D.eng][D.slot], D.slot_val)
                            seen_dma[key] = D.slot_val
                    else:
                        if D.eng == engname and not o.dma and not SAME_ENG_SYNC[engname]:
                            continue
                        if seen_eng[D.eng] < D.cnt:
                            e.wait_ge(sems[D.eng], D.cnt)
                            seen_eng[D.eng] = D.cnt
                ins = o.fn(e)
                if o.dma:
                    ins.then_inc(dsems[engname][o.slot], 16)
                elif o.need_inc:
                    ins.then_inc(sems[engname], 1)
            for s in range(NDMA_SLOTS.get(engname, 0)):
                lastv = 0
                for o in ops:
                    if o.dma and o.eng == engname and o.slot == s:
                        lastv = o.slot_val
                if lastv and seen_dma.get((engname, s), 0) < lastv:
                    e.wait_ge(dsems[engname][s], lastv)

        @block.tensor
        def _(e):
            run("pe", e)

        @block.scalar
        def _(e):
            run("act", e)

        @block.vector
        def _(e):
            run("dve", e)

        @block.gpsimd
        def _(e):
            run("pool", e)

        @block.sync
        def _(e):
            run("sp", e)


D = 1024
NIN = 3328
ALPHA = 4 ** 0.25
LN_EPS = 1e-6
RMS_EPS = 1e-6
F_MIN = 1e-6
CH = 32
NCH = 256 // CH
C_ID, C_BONES, C_MF, C_MB, C_BLK2 = 0, 128, 256, 384, 512
C_ROW, C_COL, C_RESET, C_PSW, C_RC, C_RS, C_SEL2, C_SELBC = 640, 644, 1156, 1412, 1540, 2564, 3588, 3590
NCONST = 3590 + 256


def make_consts(tok0):
    c = np.zeros((128, NCONST), np.float32)
    p = np.arange(128)
    c[:, C_ID:C_ID + 128] = np.eye(128)
    c[:, C_BONES:C_BONES + 128] = (p[:, None] // 64 == p[None, :] // 64) / 64.0
    same = (p[:, None] // CH == p[None, :] // CH)
    c[:, C_MF:C_MF + 128] = same & (p[:, None] <= p[None, :])
    c[:, C_MB:C_MB + 128] = same & (p[:, None] >= p[None, :])
    c[:, C_BLK2:C_BLK2 + 128] = (p[:, None] // 64 == p[None, :] // 64)
    c[:, C_ROW:C_ROW + 4] = (p[:, None] // CH == np.arange(4)[None, :])
    c[:, C_COL:C_COL + 512] = np.broadcast_to((np.arange(4)[:, None] == (p[None, :] // CH)).reshape(1, 512), (128, 512))
    t = np.arange(256)
    c[:, C_RESET:C_RESET + 256] = np.broadcast_to((t % CH != 0)[None, :], (128, 256))
    c[:, C_PSW:C_PSW + 128] = (p[:, None] == (p[None, :] ^ 1))
    pos = np.arange(1024)
    row = (pos // 64).astype(np.float32)
    col = (pos % 64).astype(np.float32)
    inv = (10000.0 ** (-np.arange(16, dtype=np.float32) / 16)).astype(np.float32)
    ang = np.concatenate([row[:, None] * inv, col[:, None] * inv], axis=-1).astype(np.float32)
    f = p % 64
    cosT = np.cos(ang)[:, f // 2].T
    sinT = np.sin(ang)[:, f // 2].T
    sgn = np.where(f % 2 == 0, -1.0, 1.0)[:, None]
    c[:, C_RC:C_RC + 1024] = cosT
    c[:, C_RS:C_RS + 1024] = sinT * sgn
    c[0, C_SEL2] = 1.0
    c[1, C_SEL2 + 1] = 1.0
    c[0, C_SELBC:C_SELBC + 128] = 1.0
    c[1, C_SELBC + 128:C_SELBC + 256] = 1.0
    return c


def build_nc(with_sample=True, stop=None):
    nc = bass.Bass("TRN2", target_bir_lowering=False)
    din = lambda n, s: nc.dram_tensor(n, list(s), F32, kind="ExternalInput").ap()
    dout = lambda n, s: nc.dram_tensor(n, list(s), F32, kind="ExternalOutput").ap()
    xp = din("xp", [512, D]); xs = din("xs", [1024, D])
    cmodT = din("cmodT", [128, 16]); consts = din("consts", [128, NCONST])
    w_ada = din("w_ada", [2, D, 6 * D]); b_ada = din("b_ada", [2, 6 * D])
    w_in = din("w_in", [2, D, NIN]); w_out = din("w_out", [2, D, D])
    w_ff1 = din("w_ff1", [2, D, 4 * D]); w_ff2 = din("w_ff2", [2, 4 * D, D])
    lamv = din("lamv", [2, 4, 64]); subln_g = din("subln_g", [2, 128])
    lbl = din("lbl", [2, 2, 256])
    gnorm_g = din("gnorm_g", [2, 64]); qnorm_g = din("qnorm_g", [2, 64]); knorm_g = din("knorm_g", [2, 64])
    lnp = din("lnp", [2, 4, D])
    ck_a = din("ck_a", [2, 512, 512]); cv_a = din("cv_a", [2, 512, 512])
    ck_c = din("ck_c", [2, 512, 128]); cv_c = din("cv_c", [2, 512, 128])
    st_f = din("st_f", [2, 4, 64, 64]); st_b = din("st_b", [2, 4, 64, 64])
    onehot = din("onehot", [128, 8])
    yp = dout("yp", [512, D]); ys = dout("ys", [256, D])
    o_ak = dout("o_ak", [2, 2, 256, 512]); o_av = dout("o_av", [2, 2, 256, 512])
    o_ck = dout("o_ck", [2, 2, 256, 128]); o_cv = dout("o_cv", [2, 2, 256, 128])
    o_sf = dout("o_sf", [2, 2, 4, 64, 64]); o_sb = dout("o_sb", [2, 2, 4, 64, 64])

    with ExitStack() as stk:
        P = Prog(nc, stk)
        P.init_psum(8)
        rr = [0]

        def evac_eng():
            rr[0] += 1
            return "act" if rr[0] % 2 else "dve"

        cst = P.sb("cst", [128, NCONST])
        idf = cst[:, C_ID:C_ID + 128]
        x_t = P.sb("x_t", [128, 2, D])
        x_tm = [x_t, x_t, x_t]
        XP = [P.dram(f"XP{u}", [128, 2, D]) for u in range(2)]
        hT = P.sb("hT", [128, 8, 256], BF16)
        wr = [P.sb(f"wr{i}", [128, 4096], BF16) for i in range(3)]
        wctr = [0]
        x_tB = P.sb("x_tB", [128, 2, D])
        scT = P.sb("scT", [128, 16]); scTb = P.sb("scTb", [128, 16], BF16)
        mc = P.sb("mc", [128, 4, 8, 2])
        gbc = P.sb("gbc", [128, 2, 2, D], BF16)
        lnbc = P.sb("lnbc", [128, 4, D])
        pswb = P.sb("pswb", [128, 128], BF16)
        zqk = P.sb("zqk", [128, 12, 256], BF16)
        kseq = P.sb("kseq", [128, 6, 1536], BF16)
        ropeb = P.sb("ropeb", [128, 256], BF16)
        XS = [P.dram(f"XS{q}", [128, 2, D]) for q in range(4)]
        QS = [P.dram(f"QS{q}", [128, 6, 256], BF16) for q in range(4)]
        HS = [P.dram(f"HS{q}", [128, 4, 2, 256], BF16) for q in range(4)]
        KT = [P.dram(f"KT{q}", [128, 2, 2, 256], BF16) for q in range(4)]
        EE = [P.dram(f"EE{q}", [128, 2, 2, NCH]) for q in range(4)]
        IB = [P.dram(f"IB{q}", [128, 2, 256], BF16) for q in range(4)]
        WG = [P.dram(f"WG{q}", [128, 2, 256]) for q in range(4)]
        SS = [P.dram(f"SS{q}", [128, 2, 2, NCH + 1, 128], BF16) for q in range(4)]
        ftmp = [P.sb(f"ftmp{i}", [128, 256]) for i in range(4)]
        fctr = [0]
        ttmp = [P.sb(f"ttmp{i}", [128, 512]) for i in range(2)]
        tctr = [0]
        vaug = P.sb("vaug", [128, 12, 4, 129], BF16)
        vcaug = P.sb("vcaug", [128, 12, 2, 65], BF16)
        ibb = P.sb("ibb", [128, 2, 256], BF16)
        wg = P.sb("wg", [128, 2, 256])
        big = P.sb("big", [128, 9216], BF16)

        class View:
            def __init__(self, ap_fn):
                self._res = big._res
                self.ap_fn = ap_fn

            def __getitem__(self, k):
                return self.ap_fn()[k]
        Et = [View(lambda i=i: big[:, i * 3072:(i + 1) * 3072].rearrange("p (k q) -> p k q", k=12)) for i in range(3)]
        hidT = View(lambda: big[:, 0:8192].rearrange("p (k q) -> p k q", k=32))
        ectr = [0]
        mixed = P.sb("mixed", [128, 2, D])
        sq = P.sb("sq", [128, 2, 256])
        qt = [P.sb(f"qt{d}", [128, 2, 256], BF16) for d in range(2)]
        kt = [P.sb(f"kt{d}", [128, 2, 256], BF16) for d in range(2)]
        kt32 = P.sb("kt32", [128, 256])
        ktok = [P.sb(f"ktok{d}", [128, 2, 256], BF16) for d in range(2)]
        eend = [P.sb(f"eend{d}", [128, 2, NCH]) for d in range(2)]
        S32 = [P.sb(f"S32{d}", [128, 2, 2, 128]) for d in range(2)]
        Sbf = [P.sb(f"Sbf{d}", [128, 2, NCH + 1, 128], BF16) for d in range(2)]
        vblk = P.sb("vblk", [128, 2, 2, 4, 128], BF16)
        qblk2 = [P.sb(f"qblk{i}", [128, 4, 128], BF16) for i in range(2)]
        Ue4 = [P.sb(f"Ue{i}", [128, 4, 128]) for i in range(4)]
        Am = [P.sb(f"Am{i}", [128, 2, 128], BF16) for i in range(2)]
        ytmp = P.sb("ytmp", [128, D])
        small = P.sb("small", [128, 64])
        smallA = P.sb("smallA", [128, 64])
        lamt = P.sb("lamt", [128, 2, 4, 64]); lamc = P.sb("lamc", [128, 2, 4])
        subg = P.sb("subg", [128, 2, 128]); gng = P.sb("gng", [128, 2, 256])
        qkg = P.sb("qkg", [128, 2, 2])
        lbt = P.sb("lbt", [128, 2, 2, 2, 2])
        lbraw = P.sb("lbraw", [128, 2, 2, 2])
        stats = P.sb("stats", [128, 16])
        epsc = P.sb("epsc", [128, 1])
        stats2 = [stats, P.sb("statsB", [128, 16])]
        ohT = P.sb("ohT", [128, 8])

        def act(out, in_, func, reads, writes, **kw):
            return P.op("act", lambda e: e.activation(out=out, in_=in_, func=func, **kw), reads, writes)

        def dve(fn, reads, writes):
            return P.op("dve", fn, reads, writes)

        def next_ftmp():
            fctr[0] += 1
            return ftmp[fctr[0] % len(ftmp)]

        def next_ttmp():
            tctr[0] += 1
            return ttmp[tctr[0] % len(ttmp)]

        WB = {}

        def load_w(view, key=None, shape3=None):
            s = wr[wctr[0] % len(wr)]
            wctr[0] += 1
            a, b = view.shape[1], view.shape[2]
            n = a * b
            tile3 = s[:, 0:n].rearrange("p (a b) -> p a b", a=a)
            h = a // 2
            sub1 = 1 if n == 4096 else 0
            if key is None or key not in WB:
                P.dma("pool", tile3[:, 0:h, :], view[:, 0:h, :], [], [(s, 0)])
                P.dma("pool", tile3[:, h:a, :], view[:, h:a, :], [], [(s, sub1)])
                if key is not None:
                    WB[key] = P.dram("WB_%s" % "_".join(str(k) for k in key), [128, n], BF16)
                    P.dma("sp", WB[key].ap(), s[:, 0:n], [(s, 0), (s, 1)], [WB[key]], keep_queue=True)
            else:
                wv = WB[key].ap().rearrange("p (a b) -> p a b", a=a)
                P.dma("sp", tile3[:, 0:h, :], wv[:, 0:h, :], [WB[key]], [(s, 0)])
                P.dma("sp", tile3[:, h:a, :], wv[:, h:a, :], [WB[key]], [(s, sub1)])
            if n < 4096:
                h = a
            return s, tile3, h

        def wsub(s, h, k):
            return (s, 0 if k < h else 1)

        P.dma("sp", cst[:], consts, [], [cst])
        P.dma("sp", scT[:], cmodT, [], [scT])
        P.dma("sp", ohT[:], onehot, [], [ohT])
        P.dma("sp", lamt[:].rearrange("p l f d -> p (l f d)"), lamv.rearrange("l f d -> (l f d)").partition_broadcast(128), [], [lamt])
        P.dma("sp", subg[:].rearrange("p l d -> p (l d)"), subln_g.rearrange("l d -> (l d)").partition_broadcast(128), [], [subg])
        for li in range(2):
            for r in range(4):
                P.dma("sp", gng[:, li, r * 64:(r + 1) * 64], gnorm_g[li].partition_broadcast(128), [], [gng])
            for hh in range(2):
                P.dma("sp", qkg[hh * 64:(hh + 1) * 64, li, 0:1], qnorm_g[li].rearrange("(d o) -> d o", o=1), [], [qkg])
                P.dma("sp", qkg[hh * 64:(hh + 1) * 64, li, 1:2], knorm_g[li].rearrange("(d o) -> d o", o=1), [], [qkg])
        for d in range(2):
            for li in range(2):
                for c in range(2):
                    P.dma("sp", lbraw[:, d, li, c:c + 1], lbl[d, li, c * 128:(c + 1) * 128].rearrange("(p o) -> p o", o=1), [], [lbraw])
        dve(lambda e: e.tensor_copy(out=pswb[:], in_=cst[:, C_PSW:C_PSW + 128]), [cst], [pswb])
        dve(lambda e: e.memset(epsc[:], 1e-6), [], [epsc])
        for li in range(2):
            for j in range(2):
                dve(lambda e, li=li, j=j: e.tensor_tensor(out=lamt[:, li, 2 * j, :], in0=lamt[:, li, 2 * j, :],
                                                          in1=lamt[:, li, 2 * j + 1, :], op=ALU.mult), [lamt], [lamt])
                dve(lambda e, li=li, j=j: e.reduce_sum(out=lamc[:, li, j:j + 1], in_=lamt[:, li, 2 * j, :], axis=AX.X),
                    [lamt], [lamc])
            act(lamc[:, li, 0:2], lamc[:, li, 0:2], AF.Exp, [lamc], [lamc])
            lam_init = 0.8 - 0.6 * float(np.exp(-0.3 * li))
            dve(lambda e, li=li, lam_init=lam_init: e.scalar_tensor_tensor(
                out=lamc[:, li, 2:3], in0=lamc[:, li, 1:2], scalar=-lam_init, in1=lamc[:, li, 0:1],
                op0=ALU.add, op1=ALU.subtract), [lamc], [lamc])
            dve(lambda e, li=li, lam_init=lam_init: e.tensor_scalar_mul(out=subg[:, li, :], in0=subg[:, li, :],
                                                                      scalar1=1.0 - lam_init), [subg], [subg])
        for d in range(2):
            dve(lambda e, d=d: e.memset(lbt[:, d, 0, :, 0:1], 0.0), [], [lbt])
            dve(lambda e, d=d: e.memset(lbt[:, d, 0, :, 1:2], 1.0), [], [lbt])
            dve(lambda e, d=d: e.tensor_sub(out=lbraw[:, d, 1, :], in0=lbraw[:, d, 1, :], in1=lbraw[:, d, 0, :]), [lbraw], [lbraw])
            act(lbt[:, d, 1, :, 0], lbraw[:, d, 1, :], AF.Sigmoid, [lbraw], [lbt])
            dve(lambda e, d=d: e.tensor_scalar(out=lbt[:, d, 1, :, 1], in0=lbt[:, d, 1, :, 0], scalar1=-1.0, scalar2=1.0,
                                               op0=ALU.mult, op1=ALU.add), [lbt], [lbt])
        act(scTb[:], scT[:], AF.Silu, [scT], [scTb])
        dve(lambda e: e.memset(vaug[:, :, :, 128:129], 1.0), [], [vaug])
        dve(lambda e: e.memset(vcaug[:, :, :, 64:65], 1.0), [], [vcaug])

        def mod_part(li, part):
            dve(lambda e: e.memset(small[:, 54:55], 0.0), [], [ytmp, ttmp[0], ttmp[1]])
            if part == 0:
                P.dma("sp", lnbc[:].rearrange("p a d -> p (a d)"), lnp[li].rearrange("a d -> (a d)").partition_broadcast(128), [], [lnbc])
            wv = w_ada[li].rearrange("(kc p) n -> p kc n", p=128)
            psc = P.psum(pin=True)
            vmap = {1: 0, 0: 1, 4: 2, 3: 3}
            groups = range(0, 4) if part == 0 else range(4, 12)
            for g in groups:
                vec, hf = g // 2, g % 2
                s, t3, h = load_w(wv[:, :, g * 512:(g + 1) * 512])
                bt = bada[g % 2]
                P.dma("sp", bt[0:1, :], b_ada[li:li + 1, g * 512:(g + 1) * 512], [], [bt])
                P.dma("sp", bt[1:2, :], b_ada[li:li + 1, g * 512:(g + 1) * 512], [], [bt])
                ps = P.psum()
                for kc in range(8):
                    P.mm(ps[0:2, :], scTb[:, 2 * kc:2 * kc + 2], t3[:, kc, :], kc == 0, kc == 7,
                         [scTb, wsub(s, h, kc)], [ps])
                mr = modrows[g % 2]
                dve(lambda e, ps=ps, mr=mr, bt=bt: e.tensor_add(out=mr[:], in0=ps[0:2, :], in1=bt[:]), [ps, bt], [mr])
                if vec in vmap:
                    vi = vmap[vec]
                    for jj in range(4):
                        j = hf * 4 + jj
                        P.mm(psc[:, (vi * 8 + j) * 2:(vi * 8 + j) * 2 + 2], mr[0:2, jj * 128:(jj + 1) * 128],
                             cst[0:2, C_SEL2:C_SEL2 + 2], True, True, [mr, cst], [psc])
                else:
                    gi = 0 if vec == 2 else 1
                    for src in range(2):
                        ps2 = P.psum()
                        P.mm(ps2[:, :], cst[0:2, C_SELBC + src * 128:C_SELBC + (src + 1) * 128], mr[0:2, :],
                             True, True, [mr, cst], [ps2])
                        act(gbc[:, gi, src, hf * 512:(hf + 1) * 512], ps2[:, :], AF.Identity, [ps2], [gbc])
            v0 = 0 if part == 0 else 2
            dve(lambda e: e.tensor_copy(out=mc[:, v0:v0 + 2].rearrange("p a b c -> p (a b c)"), in_=psc[:, v0 * 16:(v0 + 2) * 16]), [psc], [mc])
            P.unpin(psc)
            dve(lambda e: e.tensor_scalar_add(out=mc[:, v0], in0=mc[:, v0], scalar1=1.0), [mc], [mc])
            dve(lambda e: e.memset(small[:, 55:56], 0.0), [], [ytmp, ttmp[0], ttmp[1]])

        def mod_T(u, li, which):
            src = 1 if u == 2 else 0
            for lt in range(2):
                for half in range(2):
                    ps = P.psum()
                    for jj in range(4):
                        j = half * 4 + jj
                        P.tr(ps[:, jj * 128:(jj + 1) * 128], x_tm[u][:, lt, j * 128:(j + 1) * 128], idf, [x_tm[u], cst], [ps])
                    eng_bank = evac_eng()
                    for jj in range(4):
                        j = half * 4 + jj
                        a_col = mc[:, 2 * which, j, src:src + 1]
                        b_col = mc[:, 2 * which + 1, j, src:src + 1]
                        o = hT[:, j, lt * 128:(lt + 1) * 128]
                        i_ = ps[:, jj * 128:(jj + 1) * 128]
                        if eng_bank == "act":
                            act(o, i_, AF.Identity, [ps, mc], [(hT, j)], scale=a_col, bias=b_col)
                        else:
                            dve(lambda e, o=o, i_=i_, a_col=a_col, b_col=b_col: e.tensor_scalar(
                                out=o, in0=i_, scalar1=a_col, scalar2=b_col, op0=ALU.mult, op1=ALU.add), [ps, mc], [(hT, j)])

        def layernorm_to_x(u, lt, li, site):
            for hf in range(2):
                dve(lambda e, hf=hf: e.bn_stats(out=stats[:, hf * 6:(hf + 1) * 6], in_=ytmp[:, hf * 512:(hf + 1) * 512]), [ytmp], [stats])
            dve(lambda e: e.bn_aggr(out=stats[:, 12:14], in_=stats[:, 0:12]), [stats], [stats])
            act(stats[:, 14:15], stats[:, 13:14], AF.Ln, [stats], [stats], bias=epsc[:, 0:1], scale=1.0)
            act(stats[:, 15:16], stats[:, 14:15], AF.Exp, [stats], [stats], scale=-0.5)
            dve(lambda e: e.tensor_scalar(out=ytmp[:], in0=ytmp[:], scalar1=stats[:, 12:13], scalar2=stats[:, 15:16],
                                          op0=ALU.subtract, op1=ALU.mult), [ytmp, stats], [ytmp])
            dve(lambda e: e.tensor_mul(out=ytmp[:], in0=ytmp[:], in1=lnbc[:, 2 * site, :]), [ytmp, lnbc], [ytmp])
            xt = x_tm[u]
            dve(lambda e, xt=xt: e.tensor_add(out=xt[:, lt, :], in0=ytmp[:], in1=lnbc[:, 2 * site + 1, :]), [ytmp, lnbc], [xt])

        def residual_ln(u, lt, li, site, banks):
            src = 1 if u == 2 else 0
            for hf in range(2):
                ps = banks[hf]
                t = next_ttmp()
                dve(lambda e, ps=ps, t=t, hf=hf: e.tensor_mul(out=t[:], in0=ps[:, :], in1=gbc[:, site, src, hf * 512:(hf + 1) * 512]),
                    [ps, gbc], [t])
                xt = x_tm[u]
                dve(lambda e, t=t, hf=hf, xt=xt: e.scalar_tensor_tensor(out=ytmp[:, hf * 512:(hf + 1) * 512],
                                                                        in0=xt[:, lt, hf * 512:(hf + 1) * 512], scalar=ALPHA, in1=t[:],
                                                                        op0=ALU.mult, op1=ALU.add), [t, xt], [ytmp])
            layernorm_to_x(u, lt, li, site)

        def g_resln(u, lt, li, site, banks):
            src = 1 if u == 2 else 0
            xt = x_tm[u]
            st = stats2[lt]
            yv = mixed[:, lt, :]
            yres = (mixed, lt)
            for hf in range(2):
                ps = banks[hf]
                t = next_ttmp()
                dve(lambda e, ps=ps, t=t, hf=hf: e.tensor_mul(out=t[:], in0=ps[:, :], in1=gbc[:, site, src, hf * 512:(hf + 1) * 512]),
                    [ps, gbc], [t])
                P.unpin(ps)
                dve(lambda e, t=t, hf=hf: e.scalar_tensor_tensor(out=yv[:, hf * 512:(hf + 1) * 512],
                                                                 in0=xt[:, lt, hf * 512:(hf + 1) * 512], scalar=ALPHA, in1=t[:],
                                                                 op0=ALU.mult, op1=ALU.add), [t, xt], [yres])
            for hf in range(2):
                dve(lambda e, hf=hf: e.bn_stats(out=st[:, hf * 6:(hf + 1) * 6], in_=yv[:, hf * 512:(hf + 1) * 512]), [yres], [st])
            dve(lambda e: e.bn_aggr(out=st[:, 12:14], in_=st[:, 0:12]), [st], [st])
            yield
            act(st[:, 14:15], st[:, 13:14], AF.Ln, [st], [st], bias=epsc[:, 0:1], scale=1.0)
            act(st[:, 15:16], st[:, 14:15], AF.Exp, [st], [st], scale=-0.5)
            yield
            dve(lambda e: e.tensor_scalar(out=yv, in0=yv, scalar1=st[:, 12:13], scalar2=st[:, 15:16],
                                          op0=ALU.subtract, op1=ALU.mult), [yres, st], [yres])
            dve(lambda e: e.tensor_mul(out=yv, in0=yv, in1=lnbc[:, 2 * site, :]), [yres, lnbc], [yres])
            dve(lambda e: e.tensor_add(out=xt[:, lt, :], in0=yv, in1=lnbc[:, 2 * site + 1, :]), [yres, lnbc], [xt])

        def rms_feat(ps, gcol, out_ap, out_res, reads_extra=()):
            t = next_ftmp()
            act(t[:], ps[:, 0:256], AF.Square, [ps], [t])
            ps2 = P.psum()
            P.mm(ps2[:, 0:256], cst[:, C_BONES:C_BONES + 128], t[:], True, True, [cst, t], [ps2])
            t2 = next_ftmp()
            act(t2[:], ps2[:, 0:256], AF.Ln, [ps2], [t2], bias=epsc[:, 0:1], scale=1.0)
            act(t2[:], t2[:], AF.Exp, [t2], [t2], scale=-0.5)
            dve(lambda e, t2=t2: e.scalar_tensor_tensor(out=out_ap, in0=ps[:, 0:256], scalar=gcol, in1=t2[:],
                                                        op0=ALU.mult, op1=ALU.mult), [ps, t2, qkg], [out_res])

        def attention(u, li, nk, keyA, keyC, keyCs, kres):
            dve(lambda e: e.memset(small[:, 58:59], 0.0), [], [big])
            chains = []
            for h in range(4):
                Es = []
                for m in range(2):
                    Es.append(Et[ectr[0] % 3])
                    ectr[0] += 1
                for kb in range(nk // 2):
                    pss = [P.psum(), P.psum()]
                    for k2 in range(2):
                        kc = kb * 2 + k2
                        for m in range(2):
                            P.mm(pss[m][:, k2 * 256:(k2 + 1) * 256], keyA(h, m, kc), zqk[m * 64:(m + 1) * 64, h, :], True, True,
                                 [zqk, kres], [pss[m]])
                    for m in range(2):
                        E = Es[m]
                        act(E[:, kb * 2:kb * 2 + 2, :], pss[m][:, :].rearrange("p (a b) -> p a b", a=2), AF.Exp, [pss[m]], [(E, ('E', id(E)))], scale=0.125)
                for lt in range(2):
                    po = P.psum()
                    for m in range(2):
                        for kc in range(nk):
                            P.mm(po[:, m * 129:(m + 1) * 129], Es[m][:, kc, lt * 128:(lt + 1) * 128], vaug[:, kc, h, :],
                                 kc == 0, kc == nk - 1, [(Es[m], ('E', id(Es[m]))), vaug], [po])
                    k = h * 2 + lt
                    sm = smallA[:, k * 8:(k + 1) * 8]
                    smr = (smallA, k)
                    actr[0] += 1
                    t1 = atmp[actr[0] % len(atmp)]
                    pv = po[:, 0:258].rearrange("p (m c) -> p m c", c=129)
                    dve(lambda e, pv=pv, sm=sm: e.reciprocal(out=sm[:, 0:2].rearrange("p (m o) -> p m o", o=1), in_=pv[:, :, 128:129]), [po], [smr])
                    dve(lambda e, sm=sm: e.tensor_mul(out=sm[:, 2:3], in0=sm[:, 1:2], in1=lamc[:, li, 2:3]), [smr, lamc], [smr])
                    dve(lambda e, po=po, t1=t1, sm=sm: e.tensor_scalar_mul(out=t1[:, 0:128], in0=po[:, 0:128], scalar1=sm[:, 0:1]), [po, smr], [t1])
                    dve(lambda e, po=po, t1=t1, sm=sm: e.scalar_tensor_tensor(out=t1[:, 0:128], in0=po[:, 129:257], scalar=sm[:, 2:3],
                                                                              in1=t1[:, 0:128], op0=ALU.mult, op1=ALU.add), [po, smr, t1], [t1])
                    dve(lambda e, sm=sm: e.memset(sm[:, 3:4], 0.0), [], [smr])

                    def chain(t1=t1, sm=sm, smr=smr, lt=lt, h=h):
                        act(t1[:, 128:256], t1[:, 0:128], AF.Square, [t1], [t1, smr], accum_out=sm[:, 3:4])
                        act(sm[:, 4:5], sm[:, 3:4], AF.Ln, [smr], [smr], bias=epsc[:, 0:1], scale=1.0 / 128)
                        act(sm[:, 5:6], sm[:, 4:5], AF.Exp, [smr], [smr], scale=-0.5)
                        yield
                        dve(lambda e: e.scalar_tensor_tensor(out=mixed[:, lt, h * 128:(h + 1) * 128], in0=t1[:, 0:128],
                                                             scalar=sm[:, 5:6], in1=subg[:, li, :],
                                                             op0=ALU.mult, op1=ALU.mult), [t1, smr, subg], [(mixed, lt)])
                    chains.append(chain())
            pos = [P.psum(pin=True), P.psum(pin=True)]
            for hp in range(2):
                hqs = (2 * hp, 2 * hp + 1)
                Ec = {}
                for hq in hqs:
                    Ec[hq] = Et[ectr[0] % 3]
                    ectr[0] += 1
                for kb in range(nk // 2):
                    pss = {hq: P.psum() for hq in hqs}
                    for k2 in range(2):
                        kc = kb * 2 + k2
                        for hq in hqs:
                            pb = (hq % 2) * 64
                            kfn = keyC if hq in (0, 3) else keyCs
                            P.mm(pss[hq][:, k2 * 256:(k2 + 1) * 256], kfn(pb, kc), zqk[pb:pb + 64, 8 + hq // 2, :], True, True, [zqk, kres], [pss[hq]])
                    for hq in hqs:
                        E = Ec[hq]
                        act(E[:, kb * 2:kb * 2 + 2, :], pss[hq][:, :].rearrange("p (a b) -> p a b", a=2), AF.Exp, [pss[hq]], [(E, ('E', id(E)))], scale=0.125)
                for hq in hqs:
                    E = Ec[hq]
                    for lt in range(2):
                        for kc in range(nk):
                            P.mm(pos[lt][:, hq * 65:(hq + 1) * 65], E[:, kc, lt * 128:(lt + 1) * 128], vcaug[:, kc, hq // 2, :],
                                 kc == 0, kc == nk - 1, [(E, ('E', id(E))), vcaug], [pos[lt]])
            run_rr(chains)
            for lt in range(2):
                pv = pos[lt][:, 0:260].rearrange("p (h c) -> p h c", c=65)
                dve(lambda e, pv=pv: e.reciprocal(out=small[:, 8:12].rearrange("p (h o) -> p h o", o=1), in_=pv[:, :, 64:65]), [pos[lt]], [small])
                dve(lambda e, pv=pv, lt=lt: e.tensor_tensor(out=mixed[:, lt, 768:1024].rearrange("p (h c) -> p h c", c=64), in0=pv[:, :, 0:64],
                                                            in1=small[:, 8:12].rearrange("p (h o) -> p h o", o=1).to_broadcast([128, 4, 64]),
                                                            op=ALU.mult), [pos[lt], small], [(mixed, lt)])
                P.unpin(pos[lt])

        def hgrn_elementwise(li, d, c, sig):
            lbc = lbt[:, d, li, c, 0:1]; omc = lbt[:, d, li, c, 1:2]
            dve(lambda e: e.tensor_scalar(out=sig[:], in0=sig[:], scalar1=omc, scalar2=lbc, op0=ALU.mult, op1=ALU.add), [sig, lbt], [sig])
            dve(lambda e: e.tensor_scalar_max(out=sig[:], in0=sig[:], scalar1=F_MIN), [sig], [sig])
            g = next_ftmp()
            act(g[:], sig[:], AF.Ln, [sig], [g])
            b = next_ftmp()
            dve(lambda e: e.tensor_tensor_scan(out=b[:], data0=cst[:, C_RESET:C_RESET + 256], data1=g[:], initial=0.0,
                                               op0=ALU.mult, op1=ALU.add), [cst, g], [b])
            if d == 1:
                dve(lambda e: e.tensor_sub(out=kt32[:], in0=g[:], in1=b[:]), [g, b], [kt32])
                dve(lambda e: e.tensor_tensor(out=g[:].rearrange("p (j t) -> p j t", t=CH), in0=kt32[:].rearrange("p (j t) -> p j t", t=CH),
                                              in1=b[:].rearrange("p (j t) -> p j t", t=CH)[:, :, CH - 1:CH].to_broadcast([128, NCH, CH]),
                                              op=ALU.add), [kt32, b], [g])
                bb, eb = g, b
            else:
                bb, eb = b, g
            dve(lambda e: e.tensor_scalar(out=sig[:], in0=sig[:], scalar1=-1.0, scalar2=1.0, op0=ALU.mult, op1=ALU.add), [sig], [sig])
            act(eb[:], bb[:], AF.Exp, [bb], [eb])
            act(bb[:], bb[:], AF.Exp, [bb], [bb], scale=-1.0)
            pos_e = CH - 1 if d == 0 else 0
            dve(lambda e: e.tensor_copy(out=eend[d][:, c, :], in_=eb[:].rearrange("p (j t) -> p j t", t=CH)[:, :, pos_e]), [eb], [eend[d]])
            dve(lambda e: e.tensor_mul(out=qt[d][:, c, :], in0=sq[:, c, :], in1=eb[:]), [sq, eb], [(qt[d], c)])
            dve(lambda e: e.tensor_mul(out=kt32[:], in0=sig[:], in1=bb[:]), [sig, bb], [kt32])
            dve(lambda e: e.tensor_copy(out=kt[d][:, c, :], in_=kt32[:]), [kt32], [(kt[d], c)])
            ps = P.psum()
            for lt in range(2):
                P.tr(ps[:, lt * 128:(lt + 1) * 128], kt32[:, lt * 128:(lt + 1) * 128], idf, [kt32, cst], [ps])
            act(ktok[d][:, :, c * 128:(c + 1) * 128], ps[:, 0:256].rearrange("p (l k) -> p l k", l=2), AF.Identity, [ps], [(ktok[d], c)])

        def hgrn_scan(u, li, dirs=(0, 1), ibs=None, vbs=None):
            if ibs is None:
                pairs = [(ibb, vblk)]
                vbs = {d: vblk for d in dirs}
            else:
                pairs = [(ibs[d], vbs[d]) for d in dirs]
            for ibx, vbx in pairs:
                for c in range(2):
                    for lt in range(2):
                        dve(lambda e, c=c, lt=lt, ibx=ibx, vbx=vbx: e.tensor_tensor(
                            out=vbx[:, lt, c, :, :], in0=ibx[:, lt, c * 128:(c + 1) * 128].unsqueeze(1).to_broadcast([128, 4, 128]),
                            in1=cst[:, C_ROW:C_ROW + 4].unsqueeze(2).to_broadcast([128, 4, 128]), op=ALU.mult), [ibx, cst], [vbx])

            def chain(d, c, UeT):
                vblk = vbs[d]
                for step in range(2):
                    lt = step if d == 0 else 1 - step
                    ps = P.psum()
                    P.mm(ps[:, :], ktok[d][:, lt, c * 128:(c + 1) * 128], vblk[:, lt, c, :, :].rearrange("p j v -> p (j v)"), True, True,
                         [(ktok[d], c), vblk], [ps])
                    dve(lambda e, ps=ps: e.tensor_tensor(out=UeT[:], in0=ps[:, :].rearrange("p (j v) -> p j v", j=4),
                                                         in1=cst[:, C_BLK2:C_BLK2 + 128].unsqueeze(1).to_broadcast([128, 4, 128]),
                                                         op=ALU.mult), [ps, cst], [UeT])
                    yield
                    dve(lambda e, lt=lt: e.tensor_tensor(out=UeT[:], in0=UeT[:],
                                                         in1=eend[d][:, c, lt * 4:(lt + 1) * 4].unsqueeze(2).to_broadcast([128, 4, 128]),
                                                         op=ALU.mult), [UeT, eend[d]], [UeT])
                    yield
                    for s4 in range(4):
                        jl = s4 if d == 0 else 3 - s4
                        j = lt * 4 + jl
                        slot = step * 4 + s4
                        pp = slot % 2
                        if slot == 0:
                            act(Sbf[d][:, c, 0, :], S32[d][:, c, 0, :], AF.Identity, [(S32[d], (c, 0))], [(Sbf[d], c)])
                        dve(lambda e, j=j, jl=jl, pp=pp: e.scalar_tensor_tensor(
                            out=S32[d][:, c, 1 - pp, :], in0=S32[d][:, c, pp, :], scalar=eend[d][:, c, j:j + 1], in1=UeT[:, jl, :],
                            op0=ALU.mult, op1=ALU.add), [UeT, eend[d], (S32[d], (c, pp))], [(S32[d], (c, 1 - pp))])
                        act(Sbf[d][:, c, slot + 1, :], S32[d][:, c, 1 - pp, :], AF.Identity, [(S32[d], (c, 1 - pp))], [(Sbf[d], c)])
                        yield

            gens = [chain(d, c, Ue4[(d * 2 + c) % 4]) for d in dirs for c in range(2)]
            while gens:
                for g in list(gens):
                    try:
                        next(g)
                    except StopIteration:
                        gens.remove(g)

        def hgrn_output(u, li, copy_state=True):
            combos = [(lt, c) for lt in range(2) for c in range(2)]

            def qb(i):
                return SubView(big, ("qb", i), lambda i=i: big[:, i * 512:(i + 1) * 512].rearrange("p (j t) -> p j t", j=4))

            def am(i):
                return SubView(big, ("am", i), lambda i=i: big[:, 4096 + i * 128:4096 + (i + 1) * 128])

            for ci, (lt, c) in enumerate(combos):
                for d in range(2):
                    Q = qb(ci * 2 + d)
                    dve(lambda e, d=d, c=c, lt=lt, Q=Q: e.tensor_tensor(
                        out=Q[:], in0=qt[d][:, c, lt * 128:(lt + 1) * 128].unsqueeze(1).to_broadcast([128, 4, 128]),
                        in1=cst[:, C_COL:C_COL + 512].rearrange("p (j t) -> p j t", j=4), op=ALU.mult), [(qt[d], c), cst], [Q])
            for ci, (lt, c) in enumerate(combos):
                for d in range(2):
                    moff = C_MF if d == 0 else C_MB
                    for hh in range(2):
                        pa = P.psum()
                        P.mm(pa[:, 0:128], kt[d][hh * 64:(hh + 1) * 64, c, lt * 128:(lt + 1) * 128],
                             qt[d][hh * 64:(hh + 1) * 64, c, lt * 128:(lt + 1) * 128], True, True, [(kt[d], c), (qt[d], c)], [pa])
                        A = am((ci * 2 + d) * 2 + hh)
                        dve(lambda e, pa=pa, A=A, moff=moff: e.tensor_tensor(
                            out=A[:], in0=pa[:, 0:128], in1=cst[:, moff:moff + 128], op=ALU.mult), [pa, cst], [A])
            chains = []
            for ci, (lt, c) in enumerate(combos):
                po = P.psum(pin=True)
                first = True
                for d in range(2):
                    Q = qb(ci * 2 + d)
                    for jl in range(4):
                        j = lt * 4 + jl
                        slot = j if d == 0 else (NCH - 1 - j)
                        P.mm(po[:, 0:128], Q[:, jl, :], Sbf[d][:, c, slot, :], first, False, [Q, (Sbf[d], c)], [po])
                        first = False
                for d in range(2):
                    for hh in range(2):
                        A = am((ci * 2 + d) * 2 + hh)
                        last = (d == 1 and hh == 1)
                        P.mm(po[:, hh * 64:(hh + 1) * 64], A[:], ibb[:, lt, (c * 2 + hh) * 64:(c * 2 + hh + 1) * 64], False, last,
                             [A, ibb], [po])

                def norm_chain(po=po, lt=lt, c=c, ci=ci):
                    actr[0] += 1
                    t = atmp[actr[0] % len(atmp)]
                    sm = smallA[:, ci * 8:(ci + 1) * 8]
                    smr = (smallA, ci)
                    act(t[:, 0:128], po[:, 0:128], AF.Square, [po], [t])
                    yield
                    dve(lambda e: e.reduce_sum(out=sm[:, 0:2], in_=t[:, 0:128].rearrange("p (h v) -> p h v", h=2), axis=AX.X), [t], [smr])
                    yield
                    act(sm[:, 2:4], sm[:, 0:2], AF.Ln, [smr], [smr], bias=epsc[:, 0:1], scale=1.0 / 64)
                    act(sm[:, 4:6], sm[:, 2:4], AF.Exp, [smr], [smr], scale=-0.5)
                    yield
                    dve(lambda e: e.tensor_tensor(out=t[:, 128:256].rearrange("p (h v) -> p h v", h=2),
                                                  in0=po[:, 0:128].rearrange("p (h v) -> p h v", h=2),
                                                  in1=sm[:, 4:6].unsqueeze(2).to_broadcast([128, 2, 64]), op=ALU.mult),
                        [po, smr], [t])
                    P.unpin(po)
                    dve(lambda e: e.tensor_mul(out=mixed[:, lt, 512 + c * 128:512 + (c + 1) * 128], in0=t[:, 128:256],
                                               in1=wg[:, lt, c * 128:(c + 1) * 128]), [t, wg], [(mixed, lt)])
                chains.append(norm_chain())
            run_rr(chains)

        class SubView:
            def __init__(self, parent, sub, fn):
                self._res = parent._res
                self._sub = sub
                self.fn = fn

            def __getitem__(self, k):
                return self.fn()[k]

        ptmp = list(ftmp) + [SubView(mixed, ("f", i), lambda i=i: mixed[:].rearrange("p t d -> p (t d)")[:, i * 256:(i + 1) * 256])
                             for i in range(8)] + [SubView(ytmp, ("y", i), lambda i=i: ytmp[:, i * 256:(i + 1) * 256]) for i in range(4)]
        pfree = list(ptmp)

        def next_ptmp():
            assert pfree, "projection temp pool exhausted"
            return pfree.pop(0)

        def free_ptmp(*ts):
            for t in ts:
                pfree.append(t)

        ropebs = [SubView(big, ("rp", i), lambda i=i: big[:, i * 256:(i + 1) * 256]) for i in range(8)]
        atmp = list(ftmp) + [SubView(ytmp, ("y", i), lambda i=i: ytmp[:, i * 256:(i + 1) * 256]) for i in range(4)]
        bada = [SubView(ttmp[i], ("bada",), lambda i=i: ttmp[i][0:2, :]) for i in range(2)]
        modrows = [SubView(ytmp, ("mr", i), lambda i=i: ytmp[0:2, i * 512:(i + 1) * 512]) for i in range(2)]
        actr = [0]
        rctr = [0]

        def run_rr(gens):
            gens = list(gens)
            while gens:
                for g in list(gens):
                    try:
                        next(g)
                    except StopIteration:
                        gens.remove(g)

        def g_rope(ps, tin, out_ap, out_res, q):
            rb = ropebs[rctr[0] % 8]
            rctr[0] += 1
            if ps is not None:
                t = next_ptmp()
                act(t[:], ps[:, 0:256], AF.Identity, [ps], [t])
                act(rb[:], ps[:, 0:256], AF.Identity, [ps], [rb])
                P.unpin(ps)
            else:
                t = tin
                act(rb[:], t[:], AF.Identity, [t], [rb])
            yield
            pr = P.psum(pin=True)
            P.mm(pr[:, 0:256], pswb[:], rb[:], True, True, [pswb, rb], [pr])
            yield
            dve(lambda e, t=t: e.tensor_mul(out=t[:], in0=t[:], in1=cst[:, C_RC + q * 256:C_RC + (q + 1) * 256]), [t, cst], [t])
            t2 = next_ptmp()
            dve(lambda e, t2=t2, pr=pr: e.tensor_mul(out=t2[:], in0=pr[:, 0:256], in1=cst[:, C_RS + q * 256:C_RS + (q + 1) * 256]), [pr, cst], [t2])
            P.unpin(pr)
            dve(lambda e, t=t, t2=t2: e.tensor_add(out=out_ap, in0=t[:], in1=t2[:]), [t, t2], [out_res])
            free_ptmp(t2)
            if ps is not None:
                free_ptmp(t)

        def g_rms(ps, gcol, out_ap, out_res):
            t = next_ptmp()
            act(t[:], ps[:, 0:256], AF.Square, [ps], [t])
            yield
            ps2 = P.psum(pin=True)
            P.mm(ps2[:, 0:256], cst[:, C_BONES:C_BONES + 128], t[:], True, True, [cst, t], [ps2])
            yield
            t2 = next_ptmp()
            act(t2[:], ps2[:, 0:256], AF.Ln, [ps2], [t2], bias=epsc[:, 0:1], scale=1.0)
            P.unpin(ps2)
            act(t2[:], t2[:], AF.Exp, [t2], [t2], scale=-0.5)
            yield
            dve(lambda e, t2=t2: e.scalar_tensor_tensor(out=out_ap, in0=ps[:, 0:256], scalar=gcol, in1=t2[:],
                                                        op0=ALU.mult, op1=ALU.mult), [ps, t2, qkg], [out_res])
            P.unpin(ps)
            free_ptmp(t, t2)

        def g_hgrn(li, d, c, ps):
            sig = next_ptmp()
            act(sig[:], ps[:, 0:256], AF.Sigmoid, [ps], [sig])
            P.unpin(ps)
            yield
            lbc = lbt[:, d, li, c, 0:1]
            omc = lbt[:, d, li, c, 1:2]
            dve(lambda e: e.tensor_scalar(out=sig[:], in0=sig[:], scalar1=omc, scalar2=lbc, op0=ALU.mult, op1=ALU.add), [sig, lbt], [sig])
            dve(lambda e: e.tensor_scalar_max(out=sig[:], in0=sig[:], scalar1=F_MIN), [sig], [sig])
            yield
            g = next_ptmp()
            act(g[:], sig[:], AF.Ln, [sig], [g])
            yield
            b = next_ptmp()
            k32 = next_ptmp()
            dve(lambda e: e.tensor_tensor_scan(out=b[:], data0=cst[:, C_RESET:C_RESET + 256], data1=g[:], initial=0.0,
                                               op0=ALU.mult, op1=ALU.add), [cst, g], [b])
            if d == 1:
                dve(lambda e: e.tensor_sub(out=k32[:], in0=g[:], in1=b[:]), [g, b], [k32])
                dve(lambda e: e.tensor_tensor(out=g[:].rearrange("p (j t) -> p j t", t=CH), in0=k32[:].rearrange("p (j t) -> p j t", t=CH),
                                              in1=b[:].rearrange("p (j t) -> p j t", t=CH)[:, :, CH - 1:CH].to_broadcast([128, NCH, CH]),
                                              op=ALU.add), [k32, b], [g])
                bb, eb = g, b
            else:
                bb, eb = b, g
            dve(lambda e: e.tensor_scalar(out=sig[:], in0=sig[:], scalar1=-1.0, scalar2=1.0, op0=ALU.mult, op1=ALU.add), [sig], [sig])
            yield
            act(eb[:], bb[:], AF.Exp, [bb], [eb])
            act(bb[:], bb[:], AF.Exp, [bb], [bb], scale=-1.0)
            yield
            pos_e = CH - 1 if d == 0 else 0
            dve(lambda e: e.tensor_copy(out=eend[d][:, c, :], in_=eb[:].rearrange("p (j t) -> p j t", t=CH)[:, :, pos_e]), [eb], [eend[d]])
            dve(lambda e: e.tensor_mul(out=qt[d][:, c, :], in0=sq[:, c, :], in1=eb[:]), [sq, eb], [(qt[d], c)])
            dve(lambda e: e.tensor_mul(out=k32[:], in0=sig[:], in1=bb[:]), [sig, bb], [k32])
            dve(lambda e: e.tensor_copy(out=kt[d][:, c, :], in_=k32[:]), [k32], [(kt[d], c)])
            yield
            pt = P.psum(pin=True)
            for lt in range(2):
                P.tr(pt[:, lt * 128:(lt + 1) * 128], k32[:, lt * 128:(lt + 1) * 128], idf, [k32, cst], [pt])
            yield
            act(ktok[d][:, :, c * 128:(c + 1) * 128], pt[:, 0:256].rearrange("p (l k) -> p l k", l=2), AF.Identity, [pt], [(ktok[d], c)])
            P.unpin(pt)
            free_ptmp(sig, g, b, k32)

        def g_out_T(t, dst_ap):
            pt = P.psum(pin=True)
            for lt in range(2):
                P.tr(pt[:, lt * 128:(lt + 1) * 128], t[:, lt * 128:(lt + 1) * 128], idf, [t, cst], [pt])
            yield
            t2 = next_ptmp()
            act(t2[:], pt[:, 0:256], AF.Identity, [pt], [t2])
            P.unpin(pt)
            P.dma("sp", dst_ap, t2[:].rearrange("p (l f) -> p l f", l=2), [t2], [])
            free_ptmp(t2)

        def proj_phase(u, li, q=0):
            is_s = (u == 2)
            mark = lambda nm: (print("MARK", li, u, nm, len(P.ops)) if os.environ.get("DEBUGP") else None)
            mark("start")
            dve(lambda e: e.memset(small[:, 59:60], 0.0), [], [mixed, big])
            mod_T(u, li, 0)
            mark("modT")
            win = w_in[li].rearrange("(kc p) n -> p kc n", p=128)
            pieces = [
                (0, [("qa", 0), ("qa", 1), ("qa", 2), ("qa", 3)], []),
                (512, [("ka", 0), ("ka", 1), ("ka", 2), ("ka", 3)], []),
                (1024, [], [("va", 0, 512)]),
                (1536, [("qb", 0), ("qb", 1), ("ff", 0), ("ff", 1)], []),
                (2048, [("fb", 0), ("fb", 1)], [("ib", 256, 256)]),
                (2560, [None, None, ("qc", 0), ("qc", 1)], [("gb", 0, 256)]),
                (3072, [("kc", 0)], [("vc", 128, 128)]),
            ]

            def f_handler(kind, idx, ps):
                if kind == "qa" or kind == "ka":
                    zc = idx if kind == "qa" else 4 + idx
                    if not is_s:
                        act(zqk[:, zc, :], ps[:, 0:256], AF.Identity, [ps], [(zqk, zc)])
                        if kind == "ka":
                            t = next_ptmp()
                            act(t[:], ps[:, 0:256], AF.Identity, [ps], [t])
                            P.unpin(ps)
                            yield
                            yield from g_out_T(t, o_ak[u, li, :, idx * 128:(idx + 1) * 128].rearrange("(l p) f -> p l f", p=128))
                            free_ptmp(t)
                        else:
                            P.unpin(ps)
                    elif kind == "qa":
                        yield from g_rope(ps, None, zqk[:, zc, :], (zqk, zc), q)
                    else:
                        yield from g_rope(ps, None, kseq[:, idx, q * 256:(q + 1) * 256], (kseq, (idx, q)), q)
                elif kind == "qc":
                    gcol = qkg[:, li, 0:1]
                    if not is_s:
                        yield from g_rms(ps, gcol, zqk[:, 8 + idx, :], (zqk, 8 + idx))
                    else:
                        t = next_ptmp()
                        yield from g_rms(ps, gcol, t[:], t)
                        yield
                        yield from g_rope(None, t, zqk[:, 8 + idx, :], (zqk, 8 + idx), q)
                        free_ptmp(t)
                elif kind == "kc":
                    gcol = qkg[:, li, 1:2]
                    t = next_ptmp()
                    yield from g_rms(ps, gcol, t[:], t)
                    yield
                    if not is_s:
                        dve(lambda e, t=t: e.tensor_copy(out=zqk[:, 10, :], in_=t[:]), [t], [(zqk, 10)])
                        P.dma("sp", zqk[0:64, 11, :], zqk[64:128, 10, :], [(zqk, 10)], [(zqk, 11)])
                        P.dma("sp", zqk[64:128, 11, :], zqk[0:64, 10, :], [(zqk, 10)], [(zqk, 11)])
                        yield from g_out_T(t, o_ck[u, li, :, :].rearrange("(l p) f -> p l f", p=128))
                    else:
                        yield from g_rope(None, t, kseq[:, 4, q * 256:(q + 1) * 256], (kseq, (4, q)), q)
                        P.dma("sp", kseq[0:64, 5, q * 256:(q + 1) * 256], kseq[64:128, 4, q * 256:(q + 1) * 256], [(kseq, (4, q))], [(kseq, (5, q))])
                        P.dma("sp", kseq[64:128, 5, q * 256:(q + 1) * 256], kseq[0:64, 4, q * 256:(q + 1) * 256], [(kseq, (4, q))], [(kseq, (5, q))])
                    free_ptmp(t)
                elif kind == "qb":
                    act(sq[:, idx, :], ps[:, 0:256], AF.Silu, [ps], [sq])
                    P.unpin(ps)
                elif kind in ("ff", "fb"):
                    yield from g_hgrn(li, 0 if kind == "ff" else 1, idx, ps)

            def t_handler(kind, lt, ps):
                vk = (2 * q + lt) if is_s else lt
                if kind == "va":
                    act(vaug[:, vk, :, 0:128], ps[:, :].rearrange("p (h v) -> p h v", h=4), AF.Identity, [ps], [vaug])
                    if not is_s:
                        t = next_ttmp()
                        act(t[:], ps[:, :], AF.Identity, [ps], [t])
                        P.dma("sp", o_av[u, li, lt * 128:(lt + 1) * 128, :], t[:], [t], [])
                    P.unpin(ps)
                elif kind == "vc":
                    act(vcaug[:, vk, :, 0:64], ps[:, 0:128].rearrange("p (h v) -> p h v", h=2), AF.Identity, [ps], [vcaug])
                    if not is_s:
                        t = next_ttmp()
                        act(t[:, 0:128], ps[:, 0:128], AF.Identity, [ps], [t])
                        P.dma("sp", o_cv[u, li, lt * 128:(lt + 1) * 128, :], t[:, 0:128], [t], [])
                    P.unpin(ps)
                elif kind == "ib":
                    act(ibb[:, lt, :], ps[:, 0:256], AF.Identity, [ps], [ibb])
                    P.unpin(ps)
                elif kind == "gb":
                    t = next_ptmp()
                    act(t[:], ps[:, 0:256], AF.Silu, [ps], [t])
                    P.unpin(ps)
                    yield
                    dve(lambda e, t=t, lt=lt: e.tensor_mul(out=wg[:, lt, :], in0=t[:], in1=gng[:, li, :]), [t, gng], [wg])
                    free_ptmp(t)
                return
                yield

            prev = []
            for col0, fch, tgr in pieces:
                ncol = min(512, NIN - col0)
                s, t3, h = load_w(win[:, :, col0:col0 + ncol], key=("in", li, col0))
                gens = []
                for ci, fc in enumerate(fch):
                    if fc is None:
                        continue
                    kind, idx = fc
                    mark("F " + kind + str(idx))
                    ps = P.psum(pin=True)
                    for kc in range(8):
                        P.mm(ps[:, 0:256], t3[:, kc, ci * 128:(ci + 1) * 128], hT[:, kc, :], kc == 0, kc == 7, [hT, wsub(s, h, kc)], [ps])
                    gens.append(f_handler(kind, idx, ps))
                for (kind, lc0, n) in tgr:
                    mark("T " + kind)
                    for lt in range(2):
                        ps = P.psum(pin=True)
                        for kc in range(8):
                            P.mm(ps[:, 0:n], hT[:, kc, lt * 128:(lt + 1) * 128], t3[:, kc, lc0:lc0 + n], kc == 0, kc == 7, [hT, wsub(s, h, kc)], [ps])
                        gens.append(t_handler(kind, lt, ps))
                alive = []
                for g in gens:
                    try:
                        next(g)
                        alive.append(g)
                    except StopIteration:
                        pass
                run_rr(prev)
                prev = alive
            run_rr(prev)

        def prompt_mixers(u, li):
            mark = lambda nm: (print("MARK", li, u, nm, len(P.ops)) if os.environ.get("DEBUGP") else None)
            dve(lambda e: e.memset(small[:, 61:62], 0.0), [], [big, mixed])
            for d in range(2):
                dve(lambda e, d=d: e.memset(S32[d][:, :, 0, :], 0.0), [], [S32[d]])
            mark("scan")
            hgrn_scan(u, li)
            mark("hout")
            for d in range(2):
                dst = o_sf if d == 0 else o_sb
                for hd in range(4):
                    c, hh = hd // 2, hd % 2
                    P.dma("sp", dst[u, li, hd, :, :], S32[d][hh * 64:(hh + 1) * 64, c, 0, hh * 64:(hh + 1) * 64], [S32[d]], [])
            hgrn_output(u, li)
            mark("attn")
            attention(u, li, 2,
                      lambda h, m, kc: zqk[m * 64:(m + 1) * 64, 4 + h, kc * 128:(kc + 1) * 128],
                      lambda pb, kc: zqk[pb:pb + 64, 10, kc * 128:(kc + 1) * 128],
                      lambda pb, kc: zqk[pb:pb + 64, 11, kc * 128:(kc + 1) * 128], zqk)

        def tail_phase(u, li):
            mark = lambda nm: (print("MARK", li, u, nm, len(P.ops)) if os.environ.get("DEBUGP") else None)
            mark("outproj")
            for lt in range(2):
                for half in range(2):
                    ps = P.psum()
                    for jj in range(4):
                        j = half * 4 + jj
                        P.tr(ps[:, jj * 128:(jj + 1) * 128], mixed[:, lt, j * 128:(j + 1) * 128], idf, [(mixed, lt), cst], [ps])
                    if evac_eng() == "act":
                        act(hT[:, half * 4:half * 4 + 4, lt * 128:(lt + 1) * 128], ps[:, :].rearrange("p (j t) -> p j t", j=4), AF.Identity, [ps], [hT])
                    else:
                        dve(lambda e, ps=ps, half=half, lt=lt: e.tensor_copy(out=hT[:, half * 4:half * 4 + 4, lt * 128:(lt + 1) * 128],
                                                                            in_=ps[:, :].rearrange("p (j t) -> p j t", j=4)), [ps], [hT])
            wo = w_out[li].rearrange("(kc p) n -> p kc n", p=128)
            slots = [load_w(wo[:, :, hf * 512:(hf + 1) * 512], key=("out", li, hf)) for hf in range(2)]
            chains = []
            for lt in range(2):
                banks = []
                for hf in range(2):
                    s, t3, h = slots[hf]
                    ps = P.psum(pin=True)
                    for kc in range(8):
                        P.mm(ps[:, :], hT[:, kc, lt * 128:(lt + 1) * 128], t3[:, kc, :], kc == 0, kc == 7, [hT, wsub(s, h, kc)], [ps])
                    banks.append(ps)
                chains.append(g_resln(u, lt, li, 0, banks))
            run_rr(chains)
            mark("ln1 done")
            mark("mlp")
            mod_T(u, li, 1)
            dve(lambda e: e.memset(small[:, 60:61], 0.0), [], [big])
            w1 = w_ff1[li].rearrange("(kc p) n -> p kc n", p=128)
            for pc in range(8):
                s, t3, h = load_w(w1[:, :, pc * 512:(pc + 1) * 512], key=("f1", li, pc))
                for ci in range(4):
                    ps = P.psum()
                    for kc in range(8):
                        P.mm(ps[:, 0:256], t3[:, kc, ci * 128:(ci + 1) * 128], hT[:, kc, :], kc == 0, kc == 7, [hT, wsub(s, h, kc)], [ps])
                    t = next_ftmp()
                    act(t[:], ps[:, 0:256], AF.Relu, [ps], [t])
                    dve(lambda e, t=t, pc=pc, ci=ci: e.tensor_mul(out=hidT[:, pc * 4 + ci, :], in0=t[:], in1=t[:]), [t], [(hidT, ('h', pc * 4 + ci))])
            mark("ff2")
            w2 = w_ff2[li].rearrange("(kc p) n -> p kc n", p=128)
            banks = [[P.psum(pin=True), P.psum(pin=True)] for lt in range(2)]
            for pc in range(8):
                s, t3, h = load_w(w2[:, pc * 4:(pc + 1) * 4, :], key=("f2", li, pc))
                for lt in range(2):
                    for hf in range(2):
                        for kl in range(4):
                            kc = pc * 4 + kl
                            P.mm(banks[lt][hf][:, :], hidT[:, kc, lt * 128:(lt + 1) * 128], t3[:, kl, hf * 512:(hf + 1) * 512],
                                 kc == 0, kc == 31, [(hidT, ('h', kc)), wsub(s, h, kl)], [banks[lt][hf]])
            run_rr([g_resln(u, lt, li, 1, banks[lt]) for lt in range(2)])

        def rope(ps, tin, out_ap, out_res, q):
            if ps is not None:
                t = next_ftmp()
                act(t[:], ps[:, 0:256], AF.Identity, [ps], [t])
                act(ropeb[:], ps[:, 0:256], AF.Identity, [ps], [ropeb])
            else:
                t = tin
                act(ropeb[:], t[:], AF.Identity, [t], [ropeb])
            pr = P.psum()
            P.mm(pr[:, 0:256], pswb[:], ropeb[:], True, True, [pswb, ropeb], [pr])
            dve(lambda e, t=t: e.tensor_mul(out=t[:], in0=t[:], in1=cst[:, C_RC + q * 256:C_RC + (q + 1) * 256]), [t, cst], [t])
            t2 = next_ftmp()
            dve(lambda e, t2=t2, pr=pr: e.tensor_mul(out=t2[:], in0=pr[:, 0:256], in1=cst[:, C_RS + q * 256:C_RS + (q + 1) * 256]), [pr, cst], [t2])
            dve(lambda e, t=t, t2=t2: e.tensor_add(out=out_ap, in0=t[:], in1=t2[:]), [t, t2], [out_res])

        def xsrc(li, q):
            if li == 0:
                return xs[q * 256:(q + 1) * 256, :].rearrange("(t p) d -> p t d", p=128), []
            return XS[q].ap(), [XS[q]]

        def sample_layer(li, x0_loaded=False):
            for kc in range(4):
                t = next_ttmp()
                P.dma("sp", t[:], ck_a[li, kc * 128:(kc + 1) * 128, :], [], [t])
                ps = P.psum()
                for h in range(4):
                    P.tr(ps[:, h * 128:(h + 1) * 128], t[:, h * 128:(h + 1) * 128], idf, [t, cst], [ps])
                act(kseq[:, 0:4, 1024 + kc * 128:1024 + (kc + 1) * 128], ps[:, :].rearrange("p (h k) -> p h k", h=4), AF.Identity,
                    [ps], [(kseq, ("c", kc))])
                t2 = next_ttmp()
                P.dma("sp", t2[:, 0:128], ck_c[li, kc * 128:(kc + 1) * 128, :], [], [t2])
                ps2 = P.psum()
                P.tr(ps2[:, 0:128], t2[:, 0:128], idf, [t2, cst], [ps2])
                act(kseq[:, 4, 1024 + kc * 128:1024 + (kc + 1) * 128], ps2[:, 0:128], AF.Identity, [ps2], [(kseq, ("cc", kc))])
                P.dma("sp", kseq[0:64, 5, 1024 + kc * 128:1024 + (kc + 1) * 128], kseq[64:128, 4, 1024 + kc * 128:1024 + (kc + 1) * 128],
                      [(kseq, ("cc", kc))], [(kseq, ("cs", kc))])
                P.dma("sp", kseq[64:128, 5, 1024 + kc * 128:1024 + (kc + 1) * 128], kseq[0:64, 4, 1024 + kc * 128:1024 + (kc + 1) * 128],
                      [(kseq, ("cc", kc))], [(kseq, ("cs", kc))])
                P.dma("pool", vaug[:, 8 + kc, :, 0:128], cv_a[li, kc * 128:(kc + 1) * 128, :].rearrange("p (h v) -> p h v", h=4), [], [vaug])
                P.dma("pool", vcaug[:, 8 + kc, :, 0:64], cv_c[li, kc * 128:(kc + 1) * 128, :].rearrange("p (h v) -> p h v", h=2), [], [vcaug])
            for q in range(4):
                if not (q == 0 and x0_loaded):
                    src, rd = xsrc(li, q)
                    P.dma("sp", x_tm[2][:], src, rd, [x_tm[2]])
                proj_phase(2, li, q)
                P.dma("sp", QS[q].ap()[:, 0:4, :], zqk[:, 0:4, :], [zqk], [QS[q]])
                P.dma("sp", QS[q].ap()[:, 4:6, :], zqk[:, 8:10, :], [zqk], [QS[q]])
                for d in range(2):
                    P.dma("sp", HS[q].ap()[:, 2 * d], qt[d][:], [qt[d]], [HS[q]])
                    P.dma("sp", HS[q].ap()[:, 2 * d + 1], kt[d][:], [kt[d]], [HS[q]])
                    P.dma("sp", KT[q].ap()[:, d], ktok[d][:], [ktok[d]], [KT[q]])
                    P.dma("sp", EE[q].ap()[:, d], eend[d][:], [eend[d]], [EE[q]])
                P.dma("sp", IB[q].ap(), ibb[:], [ibb], [IB[q]])
                P.dma("sp", WG[q].ap(), wg[:], [wg], [WG[q]])
            dve(lambda e: e.memset(small[:, 56:57], 0.0), [], [big])
            ibS = {d: SubView(big, ("ib", d), lambda d=d: big[:, d * 2560:d * 2560 + 512].rearrange("p (l f) -> p l f", l=2)) for d in range(2)}
            vbS = {d: SubView(big, ("vb", d), lambda d=d: big[:, d * 2560 + 512:(d + 1) * 2560].rearrange("p (l c j v) -> p l c j v", l=2, c=2, j=4))
                   for d in range(2)}
            for d in range(2):
                stin = st_f if d == 0 else st_b
                dve(lambda e, d=d: e.memset(S32[d][:, :, 0, :], 0.0), [], [S32[d]])
                for hd in range(4):
                    c, hh = hd // 2, hd % 2
                    P.dma("sp", S32[d][hh * 64:(hh + 1) * 64, c, 0, hh * 64:(hh + 1) * 64], stin[li, hd, :, :], [], [S32[d]])
            for qi in range(4):
                for d in range(2):
                    q = qi if d == 0 else 3 - qi
                    P.dma("sp", ktok[d][:], KT[q].ap()[:, d], [KT[q]], [ktok[d]])
                    P.dma("sp", eend[d][:], EE[q].ap()[:, d], [EE[q]], [eend[d]])
                    P.dma("sp", ibS[d][:], IB[q].ap(), [IB[q]], [ibS[d]])
                hgrn_scan(2, li, dirs=(0, 1), ibs=ibS, vbs=vbS)
                for d in range(2):
                    q = qi if d == 0 else 3 - qi
                    P.dma("sp", SS[q].ap()[:, d], Sbf[d][:], [Sbf[d]], [SS[q]])
            def load_x(q):
                src, rd = xsrc(li, q)
                P.dma("sp", x_tm[2][:], src, rd, [x_tm[2]])

            def load_ops(q, eng):
                P.dma(eng, zqk[:, 0:4, :], QS[q].ap()[:, 0:4, :], [QS[q]], [zqk])
                P.dma(eng, zqk[:, 8:10, :], QS[q].ap()[:, 4:6, :], [QS[q]], [zqk])
                for d in range(2):
                    P.dma(eng, qt[d][:], HS[q].ap()[:, 2 * d], [HS[q]], [qt[d]])
                    P.dma(eng, kt[d][:], HS[q].ap()[:, 2 * d + 1], [HS[q]], [kt[d]])
                    P.dma(eng, Sbf[d][:], SS[q].ap()[:, d], [SS[q]], [Sbf[d]])
                P.dma(eng, ibb[:], IB[q].ap(), [IB[q]], [ibb])
                P.dma(eng, wg[:], WG[q].ap(), [WG[q]], [wg])

            def load_own():
                dve(lambda e: e.memset(small[:, 62:63], 0.0), [], [big])
                tmp32 = mixed[:].rearrange("p t d -> p (t d)")
                off = {True: 0, False: 0}

                def sel(dst, dst_res, n, srcs, f32):
                    cap = 2048 if f32 else 9216
                    base = tmp32 if f32 else big
                    pres = mixed if f32 else big
                    tms = []
                    for q in range(1, 4):
                        if off[f32] + n > cap:
                            off[f32] = 0
                            dve(lambda e: e.memset(small[:, 57:58], 0.0), [], [pres])
                        o = off[f32]
                        off[f32] += n
                        tms.append((base[:, o:o + n], (pres, ("sel", o, n))))
                    sap, sres = srcs(0)
                    P.dma("sp", dst, sap, [sres], [dst_res])
                    for q in range(1, 4):
                        sap, sres = srcs(q)
                        P.dma("sp", tms[q - 1][0], sap, [sres], [tms[q - 1][1]])
                    dve(lambda e, dst=dst: e.tensor_scalar_mul(out=dst, in0=dst, scalar1=ohT[:, 0:1]), [dst_res, ohT], [dst_res])
                    for q in range(1, 4):
                        tm = tms[q - 1][0]
                        dve(lambda e, dst=dst, tm=tm, q=q: e.scalar_tensor_tensor(out=dst, in0=tm, scalar=ohT[:, q:q + 1], in1=dst,
                                                                                  op0=ALU.mult, op1=ALU.add), [tms[q - 1][1], dst_res, ohT], [dst_res])
                xv = x_tm[2][:].rearrange("p t d -> p (t d)")
                for blk in range(4):
                    sel(xv[:, blk * 512:(blk + 1) * 512], x_tm[2], 512,
                        lambda q, blk=blk: (XS[q].ap().rearrange("p t d -> p (t d)")[:, blk * 512:(blk + 1) * 512], XS[q]), True)
                sel(zqk[:, 0:4, :].rearrange("p a b -> p (a b)"), zqk, 1024, lambda q: (QS[q].ap()[:, 0:4, :].rearrange("p a b -> p (a b)"), QS[q]), False)
                sel(zqk[:, 8:10, :].rearrange("p a b -> p (a b)"), zqk, 512, lambda q: (QS[q].ap()[:, 4:6, :].rearrange("p a b -> p (a b)"), QS[q]), False)
                for d in range(2):
                    sel(qt[d][:].rearrange("p a b -> p (a b)"), qt[d], 512, lambda q, d=d: (HS[q].ap()[:, 2 * d].rearrange("p a b -> p (a b)"), HS[q]), False)
                    sel(kt[d][:].rearrange("p a b -> p (a b)"), kt[d], 512, lambda q, d=d: (HS[q].ap()[:, 2 * d + 1].rearrange("p a b -> p (a b)"), HS[q]), False)
                    sel(Sbf[d][:].rearrange("p a b c -> p (a b c)"), Sbf[d], 2304, lambda q, d=d: (SS[q].ap()[:, d].rearrange("p a b c -> p (a b c)"), SS[q]), False)
                sel(ibb[:].rearrange("p a b -> p (a b)"), ibb, 512, lambda q: (IB[q].ap().rearrange("p a b -> p (a b)"), IB[q]), False)
                sel(wg[:].rearrange("p a b -> p (a b)"), wg, 512, lambda q: (WG[q].ap().rearrange("p a b -> p (a b)"), WG[q]), True)

            def mix_tail(prefetch=None):
                dve(lambda e: e.memset(small[:, 61:62], 0.0), [], [big, mixed])
                hgrn_output(2, li, copy_state=False)
                attention(2, li, 12,
                          lambda h, m, kc: kseq[m * 64:(m + 1) * 64, h, kc * 128:(kc + 1) * 128],
                          lambda pb, kc: kseq[pb:pb + 64, 4, kc * 128:(kc + 1) * 128],
                          lambda pb, kc: kseq[pb:pb + 64, 5, kc * 128:(kc + 1) * 128], kseq)
                if prefetch is not None:
                    load_ops(prefetch, "pool")
                tail_phase(2, li)

            if li == 0:
                load_ops(0, "sp")
                for q in range(4):
                    load_x(q)
                    mix_tail(prefetch=q + 1 if q < 3 else None)
                    P.dma("sp", XS[q].ap(), x_tm[2][:], [x_tm[2]], [XS[q]])
            else:
                load_own()
                mix_tail()
                P.dma("sp", ys.rearrange("(t p) d -> p t d", p=128), x_tm[2][:], [x_tm[2]], [])

        units = [0, 1]
        nlayers = 2
        if stop is not None:
            units = units[:stop.get("units", len(units))]
            nlayers = stop.get("layers", 2)
        xbuf = [x_t, x_tB]
        xsel = [0]

        def use_x(i):
            xsel[0] = i
            x_tm[0] = x_tm[1] = x_tm[2] = xbuf[i]
            return xbuf[i]

        def load_prompt_x(li, u, xt):
            if li == 0:
                P.dma("sp", xt[:], xp[u * 256:(u + 1) * 256, :].rearrange("(t p) d -> p t d", p=128), [], [xt])
            else:
                P.dma("sp", xt[:], XP[u].ap(), [XP[u]], [xt])

        for li in range(nlayers):
            mod_part(li, 0)
            cur = use_x(xsel[0])
            load_prompt_x(li, units[0], cur) if units else None
            for ui, u in enumerate(units):
                cur = xbuf[xsel[0]]
                proj_phase(u, li)
                if u == 0:
                    mod_part(li, 1)
                prompt_mixers(u, li)
                nxt = xbuf[1 - xsel[0]]
                if ui + 1 < len(units):
                    load_prompt_x(li, units[ui + 1], nxt)
                elif with_sample:
                    src, rd = xsrc(li, 0)
                    P.dma("sp", nxt[:], src, rd, [nxt])
                tail_phase(u, li)
                if li == 0:
                    P.dma("sp", XP[u].ap(), cur[:], [cur], [XP[u]])
                else:
                    P.dma("sp", yp[u * 256:(u + 1) * 256, :].rearrange("(t p) d -> p t d", p=128), cur[:], [cur], [])
                use_x(1 - xsel[0])
            if with_sample and (stop is None or stop.get("sample", True)):
                sample_layer(li, x0_loaded=bool(units))
        P.emit()
    return nc


def kernel(**inp):
    f32 = lambda a: np.ascontiguousarray(np.asarray(a, dtype=np.float32))
    x_prompt = f32(inp["x_prompt"]); x_sample = f32(inp["x_sample"])
    nc = build_nc(with_sample=WITH_SAMPLE)
    shared = {
        "w_ada": f32(inp["w_ada"]), "b_ada": f32(inp["b_ada"]), "w_in": f32(inp["w_in"]), "w_out": f32(inp["w_out"]),
        "w_ff1": f32(inp["w_ff1"]), "w_ff2": f32(inp["w_ff2"]),
        "lamv": f32(np.stack([inp["lam_q1"], inp["lam_k1"], inp["lam_q2"], inp["lam_k2"]], axis=1)),
        "subln_g": f32(inp["subln_g"]),
        "lbl": f32(np.stack([inp["lb_logits_fwd"], inp["lb_logits_bwd"]], axis=0)),
        "gnorm_g": f32(inp["gnorm_g"]), "qnorm_g": f32(inp["qnorm_g"]), "knorm_g": f32(inp["knorm_g"]),
        "lnp": f32(np.stack([inp["ln1_g"], inp["ln1_b"], inp["ln2_g"], inp["ln2_b"]], axis=1)),
    }
    in_maps = []
    for i in range(8):
        sq_, qd = i // 4, i % 4
        cm = np.stack([np.asarray(inp["c_ctx"], np.float32), np.asarray(inp["c"], np.float32)[sq_]], axis=0)
        cmodT = f32(cm.reshape(2, 8, 128).transpose(2, 1, 0).reshape(128, 16))
        oh = np.zeros((128, 8), np.float32)
        oh[:, qd] = 1.0
        m = dict(shared)
        m.update({
            "xp": f32(x_prompt[2 * i:2 * i + 2].reshape(512, D)),
            "xs": f32(x_sample[sq_]),
            "cmodT": cmodT, "consts": make_consts(0),
            "ck_a": f32(np.asarray(inp["cache_a_k"])[sq_].reshape(2, 512, 512)),
            "cv_a": f32(np.asarray(inp["cache_a_v"])[sq_].reshape(2, 512, 512)),
            "ck_c": f32(np.asarray(inp["cache_c_k"])[sq_].reshape(2, 512, 128)),
            "cv_c": f32(np.asarray(inp["cache_c_v"])[sq_].reshape(2, 512, 128)),
            "st_f": f32(np.asarray(inp["state_b_fwd"])[sq_]), "st_b": f32(np.asarray(inp["state_b_bwd"])[sq_]),
            "onehot": oh,
        })
        in_maps.append(m)
    if DEBUG_HOOK is not None:
        return DEBUG_HOOK(in_maps)
    res = run_bass_kernel_spmd(nc, in_maps, core_ids=list(range(8)))
    R = res.results
    y_prompt = np.concatenate([r["yp"].reshape(2, 256, D) for r in R], axis=0)
    y_sample = np.stack([np.concatenate([R[s * 4 + q]["ys"] for q in range(4)], axis=0) for s in range(2)], axis=0)
    cat = lambda k, shp: np.concatenate([r[k] for r in R], axis=0).reshape(shp)
    return (y_prompt.astype(np.float32), y_sample.astype(np.float32),
            cat("o_ak", (16, 2, 256, 4, 2, 64)), cat("o_av", (16, 2, 256, 4, 128)),
            cat("o_ck", (16, 2, 256, 2, 64)), cat("o_cv", (16, 2, 256, 2, 64)),
            cat("o_sf", (16, 2, 4, 64, 64)), cat("o_sb", (16, 2, 4, 64, 64)))


WITH_SAMPLE = True
DEBUG_HOOK = None
```
